# Optimizing a Trainium2 kernel written in Bass

```python
import math
import jax, jax.numpy as jnp
from jax import lax
import numpy as np


D_MODEL = 2048
BATCH = 2
SEQ = 8192
DEPTH = 1

SSM_WIDTH = D_MODEL // 2
SSM_GROUP = 16
SSM_GROUPS = SSM_WIDTH // SSM_GROUP
SSM_STATE = 64
ATTN_WIDTH = D_MODEL - SSM_WIDTH
HEAD_DIM = 128
N_HEADS = ATTN_WIDTH // HEAD_DIM
DILATED_BRANCHES = ((128, 1), (512, 4), (2048, 16))
FFN_HIDDEN = ((8 * D_MODEL + 3 * 256 - 1) // (3 * 256)) * 256
IN_PROJ_WIDTH = SSM_WIDTH + 3 * ATTN_WIDTH
N_MOD = 6
DT_MIN = 1e-3
DT_MAX = 1e-1
NORM_EPS = 1e-6
MASK_VALUE = -1e30

kernel_name = "hybrid_s5_dilated_alibi_block"


def _rms(x, w):
    xf = x.astype(jnp.float32)
    y = xf * lax.rsqrt(jnp.mean(xf * xf, axis=-1, keepdims=True) + NORM_EPS)
    return y * w.astype(jnp.float32)


def _modulate(y, shift, scale):
    return y * (1.0 + scale.astype(jnp.float32)[:, None, :]) + shift.astype(jnp.float32)[:, None, :]


def _complex_linear_combine(e1, e2):
    a1r, a1i, b1r, b1i = e1
    a2r, a2i, b2r, b2i = e2
    ar = a1r * a2r - a1i * a2i
    ai = a1r * a2i + a1i * a2r
    br = a2r * b1r - a2i * b1i + b2r
    bi = a2r * b1i + a2i * b1r + b2i
    return ar, ai, br, bi


def _s5_mixer(u, a_re, a_im, log_dt, b_re, b_im, c_re, c_im, d, w_glu):
    B, S, _ = u.shape
    f32 = jnp.float32
    uf = u.astype(f32).reshape(B, S, SSM_GROUPS, SSM_GROUP)
    a_re = a_re.astype(f32)
    a_im = a_im.astype(f32)
    dt = jnp.exp(log_dt.astype(f32))[:, None]
    mag = jnp.exp(a_re * dt)
    abar_re = mag * jnp.cos(a_im * dt)
    abar_im = mag * jnp.sin(a_im * dt)
    num_re = abar_re - 1.0
    num_im = abar_im
    den = a_re * a_re + a_im * a_im
    z_re = (num_re * a_re + num_im * a_im) / den
    z_im = (num_im * a_re - num_re * a_im) / den
    b_re = b_re.astype(f32)
    b_im = b_im.astype(f32)
    bbar_re = z_re[..., None] * b_re - z_im[..., None] * b_im
    bbar_im = z_re[..., None] * b_im + z_im[..., None] * b_re
    bu_re = jnp.einsum('bsgc,gnc->sbgn', uf, bbar_re)
    bu_im = jnp.einsum('bsgc,gnc->sbgn', uf, bbar_im)
    a_seq_re = jnp.broadcast_to(abar_re[None, None], (S, 1, SSM_GROUPS, SSM_STATE))
    a_seq_im = jnp.broadcast_to(abar_im[None, None], (S, 1, SSM_GROUPS, SSM_STATE))
    _, _, h_re, h_im = lax.associative_scan(
        _complex_linear_combine, (a_seq_re, a_seq_im, bu_re, bu_im), axis=0)
    y = (jnp.einsum('sbgn,gcn->bsgc', h_re, c_re.astype(f32))
         - jnp.einsum('sbgn,gcn->bsgc', h_im, c_im.astype(f32))
         + d.astype(f32) * uf)
    y = jax.nn.gelu(y.reshape(B, S, SSM_WIDTH))
    return y * jax.nn.sigmoid(y @ w_glu.astype(f32))


def _dilated_branch(q, k, v, slopes, window, dilation):
    B, S, H, E = q.shape
    blk = window // dilation
    span = blk * dilation
    s_pad = ((S + span - 1) // span) * span
    pad = s_pad - S
    L = s_pad // dilation
    nb = L // blk

    def strided_blocks(t):
        t = jnp.pad(t, ((0, 0), (0, pad), (0, 0), (0, 0)))
        t = t.reshape(B, L, dilation, H, E).transpose(0, 2, 3, 1, 4)
        return t.reshape(B, dilation, H, nb, blk, E)

    qb, kb, vb = strided_blocks(q), strided_blocks(k), strided_blocks(v)

    def with_prev(t):
        prev = jnp.concatenate([jnp.zeros_like(t[:, :, :, :1]), t[:, :, :, :-1]], axis=3)
        return jnp.concatenate([prev, t], axis=4)

    kc, vc = with_prev(kb), with_prev(vb)
    scores = jnp.einsum('bdhnqe,bdhnke->bdhnqk', qb, kc)
    qi = jnp.arange(blk)[:, None]
    ki = jnp.arange(2 * blk)[None, :]
    dist = qi + blk - ki
    band = (dist >= 0) & (dist <= blk)
    first = (jnp.arange(nb) == 0)[:, None, None]
    valid = band[None] & ~(first & (ki < blk)[None])
    bias = -slopes[:, None, None, None] * (dilation * dist).astype(jnp.float32)[None, None]
    scores = jnp.where(valid, scores + bias, MASK_VALUE)
    m = jnp.max(scores, axis=-1)
    p = jnp.exp(scores - m[..., None])
    l = jnp.sum(p, axis=-1)
    o = jnp.einsum('bdhnqk,bdhnke->bdhnqe', p, vc)

    m = m.reshape(B, dilation, H, L).transpose(0, 3, 1, 2).reshape(B, s_pad, H)[:, :S]
    l = l.reshape(B, dilation, H, L).transpose(0, 3, 1, 2).reshape(B, s_pad, H)[:, :S]
    o = o.reshape(B, dilation, H, L, E).transpose(0, 3, 1, 2, 4).reshape(B, s_pad, H, E)[:, :S]
    return m, l, o


def _dilated_attention(q, k, v, q_norm_w, k_norm_w):
    B, S, _ = q.shape
    q = _rms(q.reshape(B, S, N_HEADS, HEAD_DIM), q_norm_w) * (HEAD_DIM ** -0.5)
    k = _rms(k.reshape(B, S, N_HEADS, HEAD_DIM), k_norm_w)
    v = v.astype(jnp.float32).reshape(B, S, N_HEADS, HEAD_DIM)
    slopes = jnp.exp2(-8.0 * jnp.arange(1, N_HEADS + 1, dtype=jnp.float32) / N_HEADS)
    outs = [_dilated_branch(q, k, v, slopes, w, dl) for (w, dl) in DILATED_BRANCHES]
    ms = jnp.stack([o[0] for o in outs])
    ls = jnp.stack([o[1] for o in outs])
    os_ = jnp.stack([o[2] for o in outs])
    w = jnp.exp(ms - jnp.max(ms, axis=0, keepdims=True))
    num = jnp.sum(w[..., None] * os_, axis=0)
    den = jnp.sum(w * ls, axis=0)
    return (num / den[..., None]).reshape(B, S, ATTN_WIDTH)


def setup_inputs(seed: int = 0) -> dict:
    key = jax.random.key(seed)
    ks = jax.random.split(key, 24)
    f32 = jnp.float32
    nrm = lambda k, shape, s: jax.random.normal(k, shape, f32) * s
    D, G, N, C = D_MODEL, SSM_GROUPS, SSM_STATE, SSM_GROUP
    x = jax.random.normal(ks[0], (BATCH, SEQ, D), f32)
    c = jax.random.normal(ks[1], (BATCH, D), f32)
    w_ada = nrm(ks[2], (DEPTH, D, N_MOD * D), 0.5 * D ** -0.5)
    b_ada = nrm(ks[3], (DEPTH, N_MOD * D), 0.05)
    norm1_w = 1.0 + nrm(ks[4], (DEPTH, D), 0.02)
    w_in = nrm(ks[5], (DEPTH, D, IN_PROJ_WIDTH), D ** -0.5)
    ssm_a_re = -0.5 + nrm(ks[6], (DEPTH, G, N), 0.01)
    ssm_a_im = jnp.pi * jnp.arange(N, dtype=f32)[None, None, :] + nrm(ks[7], (DEPTH, G, N), 0.01)
    ssm_log_dt = jax.random.uniform(ks[8], (DEPTH, G), f32, math.log(DT_MIN), math.log(DT_MAX))
    ssm_b_re = nrm(ks[9], (DEPTH, G, N, C), (2 * C) ** -0.5)
    ssm_b_im = nrm(ks[10], (DEPTH, G, N, C), (2 * C) ** -0.5)
    ssm_c_re = nrm(ks[11], (DEPTH, G, C, N), (2 * N) ** -0.5)
    ssm_c_im = nrm(ks[12], (DEPTH, G, C, N), (2 * N) ** -0.5)
    ssm_d = nrm(ks[13], (DEPTH, G, C), 1.0)
    ssm_w_glu = nrm(ks[14], (DEPTH, SSM_WIDTH, SSM_WIDTH), SSM_WIDTH ** -0.5)
    q_norm_w = 1.0 + nrm(ks[15], (DEPTH, HEAD_DIM), 0.02)
    k_norm_w = 1.0 + nrm(ks[16], (DEPTH, HEAD_DIM), 0.02)
    w_out = nrm(ks[17], (DEPTH, D, D), D ** -0.5)
    norm2_w = 1.0 + nrm(ks[18], (DEPTH, D), 0.02)
    w_ffn_gate = nrm(ks[19], (DEPTH, D, FFN_HIDDEN), D ** -0.5)
    w_ffn_up = nrm(ks[20], (DEPTH, D, FFN_HIDDEN), D ** -0.5)
    w_ffn_down = nrm(ks[21], (DEPTH, FFN_HIDDEN, D), FFN_HIDDEN ** -0.5)
    return {"x": x, "c": c, "w_ada": w_ada, "b_ada": b_ada, "norm1_w": norm1_w,
            "w_in": w_in, "ssm_a_re": ssm_a_re, "ssm_a_im": ssm_a_im,
            "ssm_log_dt": ssm_log_dt, "ssm_b_re": ssm_b_re, "ssm_b_im": ssm_b_im,
            "ssm_c_re": ssm_c_re, "ssm_c_im": ssm_c_im, "ssm_d": ssm_d,
            "ssm_w_glu": ssm_w_glu, "q_norm_w": q_norm_w, "k_norm_w": k_norm_w,
            "w_out": w_out, "norm2_w": norm2_w, "w_ffn_gate": w_ffn_gate,
            "w_ffn_up": w_ffn_up, "w_ffn_down": w_ffn_down}


def reference(x, c, w_ada, b_ada, norm1_w, w_in, ssm_a_re, ssm_a_im, ssm_log_dt,
              ssm_b_re, ssm_b_im, ssm_c_re, ssm_c_im, ssm_d, ssm_w_glu,
              q_norm_w, k_norm_w, w_out, norm2_w, w_ffn_gate, w_ffn_up, w_ffn_down):
    cond = jax.nn.silu(c)
    splits = [SSM_WIDTH, SSM_WIDTH + ATTN_WIDTH, SSM_WIDTH + 2 * ATTN_WIDTH]
    for i in range(DEPTH):
        mod = cond @ w_ada[i] + b_ada[i]
        shift1, scale1, gate1, shift2, scale2, gate2 = jnp.split(mod, N_MOD, axis=-1)

        h = _modulate(_rms(x, norm1_w[i]), shift1, scale1).astype(x.dtype)
        proj = h @ w_in[i]
        u, q, k, v = jnp.split(proj, splits, axis=-1)
        y_ssm = _s5_mixer(u, ssm_a_re[i], ssm_a_im[i], ssm_log_dt[i], ssm_b_re[i], ssm_b_im[i],
                          ssm_c_re[i], ssm_c_im[i], ssm_d[i], ssm_w_glu[i])
        y_att = _dilated_attention(q, k, v, q_norm_w[i], k_norm_w[i])
        mixed = jnp.concatenate([y_ssm, y_att], axis=-1).astype(x.dtype) @ w_out[i]
        x = x + gate1[:, None, :] * mixed

        h2 = _modulate(_rms(x, norm2_w[i]), shift2, scale2).astype(x.dtype)
        ffn = (jax.nn.silu(h2 @ w_ffn_gate[i]) * (h2 @ w_ffn_up[i])) @ w_ffn_down[i]
        x = x + gate2[:, None, :] * ffn
    return x
```

```python
import contextlib
import numpy as np
import ml_dtypes
import concourse.bass as bass
import concourse.mybir as mybir
from concourse.bass_utils import run_bass_kernel_spmd

F32 = mybir.dt.float32
BF16 = mybir.dt.bfloat16
ALU = mybir.AluOpType
AF = mybir.ActivationFunctionType
AX = mybir.AxisListType

NCORES = 8
DM = 2048
SEQ = 8192
TOK = 2048
FFH = 5632
NHT = FFH // 128
EPS = 1e-6
NEG = -1.0e30
STAGE = 99
DEBUG = {}


class Res:
    __slots__ = ("w", "rs")

    def __init__(self):
        self.w = {}
        self.rs = []


class Sched:
    ENG = ("pe", "act", "dve", "pool", "sp")
    NDS = 6

    def __init__(self, nc, es):
        self.nc = nc
        self.e = dict(pe=nc.tensor, act=nc.scalar, dve=nc.vector, pool=nc.gpsimd, sp=nc.sync)
        self.sem = {k: es.enter_context(nc.semaphore("s_" + k)) for k in self.ENG}
        self.cnt = {k: 0 for k in self.ENG}
        self.seen = {k: {} for k in self.ENG}
        self.dsem = {}
        self.dval = {}
        self.dnext = {}
        for q in ("sp", "act", "pool"):
            for i in range(self.NDS):
                self.dsem[(q, i)] = es.enter_context(nc.semaphore("d_%s%d" % (q, i)))
                self.dval[(q, i)] = 0
            self.dnext[q] = 0
        self.ninst = 0

    def _wait(self, eng, tok):
        if tok is None:
            return
        kind, key, val = tok
        if kind == "c":
            if eng == "pe" and key == "pe":
                return
            sem = self.sem[key]
        else:
            sem = self.dsem[key]
        if self.seen[eng].get((kind, key), 0) >= val:
            return
        self.e[eng].wait_ge(sem, val)
        self.seen[eng][(kind, key)] = val

    def _deps(self, eng, r, w, wp=()):
        for x in r:
            for t in list(x.w.values()):
                self._wait(eng, t)
        for x in w:
            for t in list(x.w.values()):
                self._wait(eng, t)
            for t in x.rs:
                self._wait(eng, t)
        for x in wp:
            for t in x.rs:
                self._wait(eng, t)

    def _commit(self, tok, r, w, wp=()):
        for x in wp:
            x.w[(tok[0], tok[1])] = tok
        for x in r:
            x.rs.append(tok)
            if len(x.rs) > 24:
                best = {}
                for t in x.rs:
                    k = (t[0], t[1])
                    if k not in best or best[k][2] < t[2]:
                        best[k] = t
                x.rs = list(best.values())
        for x in w:
            x.w = {(tok[0], tok[1]): tok}
            x.rs = []

    def op(self, eng, fn, r=(), w=(), inc=True, wp=()):
        self._deps(eng, r, w, wp)
        inst = fn(self.e[eng])
        self.ninst += 1
        if inc:
            self.cnt[eng] += 1
            inst.then_inc(self.sem[eng], 1)
            tok = ("c", eng, self.cnt[eng])
        else:
            tok = ("c", eng, self.cnt[eng] + 1)
        self._commit(tok, r, w, wp)
        return tok

    def dma(self, q, out, in_, r=(), w=(), wp=()):
        self._deps(q, r, w, wp)
        i = self.dnext[q]
        self.dnext[q] = (i + 1) % self.NDS
        key = (q, i)
        if self.dval[key] > 0:
            self._wait(q, ("d", key, self.dval[key]))
        self.dval[key] += 16
        self.e[q].dma_start(out=out, in_=in_).then_inc(self.dsem[key], 16)
        self.ninst += 1
        tok = ("d", key, self.dval[key])
        self._commit(tok, r, w, wp)
        return tok

    def barrier(self):
        toks = [("c", k, self.cnt[k]) for k in self.ENG if self.cnt[k] > 0]
        toks += [("d", k, v) for k, v in self.dval.items() if v > 0]
        for eng in self.ENG:
            for t in toks:
                if t[0] == "c" and t[1] == eng:
                    continue
                if eng == "pe" and t[0] == "c" and t[1] == "pe":
                    continue
                self._wait(eng, t)


def build_program(stage=99):
    nc = bass.Bass("TRN2", target_bir_lowering=False)
    Dr = {}

    def din(name, shape, dt=F32):
        Dr[name] = nc.dram_tensor(name, list(shape), dt, kind="ExternalInput").ap()

    def dscr(name, shape, dt):
        Dr[name] = nc.dram_tensor(name, list(shape), dt, kind="Internal").ap()

    def dout(name, shape, dt=F32):
        Dr[name] = nc.dram_tensor(name, list(shape), dt, kind="ExternalOutput").ap()

    din("xT", [DM, 4 * TOK])
    din("cT", [128, 16])
    din("smask", [128, 4])
    din("w_ada", [DM, 6 * DM])
    din("badaT", [128, 96])
    din("n1w", [128, 16])
    din("n2w", [128, 16])
    din("w_in", [DM, 4096])
    din("a_re", [128, 32])
    din("a_im", [128, 32])
    din("ldt", [128, 32])
    din("b_re", [128, 512])
    din("b_im", [128, 512])
    din("c_reT", [128, 512])
    din("c_imT", [128, 512])
    din("ssm_d", [128, 8])
    din("w_glu", [1024, 1024])
    din("qnw", [128, 1])
    din("knw", [128, 1])
    din("w_out", [DM, DM])
    din("wg_t", [22, 128, 16 * 256])
    din("wu_t", [22, 128, 16 * 256])
    din("wd_t", [16, 128, NHT * 128])
    din("btab", [8, 128, 12 * 512])
    din("ident", [128, 128])
    din("bdmask", [128, 128])
    dout("outT", [DM, TOK])
    dscr("uT_s", [8, 128, 4 * TOK], BF16)
    dscr("qT_s", [8, 128, TOK], BF16)
    dscr("kT_s", [8, 128, 2 * TOK], BF16)
    dscr("vT_s", [8, 128, 2 * TOK], BF16)
    dscr("x1_s", [16, 128, TOK], F32)
    dscr("h2_s", [16, 128, TOK], BF16)
    if stage < 99:
        dout("dbg", [128, 4096])

    es = contextlib.ExitStack()
    with es:
        S = Sched(nc, es)

        def sb(st, name, shape, dt=F32):
            return st.enter_context(nc.sbuf_tensor(name, list(shape), dt))

        PS = [es.enter_context(nc.psum_tensor("ps%d" % i, [128, 512], F32)) for i in range(8)]
        RPS = [Res() for _ in range(8)]

        identf = sb(es, "identf", [128, 128])
        identb = sb(es, "identb", [128, 128], BF16)
        onesb = sb(es, "onesb", [128, 128], BF16)
        bdm = sb(es, "bdm", [128, 128])
        modT = sb(es, "modT", [128, 96])
        vecs = sb(es, "vecs", [128, 16 * 8])
        smk = sb(es, "smk", [128, 4])
        qkw = sb(es, "qkw", [128, 2])
        epsb = sb(es, "epsb", [128, 1])
        RC = Res()
        S.dma("sp", identf[:], Dr["ident"][:, :], wp=[RC])
        S.dma("sp", bdm[:], Dr["bdmask"][:, :], wp=[RC])
        S.dma("sp", vecs[:, 0:16], Dr["n1w"][:, :], wp=[RC])
        S.dma("sp", vecs[:, 16:32], Dr["n2w"][:, :], wp=[RC])
        S.dma("sp", smk[:], Dr["smask"][:, :], wp=[RC])
        S.dma("sp", qkw[:, 0:1], Dr["qnw"][:, :], wp=[RC])
        S.dma("sp", qkw[:, 1:2], Dr["knw"][:, :], wp=[RC])
        S.op("dve", lambda e: e.tensor_copy(out=identb[:], in_=identf[:]), r=[RC], w=[RC])
        S.op("dve", lambda e: e.memset(onesb[:], 1.0), w=[RC])
        S.op("dve", lambda e: e.memset(epsb[:], EPS), w=[RC])
        S.op("dve", lambda e: e.tensor_scalar(out=qkw[:, 0:1], in0=qkw[:, 0:1], scalar1=float(128.0 ** -0.5), scalar2=None, op0=ALU.mult), r=[RC], w=[RC])
        n1w = vecs[:, 0:16]
        n2w = vecs[:, 16:32]
        s1p = vecs[:, 32:48]
        s2p = vecs[:, 48:64]
        shift1 = modT[:, 0:16]
        gate1 = modT[:, 32:48]
        shift2 = modT[:, 48:64]
        gate2 = modT[:, 80:96]
        RMOD = Res()

        st0 = contextlib.ExitStack()
        st0.__enter__()
        cTs = sb(st0, "cTs", [128, 16])
        condb = sb(st0, "condb", [128, 16], BF16)
        badaT = sb(st0, "badaTs", [128, 96])
        wa = [sb(st0, "wa%d" % i, [128, 16, 128], BF16) for i in range(2)]
        RWA = [Res(), Res()]
        S.dma("sp", cTs[:], Dr["cT"][:, :], wp=[RC])
        S.dma("sp", badaT[:], Dr["badaT"][:, :], wp=[RC])
        S.op("act", lambda e: e.activation(out=condb[:], in_=cTs[:], func=AF.Silu), r=[RC], w=[RC])
        wada_v = Dr["w_ada"].rearrange("(k p) c -> p k c", p=128)

        def p0_block(cb):
            buf = cb % 2
            S.dma("pool", wa[buf][:], wada_v[:, :, cb * 128:(cb + 1) * 128], w=[RWA[buf]])
            for ct in range(1):
                col = cb
                for k in range(16):
                    S.op("pe", lambda e, k=k, ct=ct, col=col: e.matmul(PS[7][:, col:col + 1], lhsT=wa[buf][:, k, ct * 128:(ct + 1) * 128], rhs=condb[:, k:k + 1], start=(k == 0), stop=(k == 15)),
                         r=[RWA[buf], RC], w=[RPS[7]], inc=(k == 15))

        for cb in range(32):
            p0_block(cb)
        S.op("dve", lambda e: e.tensor_tensor(out=modT[:, 0:32], in0=PS[7][:, 0:32], in1=badaT[:, 0:32], op=ALU.add), r=[RPS[7], RC], w=[RMOD])
        S.op("dve", lambda e: e.scalar_tensor_tensor(out=s1p, in0=modT[:, 16:32], scalar=1.0, in1=n1w, op0=ALU.add, op1=ALU.mult), r=[RMOD, RC], w=[RMOD])

        st1 = contextlib.ExitStack()
        st1.__enter__()
        win = sb(st1, "win", [128, 16, 4096], BF16)
        RWIN = Res()
        winv = Dr["w_in"].rearrange("(k p) c -> p k c", p=128)
        RWIN2 = Res()
        for k in range(16):
            S.dma("pool", win[:, k, 0:1024], winv[:, k, 0:1024], wp=[RWIN])
        for k in range(16):
            S.dma("pool", win[:, k, 1024:4096], winv[:, k, 1024:4096], wp=[RWIN2])
        xt = [sb(st1, "xt%d" % i, [128, 16, 256]) for i in range(2)]
        sq = [sb(st1, "sq0", [128, 16, 256], BF16)] * 2
        hT = [sb(st1, "hT%d" % i, [128, 16, 256], BF16) for i in range(2)]
        rstd = [sb(st1, "rstd%d" % i, [128, 256]) for i in range(2)]
        stg = [sb(st1, "stg%d" % i, [128, 2, 256], BF16) for i in range(4)]
        sqq = [sb(st1, "sqq%d" % i, [128, 512], BF16) for i in range(2)]
        rq = [sb(st1, "rq%d" % i, [128, 512]) for i in range(2)]
        RXT = [Res(), Res()]
        RSQ = [Res()] * 2
        RHT = [Res(), Res()]
        RRS = [Res(), Res()]
        RSTG = [Res() for _ in range(4)]
        RSQQ = [Res(), Res()]
        RRQ = [Res(), Res()]
        RSCR = Res()
        xTv = Dr["xT"].rearrange("(k p) c -> p k c", p=128)
        nstg = [0]
        nqk = [0]
        nbank = [0]

        def norm_mod(xtile, Rx, sqb, Rsq, rs, Rrs, hout, Rh, ncol, sp_, sh_, psb, Rpsb):
            S.op("act", lambda e: e.activation(out=sqb, in_=xtile, func=AF.Square), r=[Rx], w=[Rsq])
            for k in range(16):
                S.op("pe", lambda e, k=k: e.matmul(psb, lhsT=onesb[:], rhs=sqb[:, k, :], start=(k == 0), stop=(k == 15)), r=[Rsq, RC], w=[Rpsb], inc=(k == 15))
            S.op("act", lambda e: e.activation(out=rs, in_=psb, func=AF.Sqrt, bias=epsb[:, 0:1], scale=1.0 / DM), r=[Rpsb, RC], w=[Rrs])
            S.op("dve", lambda e: e.reciprocal(out=rs, in_=rs), r=[Rrs], w=[Rrs])
            S.op("dve", lambda e: e.tensor_tensor(out=xtile, in0=xtile, in1=rs.unsqueeze(1).broadcast_to([128, 16, ncol]), op=ALU.mult), r=[Rx, Rrs], w=[Rx])
            for k in range(16):
                if k % 2 == 0:
                    S.op("act", lambda e, k=k: e.activation(out=hout[:, k, :], in_=xtile[:, k, :], func=AF.Identity, bias=sh_[:, k:k + 1], scale=sp_[:, k:k + 1]), r=[Rx, RMOD], wp=[Rh])
                else:
                    S.op("dve", lambda e, k=k: e.tensor_scalar(out=hout[:, k, :], in0=xtile[:, k, :], scalar1=sp_[:, k:k + 1], scalar2=sh_[:, k:k + 1], op0=ALU.mult, op1=ALU.add), r=[Rx, RMOD], wp=[Rh])

        def load_x(t_):
            S.dma("sp", xt[t_ % 2][:], xTv[:, :, (t_ // 8) * TOK + (t_ % 8) * 256: (t_ // 8) * TOK + (t_ % 8) * 256 + 256], w=[RXT[t_ % 2]])

        def norm_a(t_):
            bb = t_ % 2
            S.op("act", lambda e: e.activation(out=sq[bb][:], in_=xt[bb][:], func=AF.Square), r=[RXT[bb]], w=[RSQ[bb]])

        def norm_b(t_):
            bb = t_ % 2
            xtile, rs, hout = xt[bb][:], rstd[bb][:], hT[bb]
            psb = PS[6][:, 0:256]
            for k in range(16):
                S.op("pe", lambda e, k=k: e.matmul(psb, lhsT=onesb[:], rhs=sq[bb][:, k, :], start=(k == 0), stop=(k == 15)), r=[RSQ[bb], RC], w=[RPS[6]], inc=(k == 15))
            S.op("act", lambda e: e.activation(out=rs, in_=psb, func=AF.Sqrt, bias=epsb[:, 0:1], scale=1.0 / DM), r=[RPS[6], RC], w=[RRS[bb]])
            S.op("dve", lambda e: e.reciprocal(out=rs, in_=rs), r=[RRS[bb]], w=[RRS[bb]])
            S.op("dve", lambda e: e.tensor_tensor(out=xtile, in0=xtile, in1=rs.unsqueeze(1).broadcast_to([128, 16, 256]), op=ALU.mult), r=[RXT[bb], RRS[bb]], w=[RXT[bb]])
            for k in range(16):
                if k % 2 == 0:
                    S.op("act", lambda e, k=k: e.activation(out=hout[:, k, :], in_=xtile[:, k, :], func=AF.Identity, bias=shift1[:, k:k + 1], scale=s1p[:, k:k + 1]), r=[RXT[bb], RMOD], wp=[RHT[bb]])
                else:
                    S.op("dve", lambda e, k=k: e.tensor_scalar(out=hout[:, k, :], in0=xtile[:, k, :], scalar1=s1p[:, k:k + 1], scalar2=shift1[:, k:k + 1], op0=ALU.mult, op1=ALU.add), r=[RXT[bb], RMOD], wp=[RHT[bb]])

        pending = []

        def flush():
            while pending:
                pending.pop(0)()

        load_x(0)
        norm_a(0)
        norm_b(0)
        load_x(1)
        for tt in range(32):
            slot = tt // 8
            loc = (tt % 8) * 256
            b = tt % 2
            if tt + 1 < 32:
                norm_a(tt + 1)
            if slot < 2:
                ots = list(range(0, 8))
            elif slot == 2:
                ots = list(range(0, 8)) + list(range(16, 32))
            else:
                ots = list(range(0, 32))
            for oi in range(0, len(ots), 2):
                ot = ots[oi]
                bank = nbank[0] % 5
                nbank[0] += 1
                for half in range(2):
                    o = ot + half
                    for k in range(16):
                        S.op("pe", lambda e, k=k, o=o, half=half, bank=bank: e.matmul(PS[bank][:, half * 256:(half + 1) * 256], lhsT=win[:, k, o * 128:(o + 1) * 128], rhs=hT[b][:, k, :], start=(k == 0), stop=(k == 15)),
                             r=[(RWIN if o < 8 else RWIN2), RHT[b]], w=[RPS[bank]], inc=(k == 15))
                flush()
                if oi == 0 and tt + 1 < 32:
                    norm_b(tt + 1)
                    if tt + 2 < 32:
                        load_x(tt + 2)
                si = nstg[0] % 4
                nstg[0] += 1
                sv = stg[si][:].rearrange("p a c -> p (a c)")
                kind = ot // 8
                if kind == 0:
                    S.op("act", lambda e, bank=bank, sv=sv: e.activation(out=sv, in_=PS[bank][:], func=AF.Copy, scale=smk[:, slot:slot + 1]), r=[RPS[bank], RC], w=[RSTG[si]])
                    dst = Dr["uT_s"][ot:ot + 2, :, slot * TOK + loc: slot * TOK + loc + 256]
                    S.dma("sp", dst.rearrange("t p c -> p t c"), stg[si][:], r=[RSTG[si]], wp=[RSCR])
                elif kind == 3:
                    S.op("dve", lambda e, bank=bank, sv=sv: e.tensor_copy(out=sv, in_=PS[bank][:]), r=[RPS[bank]], w=[RSTG[si]])
                    dst = Dr["vT_s"][ot - 24:ot - 22, :, (slot - 2) * TOK + loc:(slot - 2) * TOK + loc + 256]
                    S.dma("sp", dst.rearrange("t p c -> p t c"), stg[si][:], r=[RSTG[si]], wp=[RSCR])
                else:
                    qi = nqk[0] % 2
                    nqk[0] += 1
                    S.op("act", lambda e, bank=bank, qi=qi: e.activation(out=sqq[qi][:], in_=PS[bank][:], func=AF.Square), r=[RPS[bank]], w=[RSQQ[qi]])
                    wcol = qkw[:, 0:1] if kind == 1 else qkw[:, 1:2]
                    if kind == 1:
                        dst = Dr["qT_s"][ot - 8:ot - 6, :, loc:loc + 256]
                    else:
                        dst = Dr["kT_s"][ot - 16:ot - 14, :, (slot - 2) * TOK + loc:(slot - 2) * TOK + loc + 256]

                    def rest(bank=bank, qi=qi, sv=sv, si=si, wcol=wcol, dst=dst):
                        S.op("pe", lambda e: e.matmul(PS[5][:], lhsT=onesb[:], rhs=sqq[qi][:], start=True, stop=True), r=[RSQQ[qi], RC], w=[RPS[5]])
                        S.op("act", lambda e: e.activation(out=rq[qi][:], in_=PS[5][:], func=AF.Sqrt, bias=epsb[:, 0:1], scale=1.0 / 128.0), r=[RPS[5], RC], w=[RRQ[qi]])
                        S.op("dve", lambda e: e.reciprocal(out=rq[qi][:], in_=rq[qi][:]), r=[RRQ[qi]], w=[RRQ[qi]])
                        S.op("dve", lambda e: e.scalar_tensor_tensor(out=sv, in0=PS[bank][:], scalar=wcol, in1=rq[qi][:], op0=ALU.mult, op1=ALU.mult), r=[RPS[bank], RRQ[qi], RC], w=[RSTG[si]])
                        S.dma("sp", dst.rearrange("t p c -> p t c"), stg[si][:], r=[RSTG[si]], wp=[RSCR])
                    pending.append(rest)
            flush()
            p0_block(32 + 2 * tt)
            p0_block(33 + 2 * tt)
            if tt == 31:
                S.op("dve", lambda e: e.tensor_tensor(out=modT[:, 32:96], in0=PS[7][:, 32:96], in1=badaT[:, 32:96], op=ALU.add), r=[RPS[7], RC], w=[RMOD])
                S.op("dve", lambda e: e.scalar_tensor_tensor(out=s2p, in0=modT[:, 64:80], scalar=1.0, in1=n2w, op0=ALU.add, op1=ALU.mult), r=[RMOD, RC], w=[RMOD])
        S.barrier()
        st1.__exit__(None, None, None)
        st0.__exit__(None, None, None)

        if stage == 1:
            with contextlib.ExitStack() as sd:
                db = sb(sd, "db", [128, 4096], BF16)
                df = sb(sd, "df", [128, 4096])
                RD = Res()
                S.dma("sp", db[:, 0:1024], Dr["uT_s"][0, :, 3 * TOK:3 * TOK + 1024], r=[RSCR], w=[RD])
                S.dma("sp", db[:, 1024:2048], Dr["qT_s"][0, :, 0:1024], r=[RSCR], w=[RD])
                S.dma("sp", db[:, 2048:2560], Dr["kT_s"][0, :, 0:512], r=[RSCR], w=[RD])
                S.dma("sp", db[:, 2560:3072], Dr["kT_s"][0, :, TOK:TOK + 512], r=[RSCR], w=[RD])
                S.dma("sp", db[:, 3072:4096], Dr["vT_s"][0, :, TOK:TOK + 1024], r=[RSCR], w=[RD])
                S.op("dve", lambda e: e.tensor_copy(out=df[:], in_=db[:]), r=[RD], w=[RD])
                S.op("dve", lambda e: e.tensor_copy(out=df[:, 0:96], in_=modT[:]), r=[RD, RMOD], w=[RD])
                S.dma("sp", Dr["dbg"][:, :], df[:], r=[RD], w=[RD])
                S.barrier()
            return nc

        class G:
            pass
        G.nc, G.S, G.es, G.sb, G.Dr, G.PS, G.RPS, G.RC, G.RSCR = nc, S, es, sb, Dr, PS, RPS, RC, RSCR
        G.identf, G.identb, G.onesb, G.bdm, G.stage, G.RMOD = identf, identb, onesb, bdm, stage, RMOD
        G.modT, G.vecs, G.epsb = modT, vecs, epsb
        G.h2_s = Dr["h2_s"]
        stY = contextlib.ExitStack()
        stY.__enter__()
        G.YS = sb(stY, "YS", [128, 8, TOK], BF16)
        G.sub = DEBUG.get('sub', 0)
        G.RYS = Res()
        phase_ssm(G)
        if stage == 2:
            with contextlib.ExitStack() as sd:
                df = sb(sd, "df", [128, 4096])
                RD = Res()
                S.op("dve", lambda e: e.tensor_copy(out=df[:, 0:2048], in_=G.YS[:, 0, :]), r=[G.RYS], w=[RD])
                S.op("dve", lambda e: e.tensor_copy(out=df[:, 2048:4096], in_=G.YS[:, 7, :]), r=[G.RYS], w=[RD])
                S.dma("sp", Dr["dbg"][:, :], df[:], r=[RD], w=[RD])
                S.barrier()
            stY.__exit__(None, None, None)
            return nc
        G.YA = sb(stY, "YA", [128, 8, TOK], BF16)
        G.RYA = Res()
        phase_attn(G)
        if stage == 3:
            with contextlib.ExitStack() as sd:
                df = sb(sd, "df", [128, 4096])
                RD = Res()
                S.op("dve", lambda e: e.tensor_copy(out=df[:, 0:2048], in_=G.YA[:, 0, :]), r=[G.RYA], w=[RD])
                S.op("dve", lambda e: e.tensor_copy(out=df[:, 2048:4096], in_=G.YA[:, 7, :]), r=[G.RYA], w=[RD])
                S.dma("sp", Dr["dbg"][:, :], df[:], r=[RD], w=[RD])
                S.barrier()
            stY.__exit__(None, None, None)
            return nc
        phase_out(G, stY)
    return nc


def phase_ssm(G):
    nc, S, sb, Dr, PS, RPS, RC = G.nc, G.S, G.sb, G.Dr, G.PS, G.RPS, G.RC
    identf, identb, bdm = G.identf, G.identb, G.bdm
    stA = contextlib.ExitStack()
    stA.__enter__()
    YG = sb(stA, "YG", [128, 8, TOK], BF16)
    RYG = Res()
    stB = contextlib.ExitStack()
    stB.__enter__()
    NPL = 36
    pl = sb(stB, "ssm_pl", [128, NPL, 32])
    names = "are aim dt z p e1 t t2 qv wi r wr A B C ur ui ar1 ai rho1 rho vr vi den zr zi Dd".split()
    P = {n: pl[:, i, :] for i, n in enumerate(names)}
    Tm = [sb(stB, "ssm_T%d" % i, [128, 1024]) for i in range(4)]
    Bbr = sb(stB, "Bbr", [128, 32, 16])
    Bbi = sb(stB, "Bbi", [128, 32, 16])
    Cre = sb(stB, "CreT", [128, 32, 16])
    Cim = sb(stB, "CimT", [128, 32, 16])
    Pr = sb(stB, "Pr", [128, 17, 32])
    Pi = sb(stB, "Pi", [128, 17, 32])
    P2 = sb(stB, "P2", [128, 16, 2, 128], BF16)
    Q2 = sb(stB, "Q2", [128, 2, 128], BF16)
    BD = sb(stB, "BD", [128, 16, 128], BF16)
    WS = sb(stB, "WSpad", [128, 2, 16, 2, 128], BF16)
    WH = sb(stB, "WHpad", [128, 16, 2, 4, 128], BF16)
    uT = sb(stB, "uTt", [128, 4 * TOK], BF16)
    Er = sb(stB, "Er", [128, 4, 513])
    Ei = sb(stB, "Ei", [128, 4, 513])
    Gre = sb(stB, "Gre", [128, 513])
    Gim = sb(stB, "Gim", [128, 513])
    Hb = sb(stB, "Hb", [128, 4, 2, 128], BF16)
    pmask = sb(stB, "pmask", [128, 4])
    dsm = sb(stB, "dsm", [128, 8])
    RS = Res()
    R_T, R_P2, R_BD, R_WS, R_WH, R_E, R_G = Res(), Res(), Res(), Res(), Res(), Res(), Res()
    ctx = {"r": [RS], "w": [RS]}

    def tt(o, a, b, op):
        S.op("dve", lambda e: e.tensor_tensor(out=o, in0=a, in1=b, op=op), r=ctx["r"], w=ctx["w"])

    def ts(o, a, s1, s2, op0, op1=None):
        if op1 is None:
            S.op("dve", lambda e: e.tensor_scalar(out=o, in0=a, scalar1=s1, scalar2=None, op0=op0), r=ctx["r"], w=ctx["w"])
        else:
            S.op("dve", lambda e: e.tensor_scalar(out=o, in0=a, scalar1=s1, scalar2=s2, op0=op0, op1=op1), r=ctx["r"], w=ctx["w"])

    def stt(o, a, sc, b, op0, op1):
        S.op("dve", lambda e: e.scalar_tensor_tensor(out=o, in0=a, scalar=sc, in1=b, op0=op0, op1=op1), r=ctx["r"], w=ctx["w"])

    def cp(o, a):
        S.op("dve", lambda e: e.tensor_copy(out=o, in_=a), r=ctx["r"], w=ctx["w"])

    M, AD, SUB = ALU.mult, ALU.add, ALU.subtract
    Bre = Tm[2][:, 0:512].rearrange("p (a c) -> p a c", c=16)
    Bim = Tm[3][:, 0:512].rearrange("p (a c) -> p a c", c=16)
    S.dma("sp", P["are"], Dr["a_re"][:, :], wp=[RS])
    S.dma("sp", P["aim"], Dr["a_im"][:, :], wp=[RS])
    S.dma("sp", P["dt"], Dr["ldt"][:, :], wp=[RS])
    S.dma("sp", Tm[2][:, 0:512], Dr["b_re"][:, :], wp=[RS])
    S.dma("sp", Tm[3][:, 0:512], Dr["b_im"][:, :], wp=[RS])
    S.dma("sp", Cre[:].rearrange("p a c -> p (a c)"), Dr["c_reT"][:, :], wp=[RS])
    S.dma("sp", Cim[:].rearrange("p a c -> p (a c)"), Dr["c_imT"][:, :], wp=[RS])
    S.dma("sp", dsm[:], Dr["ssm_d"][:, :], wp=[RS])
    S.op("act", lambda e: e.activation(out=P["dt"], in_=P["dt"], func=AF.Exp), r=[RS], w=[RS])
    for pp in range(4):
        tt(pmask[:, pp:pp + 1], bdm[:, 32 * pp:32 * pp + 1], bdm[:, 32 * pp + 16:32 * pp + 17], AD)
    S.op("pool", lambda e: e.memset(P2[:].rearrange("p a b c -> p (a b c)"), 0.0), wp=[RS, R_P2])
    S.op("pool", lambda e: e.memset(Q2[:].rearrange("p a c -> p (a c)"), 0.0), wp=[RS, R_P2])
    S.op("pool", lambda e: e.memset(WH[:].rearrange("p a b c d -> p (a b c d)"), 0.0), wp=[RS, R_WH])
    S.op("pool", lambda e: e.memset(Gre[:, 0:1], 0.0), wp=[RS, R_G])
    S.op("pool", lambda e: e.memset(Gim[:, 0:1], 0.0), wp=[RS, R_G])
    tt(P["z"], P["are"], P["dt"], M)
    ts(P["p"], P["z"], 0.2, 1.0, M, AD)
    for cc in (0.25, 1.0 / 3.0, 0.5):
        tt(P["p"], P["p"], P["z"], M)
        ts(P["p"], P["p"], cc, 1.0, M, AD)
    tt(P["e1"], P["p"], P["z"], M)
    tt(P["t"], P["aim"], P["dt"], M)
    ts(P["t"], P["t"], 1.0 / 64.0, None, M)
    tt(P["t2"], P["t"], P["t"], M)
    ts(P["qv"], P["t2"], -1.0 / 72.0, 1.0, M, AD)
    for cc in (42.0, 20.0, 6.0):
        tt(P["qv"], P["qv"], P["t2"], M)
        ts(P["qv"], P["qv"], -1.0 / cc, 1.0, M, AD)
    tt(P["wi"], P["qv"], P["t"], M)
    ts(P["r"], P["t2"], -1.0 / 90.0, 1.0, M, AD)
    for cc in (56.0, 30.0, 12.0):
        tt(P["r"], P["r"], P["t2"], M)
        ts(P["r"], P["r"], -1.0 / cc, 1.0, M, AD)
    tt(P["r"], P["r"], P["t2"], M)
    ts(P["wr"], P["r"], -0.5, None, M)

    def dbl(wr, wi):
        tt(P["A"], wr, wr, M)
        tt(P["B"], wi, wi, M)
        tt(P["C"], wr, wi, M)
        stt(wr, wr, 2.0, P["A"], M, AD)
        tt(wr, wr, P["B"], SUB)
        tt(P["C"], P["C"], wi, AD)
        ts(wi, P["C"], 2.0, None, M)

    for _ in range(6):
        dbl(P["wr"], P["wi"])
    cp(P["ur"], P["wr"])
    cp(P["ui"], P["wi"])
    tt(P["A"], P["e1"], P["ur"], M)
    tt(P["ar1"], P["e1"], P["ur"], AD)
    tt(P["ar1"], P["ar1"], P["A"], AD)
    tt(P["A"], P["e1"], P["ui"], M)
    tt(P["ai"], P["ui"], P["A"], AD)
    cp(P["rho1"], P["e1"])
    for _ in range(4):
        tt(P["A"], P["rho1"], P["rho1"], M)
        stt(P["rho1"], P["rho1"], 2.0, P["A"], M, AD)
    ts(P["rho"], P["rho1"], 1.0, None, AD)
    cp(P["vr"], P["ur"])
    cp(P["vi"], P["ui"])
    for _ in range(4):
        dbl(P["vr"], P["vi"])
    tt(P["A"], P["are"], P["are"], M)
    tt(P["B"], P["aim"], P["aim"], M)
    tt(P["den"], P["A"], P["B"], AD)
    S.op("dve", lambda e: e.reciprocal(out=P["den"], in_=P["den"]), r=[RS], w=[RS])
    tt(P["A"], P["ar1"], P["are"], M)
    tt(P["B"], P["ai"], P["aim"], M)
    tt(P["zr"], P["A"], P["B"], AD)
    tt(P["zr"], P["zr"], P["den"], M)
    tt(P["A"], P["ai"], P["are"], M)
    tt(P["B"], P["ar1"], P["aim"], M)
    tt(P["zi"], P["A"], P["B"], SUB)
    tt(P["zi"], P["zi"], P["den"], M)
    zrb = P["zr"].unsqueeze(2).broadcast_to([128, 32, 16])
    zib = P["zi"].unsqueeze(2).broadcast_to([128, 32, 16])
    t0v = Tm[0][:, 0:512].rearrange("p (a c) -> p a c", c=16)
    t1v = Tm[1][:, 0:512].rearrange("p (a c) -> p a c", c=16)
    tt(t0v, Bre, zrb, M)
    tt(t1v, Bim, zib, M)
    tt(Bbr[:], t0v, t1v, SUB)
    tt(t0v, Bim, zrb, M)
    tt(t1v, Bre, zib, M)
    tt(Bbi[:], t0v, t1v, AD)
    S.op("dve", lambda e: e.memset(Pr[:, 0, :], 0.0), r=[RS], w=[RS])
    S.op("dve", lambda e: e.memset(Pi[:, 0, :], 0.0), r=[RS], w=[RS])
    cp(Pr[:, 1, :], P["ar1"])
    cp(Pi[:, 1, :], P["ai"])
    for lev in range(4):
        n = 1 << lev
        cr = Pr[:, n:n + 1, :].broadcast_to([128, n, 32])
        ci = Pi[:, n:n + 1, :].broadcast_to([128, n, 32])
        xr = Pr[:, 1:n + 1, :]
        xi = Pi[:, 1:n + 1, :]
        tv = [t[:, 0:32 * n].rearrange("p (a c) -> p a c", c=32) for t in Tm[0:2]]
        tt(tv[0], xr, cr, M)
        tt(tv[1], xi, ci, M)
        tt(tv[0], tv[0], tv[1], SUB)
        tt(tv[0], tv[0], xr, AD)
        tt(Pr[:, n + 1:2 * n + 1, :], tv[0], cr, AD)
        tt(tv[0], xr, ci, M)
        tt(tv[1], xi, cr, M)
        tt(tv[0], tv[0], tv[1], AD)
        tt(tv[0], tv[0], xi, AD)
        tt(Pi[:, n + 1:2 * n + 1, :], tv[0], ci, AD)

    Elo_r = sb(stB, "Elo_r", [128, 16, 17])
    Elo_i = sb(stB, "Elo_i", [128, 16, 17])
    Ehi_r = sb(stB, "Ehi_r", [128, 16, 33])
    Ehi_i = sb(stB, "Ehi_i", [128, 16, 33])

    def build_tab(Er_, Ei_, br_, bi_, nlev):
        S.op("dve", lambda e: e.memset(Er_[:, :, 0:1], 0.0), r=[RS], w=[RS, R_T, R_E])
        S.op("dve", lambda e: e.memset(Ei_[:, :, 0:1], 0.0), r=[RS], w=[RS, R_T, R_E])
        cp(Er_[:, :, 1:2], br_)
        cp(Ei_[:, :, 1:2], bi_)
        for lev in range(nlev):
            n = 1 << lev
            cr = Er_[:, :, n:n + 1].broadcast_to([128, 16, n])
            ci = Ei_[:, :, n:n + 1].broadcast_to([128, 16, n])
            xr = Er_[:, :, 1:n + 1]
            xi = Ei_[:, :, 1:n + 1]
            tv = [t[:, 0:16 * n].rearrange("p (a c) -> p a c", a=16) for t in Tm]
            tt(tv[0], xr, cr, M)
            tt(tv[1], xi, ci, M)
            tt(tv[2], xr, ci, M)
            tt(tv[3], xi, cr, M)
            tt(tv[0], tv[0], tv[1], SUB)
            tt(tv[2], tv[2], tv[3], AD)
            tt(tv[0], tv[0], xr, AD)
            tt(tv[2], tv[2], xi, AD)
            tt(Er_[:, :, n + 1:2 * n + 1], tv[0], cr, AD)
            tt(Ei_[:, :, n + 1:2 * n + 1], tv[2], ci, AD)

    def build_coarse(half):
        ctx["r"], ctx["w"] = [RS], [RS, R_T, R_E]
        hs = slice(16 * half, 16 * half + 16)
        build_tab(Elo_r, Elo_i, P["vr"][:, hs].unsqueeze(2), P["vi"][:, hs].unsqueeze(2), 4)
        build_tab(Ehi_r, Ehi_i, Elo_r[:, :, 16:17], Elo_i[:, :, 16:17], 5)
        ts(Elo_r[:].rearrange("p a c -> p (a c)"), Elo_r[:].rearrange("p a c -> p (a c)"), 1.0, None, AD)
        ts(Ehi_r[:].rearrange("p a c -> p (a c)"), Ehi_r[:].rearrange("p a c -> p (a c)"), 1.0, None, AD)

    RUT = Res()
    RHB = Res()
    T4 = [t[:].rearrange("p (m a c) -> p m a c", m=16, a=4) for t in Tm]
    evac_n = [0]

    def evac_eng():
        evac_n[0] += 1
        return "act"

    for T in range(8):
        ps4 = slice(4 * T, 4 * T + 4)
        S.dma("sp", uT[:], Dr["uT_s"][T, :, :], r=[G.RSCR], w=[RUT])
        ctx["r"], ctx["w"] = [RS], [R_T]
        PrV = Pr[:, 0:16, ps4].unsqueeze(3).broadcast_to([128, 16, 4, 16])
        PiV = Pi[:, 0:16, ps4].unsqueeze(3).broadcast_to([128, 16, 4, 16])
        BrV = Bbr[:, ps4, :].unsqueeze(1).broadcast_to([128, 16, 4, 16])
        BiV = Bbi[:, ps4, :].unsqueeze(1).broadcast_to([128, 16, 4, 16])
        tt(T4[0], PrV, BrV, M)
        tt(T4[1], PiV, BiV, M)
        tt(T4[2], PrV, BiV, M)
        tt(T4[3], PiV, BrV, M)
        tt(T4[0], T4[0], T4[1], SUB)
        tt(T4[2], T4[2], T4[3], AD)
        ctx["r"], ctx["w"] = [RS, R_T], [R_P2]
        for hh in range(2):
            rows = slice(64 * hh, 64 * hh + 64)
            dre = P2[rows, :, 0, :].rearrange("p m (a c) -> p m a c", c=32)[:, :, :, 16 * hh:16 * hh + 16]
            dim_ = P2[rows, :, 1, :].rearrange("p m (a c) -> p m a c", c=32)[:, :, :, 16 * hh:16 * hh + 16]
            tt(dre, T4[0][rows], BrV[rows], AD)
            tt(dim_, T4[2][rows], BiV[rows], AD)
            qre = Q2[rows, 0, :].rearrange("p (a c) -> p a c", c=32)[:, :, 16 * hh:16 * hh + 16]
            qim = Q2[rows, 1, :].rearrange("p (a c) -> p a c", c=32)[:, :, 16 * hh:16 * hh + 16]
            cp(qre, Cre[rows, ps4, :])
            ts(qim, Cim[rows, ps4, :], -1.0, None, M)
        for q in range(4):
            bank = q
            for sl in range(4):
                m = 4 * q + sl
                S.op("pe", lambda e, m=m, sl=sl, bank=bank: e.matmul(PS[bank][:, sl * 128:(sl + 1) * 128], lhsT=P2[:, m, 0, :], rhs=Q2[:, 0, :], start=True, stop=False), r=[R_P2], w=[RPS[bank]], inc=False)
                S.op("pe", lambda e, m=m, sl=sl, bank=bank: e.matmul(PS[bank][:, sl * 128:(sl + 1) * 128], lhsT=P2[:, m, 1, :], rhs=Q2[:, 1, :], start=False, stop=True), r=[R_P2], w=[RPS[bank]], inc=(sl == 3))
            S.op("dve", lambda e, q=q, bank=bank: e.tensor_tensor(out=BD[:, 4 * q:4 * q + 4, :], in0=PS[bank][:].rearrange("p (a c) -> p a c", c=128), in1=bdm[:].unsqueeze(1).broadcast_to([128, 4, 128]), op=M), r=[RPS[bank], RC], w=[R_BD])
        ctx["r"], ctx["w"] = [RS, RC], [R_BD]
        stt(BD[:, 0, :], identf[:], dsm[:, T:T + 1], BD[:, 0, :], M, AD)
        if getattr(G, 'sub', 0) == 2:
            S.barrier(); stB.__exit__(None, None, None); stA.__exit__(None, None, None); return
        ctx["r"], ctx["w"] = [RS], [R_T]
        PrV = Pr[:, 1:17, ps4].unsqueeze(3).broadcast_to([128, 16, 4, 16])
        PiV = Pi[:, 1:17, ps4].unsqueeze(3).broadcast_to([128, 16, 4, 16])
        CrV = Cre[:, ps4, :].unsqueeze(1).broadcast_to([128, 16, 4, 16])
        CiV = Cim[:, ps4, :].unsqueeze(1).broadcast_to([128, 16, 4, 16])
        tt(T4[0], PrV, CrV, M)
        tt(T4[1], PiV, CiV, M)
        tt(T4[2], PrV, CiV, M)
        tt(T4[3], PiV, CrV, M)
        tt(T4[0], T4[0], T4[1], SUB)
        tt(T4[2], T4[2], T4[3], AD)
        tt(T4[2], T4[2], CiV, AD)
        ctx["r"], ctx["w"] = [RS, R_T], [R_WH]
        for hh in range(2):
            rows = slice(64 * hh, 64 * hh + 64)
            dre = WH[rows, :, 0, :, :].rearrange("p j a c -> p j (a c)").rearrange("p j (a b) -> p j a b", b=32)[:, :, 0:16:5, 16 * hh:16 * hh + 16]
            dim_ = WH[rows, :, 1, :, :].rearrange("p j a c -> p j (a c)").rearrange("p j (a b) -> p j a b", b=32)[:, :, 0:16:5, 16 * hh:16 * hh + 16]
            tt(dre, T4[0][rows], CrV[rows], AD)
            ts(dim_, T4[2][rows], -1.0, None, M)
        if T % 4 == 0:
            build_coarse(T // 4)
        ctx["r"], ctx["w"] = [RS], [R_T, R_E]
        pc0 = 4 * (T % 4)
        for a_ in range(4):
            hra = Ehi_r[:, pc0 + a_, 0:32].unsqueeze(2).broadcast_to([128, 32, 16])
            hia = Ehi_i[:, pc0 + a_, 0:32].unsqueeze(2).broadcast_to([128, 32, 16])
            lra = Elo_r[:, pc0 + a_, 0:16].unsqueeze(1).broadcast_to([128, 32, 16])
            lia = Elo_i[:, pc0 + a_, 0:16].unsqueeze(1).broadcast_to([128, 32, 16])
            t0 = Tm[0][:, 0:512].rearrange("p (j l) -> p j l", j=32)
            t1 = Tm[1][:, 0:512].rearrange("p (j l) -> p j l", j=32)
            era = Er[:, a_, 0:512].rearrange("p (j l) -> p j l", j=32)
            eia = Ei[:, a_, 0:512].rearrange("p (j l) -> p j l", j=32)
            tt(t0, hra, lra, M)
            tt(t1, hia, lia, M)
            tt(era, t0, t1, SUB)
            tt(t0, hra, lia, M)
            tt(t1, hia, lra, M)
            tt(eia, t0, t1, AD)
        cp(Er[:, :, 512:513], Ehi_r[:, pc0:pc0 + 4, 32:33])
        cp(Ei[:, :, 512:513], Ehi_i[:, pc0:pc0 + 4, 32:33])
        if getattr(G, 'sub', 0) == 3:
            S.barrier(); stB.__exit__(None, None, None); stA.__exit__(None, None, None); return
        uTv = uT[:].rearrange("p (sl r i) -> p sl r i", sl=4, r=16)
        for ph in range(2):
            for q in range(8):
                bank = q % 4
                for sl in range(4):
                    idx = 4 * q + sl
                    m, ri = idx // 2, idx % 2
                    S.op("pe", lambda e, m=m, ri=ri, sl=sl, bank=bank: e.matmul(PS[bank][:, sl * 128:(sl + 1) * 128], lhsT=P2[:, m, ri, :], rhs=identb[:], start=True, stop=True), r=[R_P2, RC], w=[RPS[bank]], inc=(sl == 3))
                for pl_ in range(2):
                    pp = 2 * ph + pl_
                    dst = WS[:, pl_, 2 * q:2 * q + 2, :, :].rearrange("p m r c -> p (m r c)")
                    eng = evac_eng()
                    if eng == "act":
                        S.op("act", lambda e, dst=dst, bank=bank, pp=pp: e.activation(out=dst, in_=PS[bank][:], func=AF.Copy, scale=pmask[:, pp:pp + 1]), r=[RPS[bank], RS], wp=[R_WS])
                    else:
                        S.op("dve", lambda e, dst=dst, bank=bank, pp=pp: e.tensor_scalar(out=dst, in0=PS[bank][:], scalar1=pmask[:, pp:pp + 1], scalar2=None, op0=M), r=[RPS[bank], RS], wp=[R_WS])
            for pl_ in range(2):
                pp = 2 * ph + pl_
                for ri in range(2):
                    bank = 2 * pl_ + ri
                    for m in range(16):
                        S.op("pe", lambda e, pl_=pl_, m=m, ri=ri, bank=bank: e.matmul(PS[bank][:].rearrange("p (a c) -> p a c", a=4), lhsT=WS[:, pl_, m, ri, :], rhs=uTv[:, :, 15 - m, :], start=(m == 0), stop=(m == 15)),
                             r=[R_WS, RUT], w=[RPS[bank]], inc=(m == 15))
            for pl_ in range(2):
                pp = 2 * ph + pl_
                b0, b1 = 2 * pl_, 2 * pl_ + 1
                cosv = Er[:, pp, 1:513]
                sinv = Ei[:, pp, 1:513]
                X0, X1, X2, X3 = (t[:, 0:512] for t in Tm)
                ctx["r"], ctx["w"] = [RS, R_E, R_G], [R_T]
                S.op("dve", lambda e, b0=b0: e.tensor_tensor(out=X0, in0=PS[b0][:], in1=cosv, op=M), r=[RS, R_E, RPS[b0]], w=[R_T])
                S.op("dve", lambda e, b1=b1: e.tensor_tensor(out=X1, in0=PS[b1][:], in1=sinv, op=M), r=[RS, R_E, RPS[b1]], w=[R_T])
                tt(X0, X0, X1, AD)
                S.op("dve", lambda e, b1=b1: e.tensor_tensor(out=X2, in0=PS[b1][:], in1=cosv, op=M), r=[RS, R_E, RPS[b1]], w=[R_T])
                S.op("dve", lambda e, b0=b0: e.tensor_tensor(out=X3, in0=PS[b0][:], in1=sinv, op=M), r=[RS, R_E, RPS[b0]], w=[R_T])
                tt(X2, X2, X3, SUB)
                rb = P["rho"][:, 4 * T + pp:4 * T + pp + 1].broadcast_to([128, 512])
                S.op("dve", lambda e, rb=rb: e.tensor_tensor_scan(out=Gre[:, 1:513], data0=rb, data1=X0, initial=0.0, op0=M, op1=AD), r=[RS, R_T], w=[R_G])
                S.op("dve", lambda e, rb=rb: e.tensor_tensor_scan(out=Gim[:, 1:513], data0=rb, data1=X2, initial=0.0, op0=M, op1=AD), r=[RS, R_T], w=[R_G])
                ck = Er[:, pp, 384:512]
                sk = Ei[:, pp, 384:512]
                gr = Gre[:, 384:512]
                gi = Gim[:, 384:512]
                a0, a1 = Tm[0][:, 512:640], Tm[1][:, 512:640]
                tt(a0, gr, ck, M)
                tt(a1, gi, sk, M)
                S.op("dve", lambda e, pp=pp: e.tensor_tensor(out=Hb[:, pp, 0, :], in0=a0, in1=a1, op=SUB), r=[R_T], w=[RHB])
                tt(a0, gr, sk, M)
                tt(a1, gi, ck, M)
                S.op("dve", lambda e, pp=pp: e.tensor_tensor(out=Hb[:, pp, 1, :], in0=a0, in1=a1, op=AD), r=[R_T], w=[RHB])
        if getattr(G, 'sub', 0) == 4:
            S.barrier(); stB.__exit__(None, None, None); stA.__exit__(None, None, None); return
        own0 = 3 * TOK
        for j in range(16):
            bank = 4 + j // 4
            reg = PS[bank][:, (j % 4) * 128:(j % 4 + 1) * 128]
            first = True
            for s in range(j + 1):
                S.op("pe", lambda e, j=j, s=s, reg=reg, first=first: e.matmul(reg, lhsT=BD[:, j - s, :], rhs=uT[:, own0 + s * 128: own0 + (s + 1) * 128], start=first, stop=False), r=[R_BD, RUT], w=[RPS[bank]], inc=False)
                first = False
            for pp in range(4):
                for ri in range(2):
                    last = (pp == 3 and ri == 1)
                    S.op("pe", lambda e, j=j, pp=pp, ri=ri, reg=reg, last=last: e.matmul(reg, lhsT=WH[:, j, ri, pp, :], rhs=Hb[:, pp, ri, :], start=False, stop=last), r=[R_WH, RHB], w=[RPS[bank]], inc=(last and j % 4 == 3))
            if j % 4 == 3:
                S.op("act", lambda e, bank=bank, T=T, j=j: e.activation(out=YG[:, T, (j // 4) * 512:(j // 4 + 1) * 512], in_=PS[bank][:], func=AF.Gelu_apprx_tanh), r=[RPS[bank]], w=[RYG])
        if getattr(G, 'sub', 0) == 5:
            break
    S.barrier()
    stB.__exit__(None, None, None)
    if getattr(G, 'sub', 0) == 6:
        stA.__exit__(None, None, None); return
    with contextlib.ExitStack() as stC:
        sg = [sb(stC, "sg%d" % i, [128, 512]) for i in range(2)]
        wglu = sb(stC, "wglu", [128, 8, 1024], BF16)
        RWG = Res()
        wgv = Dr["w_glu"].rearrange("(k p) c -> p k c", p=128)
        for k in range(8):
            S.dma("pool", wglu[:, k, :], wgv[:, k, :], wp=[RWG])
        RSG = [Res(), Res()]
        n = 0
        for ot in range(8):
            for tb in range(4):
                bank = n % 4
                si = n % 2
                n += 1
                for k in range(8):
                    S.op("pe", lambda e, k=k, ot=ot, tb=tb, bank=bank: e.matmul(PS[bank][:], lhsT=wglu[:, k, ot * 128:(ot + 1) * 128], rhs=YG[:, k, tb * 512:(tb + 1) * 512], start=(k == 0), stop=(k == 7)), r=[RWG, RYG], w=[RPS[bank]], inc=(k == 7))
                S.op("act", lambda e, bank=bank, si=si: e.activation(out=sg[si][:], in_=PS[bank][:], func=AF.Sigmoid), r=[RPS[bank]], w=[RSG[si]])
                S.op("dve", lambda e, ot=ot, tb=tb, si=si: e.tensor_tensor(out=G.YS[:, ot, tb * 512:(tb + 1) * 512], in0=YG[:, ot, tb * 512:(tb + 1) * 512], in1=sg[si][:], op=M), r=[RSG[si], RYG], w=[G.RYS])
        S.barrier()
    stA.__exit__(None, None, None)


def phase_attn(G):
    nc, S, sb, Dr, PS, RPS, RC = G.nc, G.S, G.sb, G.Dr, G.PS, G.RPS, G.RC
    identb, onesb = G.identb, G.onesb
    M = ALU.mult
    with contextlib.ExitStack() as stT:
        btbs = [sb(stT, "btb%d" % i, [128, 12, 512], BF16) for i in range(2)]
        RBTS = [Res(), Res()]
        RBT = Res()
        qTB = [sb(stT, "qTh%d" % i, [128, TOK], BF16) for i in range(2)]
        k3B = [sb(stT, "k3%d" % i, [128, 2 * TOK], BF16) for i in range(2)]
        v3B = [sb(stT, "v3%d" % i, [128, 2 * TOK], BF16) for i in range(2)]
        RQB, RKB, RVB3 = [Res(), Res()], [Res(), Res()], [Res(), Res()]
        k2 = sb(stT, "k2", [128, 20 * 128], BF16)
        v2 = sb(stT, "v2", [128, 20 * 128], BF16)
        k1 = sb(stT, "k1", [128, 17 * 128], BF16)
        v1 = sb(stT, "v1", [128, 17 * 128], BF16)
        Vb = sb(stT, "Vb", [128, 72, 128], BF16)
        PT = [sb(stT, "PT%d" % i, [128, 512], BF16) for i in range(4)]
        rden = sb(stT, "rden", [128, 512])
        numS = sb(stT, "numS", [128, 512])
        denS = sb(stT, "denS", [128, 512])
        RNUM = Res()
        cnt = [0]
        RK2, RV2, RVB, RDEN = Res(), Res(), Res(), Res()

        def load_head(h_):
            bb = h_ % 2
            S.dma("sp", qTB[bb][:], Dr["qT_s"][h_, :, :], r=[G.RSCR], w=[RQB[bb]])
            S.dma("sp", k3B[bb][:], Dr["kT_s"][h_, :, :], r=[G.RSCR], w=[RKB[bb]])
            S.dma("sp", v3B[bb][:], Dr["vT_s"][h_, :, :], r=[G.RSCR], w=[RVB3[bb]])

        load_head(0)
        RPT = [Res() for _ in range(6)]
        nev = [0]
        for h in range(8):
            btb = btbs[h % 2]
            RBTh = RBTS[h % 2]
            S.dma("pool", btb[:].rearrange("p a c -> p (a c)"), Dr["btab"][h, :, :], w=[RBTh])
            qT, k3, v3 = qTB[h % 2], k3B[h % 2], v3B[h % 2]
            RQ, RK, RV = RQB[h % 2], RKB[h % 2], RVB3[h % 2]
            if h + 1 < 8:
                load_head(h + 1)
            for (src, dst2, dst1, Rs, Rd, eng) in ((k3, k2, k1, RK, RK2, "dve"), (v3, v2, v1, RV, RV2, "act")):
                own = src[:, TOK:2 * TOK]
                halo = src[:, 0:TOK]
                ops = []
                for r4 in range(4):
                    o = dst2[:, (4 + 4 * r4) * 128:(8 + 4 * r4) * 128].rearrange("p (s4 mm ii) -> p s4 mm ii", s4=4, mm=4)
                    i_ = own.rearrange("p (mm r4 s4 ii) -> p mm r4 s4 ii", mm=4, r4=4, s4=4)[:, :, r4, :, :].rearrange("p mm s4 ii -> p s4 mm ii")
                    ops.append((o, i_))
                    o = dst2[:, r4 * 128:(r4 + 1) * 128].rearrange("p (mm ii) -> p mm ii", mm=4)
                    i_ = halo.rearrange("p (mm r4 s4 ii) -> p mm r4 s4 ii", mm=4, r4=4, s4=4)[:, :, r4, 3, :]
                    ops.append((o, i_))
                o = dst1[:, 128:17 * 128].rearrange("p (n r c) -> p n r c", n=16, r=16)
                i_ = own.rearrange("p (r n c) -> p n r c", r=16, n=16)
                ops.append((o, i_))
                o = dst1[:, 0:128].rearrange("p (r c) -> p r c", r=16)
                i_ = halo.rearrange("p (r n c) -> p r n c", r=16, n=16)[:, :, 15, :]
                ops.append((o, i_))
                for (o, i_) in ops:
                    if eng == "dve":
                        S.op("dve", lambda e, o=o, i_=i_: e.tensor_copy(out=o, in_=i_), r=[Rs], wp=[Rd])
                    else:
                        S.op("act", lambda e, o=o, i_=i_: e.activation(out=o, in_=i_, func=AF.Copy), r=[Rs], wp=[Rd])
            vsrc = [(v3, b) for b in range(32)] + [(v2, b) for b in range(20)] + [(v1, b) for b in range(17)]
            for q in range(18):
                bank = q % 8
                blks = vsrc[4 * q:4 * q + 4]
                for sl, (vt, b) in enumerate(blks):
                    S.op("pe", lambda e, vt=vt, b=b, sl=sl, bank=bank: e.matmul(PS[bank][:, sl * 128:(sl + 1) * 128], lhsT=vt[:, b * 128:(b + 1) * 128], rhs=identb[:], start=True, stop=True),
                         r=[RV, RV2, RC], w=[RPS[bank]], inc=(sl == len(blks) - 1))
                nb_ = len(blks)
                nev[0] += 1
                dst = Vb[:, 4 * q:4 * q + nb_, :].rearrange("p a c -> p (a c)")
                if nev[0] % 3:
                    S.op("act", lambda e, dst=dst, bank=bank, nb_=nb_: e.activation(out=dst, in_=PS[bank][:, 0:nb_ * 128], func=AF.Copy), r=[RPS[bank]], wp=[RVB])
                else:
                    S.op("dve", lambda e, dst=dst, bank=bank, nb_=nb_: e.tensor_copy(out=dst, in_=PS[bank][:, 0:nb_ * 128]), r=[RPS[bank]], wp=[RVB])
            q2v = qT[:].rearrange("p (mm r4 s4 ii) -> p mm r4 s4 ii", mm=4, r4=4, s4=4)
            q1v = qT[:].rearrange("p (mm r4 n c) -> p mm r4 n c", mm=4, r4=4, n=16)
            for g in range(4):
                brs = []
                us = []
                for mm in range(4):
                    r16 = 4 * mm + g
                    us.append((k3, qT[:, r16 * 128:(r16 + 1) * 128], mm * 128, 128, r16, 16 + r16, 0))
                brs.append((0, 1, us, None, None))
                us = []
                for s4 in range(4):
                    kown = 4 + 4 * g + s4
                    kprev = g if s4 == 0 else kown - 1
                    us.append((k2, q2v[:, :, g, s4, :], s4 * 128, 128, kprev, kown, 32))
                brs.append((2, 3, us, "p (mm s4 ii) -> p mm s4 ii", "p (s4 mm ii) -> p mm s4 ii"))
                us = []
                for n in range(16):
                    us.append((k1, q1v[:, :, g, n, :], n * 32, 32, n, 1 + n, 52))
                brs.append((4 + 2 * g, 5 + 2 * g, us, "p (mm n c) -> p mm n c", "p (n mm c) -> p mm n c"))
                for bi, (tprev, town, us, vS, vP) in enumerate(brs):
                    par = cnt[0] % 2
                    cnt[0] += 1
                    sbk = (2 * par, 2 * par + 1)
                    acc, den = 4 + 2 * par, 5 + 2 * par
                    for which in range(2):
                        bank = sbk[which]
                        tab = tprev if which == 0 else town
                        S.op("pe", lambda e, bank=bank, tab=tab: e.matmul(PS[bank][:], lhsT=identb[:], rhs=btb[:, tab, :], start=True, stop=False), r=[RBTh, RC], w=[RPS[bank]], inc=False)
                        for ui, (ksrc, qap, c0, N, kprev, kown, voff) in enumerate(us):
                            lastu = (ui == len(us) - 1)
                            kb = kprev if which == 0 else kown
                            rk = RK if ksrc is k3 else RK2
                            S.op("pe", lambda e, bank=bank, ksrc=ksrc, kb=kb, qap=qap, c0=c0, N=N, lastu=lastu: e.matmul(PS[bank][:, c0:c0 + N], lhsT=ksrc[:, kb * 128:(kb + 1) * 128], rhs=qap, start=False, stop=lastu), r=[rk, RQ], w=[RPS[bank]], inc=lastu)
                        S.op("act", lambda e, bank=bank: e.activation(out=PT[bank][:], in_=PS[bank][:], func=AF.Exp), r=[RPS[bank]], w=[RPT[bank]])
                    for ui, (ksrc, qap, c0, N, kprev, kown, voff) in enumerate(us):
                        lastu = (ui == len(us) - 1)
                        for which in range(2):
                            bank = sbk[which]
                            vidx = voff + (kprev if which == 0 else kown)
                            S.op("pe", lambda e, acc=acc, bank=bank, vidx=vidx, c0=c0, N=N, which=which: e.matmul(PS[acc][:, c0:c0 + N], lhsT=Vb[:, vidx, :], rhs=PT[bank][:, c0:c0 + N], start=(which == 0), stop=(which == 1)), r=[RVB, RPT[bank]], w=[RPS[acc]], inc=False)
                    for which in range(2):
                        bank = sbk[which]
                        S.op("pe", lambda e, den=den, bank=bank, which=which: e.matmul(PS[den][:], lhsT=onesb[:], rhs=PT[bank][:], start=(which == 0), stop=(which == 1)), r=[RC, RPT[bank]], w=[RPS[den]], inc=(which == 1))
                    if bi == 0:
                        S.op("dve", lambda e, acc=acc: e.tensor_copy(out=numS[:], in_=PS[acc][:]), r=[RPS[acc]], w=[RNUM])
                        S.op("dve", lambda e, den=den: e.tensor_copy(out=denS[:], in_=PS[den][:]), r=[RPS[den]], w=[RNUM])
                    else:
                        S.op("dve", lambda e, acc=acc, vS=vS, vP=vP: e.tensor_tensor(out=numS[:].rearrange(vS, mm=4, **({"s4": 4} if "s4" in vS else {"n": 16})), in0=numS[:].rearrange(vS, mm=4, **({"s4": 4} if "s4" in vS else {"n": 16})), in1=PS[acc][:].rearrange(vP, mm=4, **({"s4": 4} if "s4" in vP else {"n": 16})), op=ALU.add), r=[RPS[acc]], w=[RNUM])
                        S.op("dve", lambda e, den=den, vS=vS, vP=vP: e.tensor_tensor(out=denS[:].rearrange(vS, mm=4, **({"s4": 4} if "s4" in vS else {"n": 16})), in0=denS[:].rearrange(vS, mm=4, **({"s4": 4} if "s4" in vS else {"n": 16})), in1=PS[den][:].rearrange(vP, mm=4, **({"s4": 4} if "s4" in vP else {"n": 16})), op=ALU.add), r=[RPS[den]], w=[RNUM])
                S.op("dve", lambda e: e.reciprocal(out=denS[:], in_=denS[:]), r=[RNUM], w=[RNUM])
                outv = G.YA[:, h, :].rearrange("p (mm r4 i) -> p mm r4 i", mm=4, r4=4)[:, :, g, :]
                S.op("dve", lambda e, outv=outv: e.tensor_tensor(out=outv, in0=numS[:].rearrange("p (mm i) -> p mm i", mm=4), in1=denS[:].rearrange("p (mm i) -> p mm i", mm=4), op=M), r=[RNUM], w=[G.RYA])
        S.barrier()


def phase_out(G, stY):
    nc, S, sb, Dr, PS, RPS, RC = G.nc, G.S, G.sb, G.Dr, G.PS, G.RPS, G.RC
    onesb, epsb, modT, vecs = G.onesb, G.epsb, G.modT, G.vecs
    M, AD = ALU.mult, ALU.add
    gate1 = modT[:, 32:48]
    shift2 = modT[:, 48:64]
    gate2 = modT[:, 80:96]
    s2p = vecs[:, 48:64]
    Dr_h2 = G.h2_s
    RX1S, RH2S = Res(), Res()
    with contextlib.ExitStack() as st:
        wout = sb(st, "wout", [128, 16, DM], BF16)
        RWOB = [Res() for _ in range(8)]
        wov = Dr["w_out"].rearrange("(k p) c -> p k c", p=128)
        for cb in range(8):
            S.dma("pool", wout[:, :, cb * 256:(cb + 1) * 256], wov[:, :, cb * 256:(cb + 1) * 256], w=[RWOB[cb]])
        xt = [sb(st, "xo%d" % i, [128, 16, 256]) for i in range(2)]
        sq = sb(st, "sqo", [128, 16, 256], BF16)
        h2 = [sb(st, "h2o%d" % i, [128, 16, 256], BF16) for i in range(2)]
        rs = [sb(st, "rso%d" % i, [128, 256]) for i in range(2)]
        RXT, RH2, RRS, RSQ = [Res(), Res()], [Res(), Res()], [Res(), Res()], Res()
        xTv = Dr["xT"].rearrange("(k p) c -> p k c", p=128)
        nb = 0
        def load_xo(t_):
            S.dma("sp", xt[t_ % 2][:], xTv[:, :, 3 * TOK + t_ * 256: 3 * TOK + t_ * 256 + 256], w=[RXT[t_ % 2]])

        load_xo(0)
        for tb in range(8):
            b = tb % 2
            c0 = tb * 256
            if tb + 1 < 8:
                load_xo(tb + 1)
            for dp in range(8):
                bank = nb % 6
                nb += 1
                for half in range(2):
                    dt_ = 2 * dp + half
                    for k in range(16):
                        src = G.YS if k < 8 else G.YA
                        rr = G.RYS if k < 8 else G.RYA
                        S.op("pe", lambda e, k=k, dt_=dt_, half=half, bank=bank, src=src: e.matmul(PS[bank][:, half * 256:(half + 1) * 256], lhsT=wout[:, k, dt_ * 128:(dt_ + 1) * 128], rhs=src[:, k % 8, c0:c0 + 256], start=(k == 0), stop=(k == 15)),
                             r=[RWOB[dp], rr], w=[RPS[bank]], inc=(k == 15))
                for half in range(2):
                    dt_ = 2 * dp + half
                    S.op("dve", lambda e, dt_=dt_, half=half, bank=bank: e.scalar_tensor_tensor(out=xt[b][:, dt_, :], in0=PS[bank][:, half * 256:(half + 1) * 256], scalar=gate1[:, dt_:dt_ + 1], in1=xt[b][:, dt_, :], op0=M, op1=AD),
                         r=[RPS[bank], RXT[b], G.RMOD], w=[RXT[b]])
            S.dma("sp", Dr["x1_s"][:, :, c0:c0 + 256].rearrange("k p c -> p k c"), xt[b][:], r=[RXT[b]], wp=[RX1S])
            S.op("act", lambda e: e.activation(out=sq[:], in_=xt[b][:], func=AF.Square), r=[RXT[b]], w=[RSQ])
            for k in range(16):
                S.op("pe", lambda e, k=k: e.matmul(PS[6][:, 0:256], lhsT=onesb[:], rhs=sq[:, k, :], start=(k == 0), stop=(k == 15)), r=[RSQ, RC], w=[RPS[6]], inc=(k == 15))
            S.op("act", lambda e: e.activation(out=rs[b][:], in_=PS[6][:, 0:256], func=AF.Sqrt, bias=epsb[:, 0:1], scale=1.0 / DM), r=[RPS[6], RC], w=[RRS[b]])
            S.op("dve", lambda e: e.reciprocal(out=rs[b][:], in_=rs[b][:]), r=[RRS[b]], w=[RRS[b]])
            S.op("dve", lambda e: e.tensor_tensor(out=xt[b][:], in0=xt[b][:], in1=rs[b][:].unsqueeze(1).broadcast_to([128, 16, 256]), op=M), r=[RXT[b], RRS[b]], w=[RXT[b]])
            for k in range(16):
                if k % 2 == 0:
                    S.op("act", lambda e, k=k: e.activation(out=h2[b][:, k, :], in_=xt[b][:, k, :], func=AF.Identity, bias=shift2[:, k:k + 1], scale=s2p[:, k:k + 1]), r=[RXT[b], G.RMOD], wp=[RH2[b]])
                else:
                    S.op("dve", lambda e, k=k: e.tensor_scalar(out=h2[b][:, k, :], in0=xt[b][:, k, :], scalar1=s2p[:, k:k + 1], scalar2=shift2[:, k:k + 1], op0=M, op1=AD), r=[RXT[b], G.RMOD], wp=[RH2[b]])
            S.dma("sp", Dr_h2[:, :, c0:c0 + 256].rearrange("k p c -> p k c"), h2[b][:], r=[RH2[b]], wp=[RH2S])
        S.barrier()
    stY.__exit__(None, None, None)
    with contextlib.ExitStack() as st:
        h2T = sb(st, "h2T", [128, 16, 1024], BF16)
        act = sb(st, "ffact", [128, NHT, 1024], BF16)
        wgb = [sb(st, "wgb%d" % i, [128, 16, 256], BF16) for i in range(2)]
        wub = [sb(st, "wub%d" % i, [128, 16, 256], BF16) for i in range(2)]
        wdb = [sb(st, "wdb%d" % i, [128, NHT, 128], BF16) for i in range(2)]
        sgt = [sb(st, "sgt%d" % i, [128, 512]) for i in range(2)]
        x1t = [sb(st, "x1t%d" % i, [128, 512]) for i in range(2)]
        ost = [sb(st, "ost%d" % i, [128, 512]) for i in range(2)]
        RH, RACT = Res(), Res()
        RWG, RWU, RWD = [Res(), Res()], [Res(), Res()], [Res(), Res()]
        RSG, RX1, ROS = [Res(), Res()], [Res(), Res()], [Res(), Res()]
        ROUT = Res()
        nbk = 0
        nsg = 0
        nwd = 0
        nwg = 0
        for tile in range(2):
            t0 = tile * 1024
            S.dma("sp", h2T[:], Dr_h2[:, :, t0:t0 + 1024].rearrange("k p c -> p k c"), r=[RH2S], w=[RH])
            for hb in range(22):
                wb_ = nwg % 2
                nwg += 1
                S.dma("pool", wgb[wb_][:].rearrange("p k c -> p (k c)"), Dr["wg_t"][hb, :, :], w=[RWG[wb_]])
                S.dma("pool", wub[wb_][:].rearrange("p k c -> p (k c)"), Dr["wu_t"][hb, :, :], w=[RWU[wb_]])
                for ht2 in range(2):
                    ht = 2 * hb + ht2
                    for half in range(2):
                        bg = (nbk % 4) * 2
                        bu = bg + 1
                        nbk += 1
                        for k in range(16):
                            S.op("pe", lambda e, k=k, ht2=ht2, half=half, bg=bg: e.matmul(PS[bg][:], lhsT=wgb[wb_][:, k, ht2 * 128:(ht2 + 1) * 128], rhs=h2T[:, k, half * 512:(half + 1) * 512], start=(k == 0), stop=(k == 15)), r=[RWG[wb_], RH], w=[RPS[bg]], inc=(k == 15))
                        for k in range(16):
                            S.op("pe", lambda e, k=k, ht2=ht2, half=half, bu=bu: e.matmul(PS[bu][:], lhsT=wub[wb_][:, k, ht2 * 128:(ht2 + 1) * 128], rhs=h2T[:, k, half * 512:(half + 1) * 512], start=(k == 0), stop=(k == 15)), r=[RWU[wb_], RH], w=[RPS[bu]], inc=(k == 15))
                        si = nsg % 2
                        nsg += 1
                        S.op("act", lambda e, bg=bg, si=si: e.activation(out=sgt[si][:], in_=PS[bg][:], func=AF.Silu), r=[RPS[bg]], w=[RSG[si]])
                        S.op("dve", lambda e, bu=bu, si=si, ht=ht, half=half: e.tensor_tensor(out=act[:, ht, half * 512:(half + 1) * 512], in0=sgt[si][:], in1=PS[bu][:], op=M), r=[RSG[si], RPS[bu]], w=[RACT])
            for dt_ in range(16):
                wd_ = nwd % 2
                nwd += 1
                S.dma("pool", wdb[wd_][:].rearrange("p k c -> p (k c)"), Dr["wd_t"][dt_, :, :], w=[RWD[wd_]])
                for half in range(2):
                    bank = (nbk % 4) * 2
                    nbk += 1
                    si = nsg % 2
                    nsg += 1
                    c0 = t0 + half * 512
                    S.dma("sp", x1t[si][:], Dr["x1_s"][dt_, :, c0:c0 + 512], r=[RX1S], w=[RX1[si]])
                    for k in range(NHT):
                        S.op("pe", lambda e, k=k, half=half, bank=bank: e.matmul(PS[bank][:], lhsT=wdb[wd_][:, k, :], rhs=act[:, k, half * 512:(half + 1) * 512], start=(k == 0), stop=(k == NHT - 1)), r=[RWD[wd_], RACT], w=[RPS[bank]], inc=(k == NHT - 1))
                    S.op("dve", lambda e, bank=bank, si=si, dt_=dt_: e.scalar_tensor_tensor(out=ost[si][:], in0=PS[bank][:], scalar=gate2[:, dt_:dt_ + 1], in1=x1t[si][:], op0=M, op1=AD), r=[RPS[bank], RX1[si], G.RMOD], w=[ROS[si]])
                    S.dma("sp", Dr["outT"][dt_ * 128:(dt_ + 1) * 128, c0:c0 + 512], ost[si][:], r=[ROS[si]], wp=[ROUT])
        S.barrier()


def _perm_slot(xs):
    d = xs.shape[1]
    return xs.reshape(128, 16, d).transpose(2, 1, 0).reshape(d, 2048)


def _unperm_slot(yT):
    d = yT.shape[0]
    return yT.reshape(d, 16, 128).transpose(2, 1, 0).reshape(2048, d)


def _bias_tables(halo_valid):
    slopes = np.exp2(-8.0 * np.arange(1, 9, dtype=np.float32) / 8).astype(np.float32)
    kk = np.arange(128)[:, None]
    cc = np.arange(512)[None, :]
    out = np.zeros((8, 128, 12, 512), np.float32)

    def fill(t, mk, mq, d, is_prev, halo_cols):
        if is_prev:
            dist = mq + 128 - mk
            valid = dist <= 128
        else:
            dist = mq - mk
            valid = dist >= 0
        if is_prev and not halo_valid:
            valid = valid & (~halo_cols)
        for h in range(8):
            out[h, :, t, :] = np.where(valid, -slopes[h] * d * dist, NEG)

    allc = np.ones((1, 512), bool)
    fill(0, kk, cc % 128, 16, True, allc)
    fill(1, kk, cc % 128, 16, False, allc)
    mk2 = 4 * (kk % 32) + kk // 32
    mq2 = 4 * (cc % 32) + (cc % 128) // 32
    fill(2, mk2, mq2, 4, True, (cc // 128) == 0)
    fill(3, mk2, mq2, 4, False, allc)
    tk1 = 16 * (kk % 8) + kk // 8
    for r4 in range(4):
        tq1 = 16 * (cc % 8) + 4 * ((cc % 32) // 8) + r4
        fill(4 + 2 * r4, tk1, tq1, 1, True, (cc // 32) == 0)
        fill(5 + 2 * r4, tk1, tq1, 1, False, allc)
    return out.reshape(8, 128, 12 * 512)


def _prep_shared(inp):
    f = np.float32
    sh = {}
    sh["w_ada"] = np.ascontiguousarray(inp["w_ada"][0], f)
    sh["badaT"] = np.ascontiguousarray(inp["b_ada"][0].reshape(96, 128).T, f)
    sh["n1w"] = np.ascontiguousarray(inp["norm1_w"][0].reshape(16, 128).T, f)
    sh["n2w"] = np.ascontiguousarray(inp["norm2_w"][0].reshape(16, 128).T, f)
    sh["w_in"] = np.ascontiguousarray(inp["w_in"][0], f)

    def gn(a):
        return np.ascontiguousarray(a.reshape(32, 128).T, f)
    sh["a_re"] = gn(inp["ssm_a_re"][0])
    sh["a_im"] = gn(inp["ssm_a_im"][0])
    sh["ldt"] = gn(np.repeat(inp["ssm_log_dt"][0][:, None], 64, axis=1))

    def gnc(a):
        return np.ascontiguousarray(a.reshape(32, 2, 64, 16).transpose(1, 2, 0, 3).reshape(128, 512), f)
    sh["b_re"] = gnc(inp["ssm_b_re"][0])
    sh["b_im"] = gnc(inp["ssm_b_im"][0])
    sh["c_reT"] = gnc(inp["ssm_c_re"][0].transpose(0, 2, 1))
    sh["c_imT"] = gnc(inp["ssm_c_im"][0].transpose(0, 2, 1))
    sh["ssm_d"] = np.ascontiguousarray(inp["ssm_d"][0].reshape(8, 128).T, f)
    sh["w_glu"] = np.ascontiguousarray(inp["ssm_w_glu"][0], f)
    sh["qnw"] = np.ascontiguousarray(inp["q_norm_w"][0].reshape(128, 1), f)
    sh["knw"] = np.ascontiguousarray(inp["k_norm_w"][0].reshape(128, 1), f)
    sh["w_out"] = np.ascontiguousarray(inp["w_out"][0], f)
    wg = inp["w_ffn_gate"][0].reshape(16, 128, 22, 256).transpose(2, 1, 0, 3)
    wu = inp["w_ffn_up"][0].reshape(16, 128, 22, 256).transpose(2, 1, 0, 3)
    sh["wg_t"] = np.ascontiguousarray(wg, f).reshape(22, 128, 16 * 256)
    sh["wu_t"] = np.ascontiguousarray(wu, f).reshape(22, 128, 16 * 256)
    wd = inp["w_ffn_down"][0].reshape(NHT, 128, 16, 128).transpose(2, 1, 0, 3)
    sh["wd_t"] = np.ascontiguousarray(wd, f).reshape(16, 128, NHT * 128)
    sh["ident"] = np.eye(128, dtype=f)
    g = np.arange(128) // 16
    sh["bdmask"] = (g[:, None] == g[None, :]).astype(f)
    return sh


def _prep_core(inp, core):
    b, j = core // 4, core % 4
    x = np.asarray(inp["x"], np.float32)
    m = {}
    xT = np.zeros((DM, 4 * TOK), np.float32)
    sm = np.zeros((128, 4), np.float32)
    for s in range(4):
        jj = j - 3 + s
        if jj >= 0:
            xT[:, s * TOK:(s + 1) * TOK] = _perm_slot(x[b, jj * TOK:(jj + 1) * TOK])
            sm[:, s] = 1.0
    m["xT"] = xT
    m["smask"] = sm
    m["cT"] = np.ascontiguousarray(np.asarray(inp["c"], np.float32)[b].reshape(16, 128).T)
    m["btab"] = _bias_tables(j > 0)
    return m


def kernel(**inputs):
    inp = {k: np.asarray(v) for k, v in inputs.items()}
    nc = build_program(STAGE)
    sh = _prep_shared(inp)
    in_maps = []
    ncores = DEBUG.get("ncores", NCORES)
    for core in DEBUG.get("corelist", range(ncores)):
        m = dict(sh)
        m.update(_prep_core(inp, core))
        in_maps.append(m)
    res = run_bass_kernel_spmd(nc, in_maps, core_ids=list(range(ncores)))
    if STAGE < 99:
        DEBUG["res"] = res.results
        return None
    out = np.zeros((2, SEQ, DM), np.float32)
    for core in range(NCORES):
        b, j = core // 4, core % 4
        out[b, j * TOK:(j + 1) * TOK] = _unperm_slot(res.results[core]["outT"])
    return out
```

```python
import contextlib
import numpy as np
import ml_dtypes
import concourse.bass as bass
import concourse.mybir as mybir
from concourse.bass_utils import run_bass_kernel_spmd

F32 = mybir.dt.float32
BF16 = mybir.dt.bfloat16
ALU = mybir.AluOpType
AF = mybir.ActivationFunctionType
AX = mybir.AxisListType

NCORES = 8
DM = 2048
SEQ = 8192
TOK = 2048
FFH = 5632
NHT = FFH // 128
EPS = 1e-6
NEG = -1.0e30
STAGE = 99
DEBUG = {}


class Res:
    __slots__ = ("w", "rs")

    def __init__(self):
        self.w = {}
        self.rs = []


class Sched:
    ENG = ("pe", "act", "dve", "pool", "sp")
    NDS = 6

    def __init__(self, nc, es):
        self.nc = nc
        self.e = dict(pe=nc.tensor, act=nc.scalar, dve=nc.vector, pool=nc.gpsimd, sp=nc.sync)
        self.sem = {k: es.enter_context(nc.semaphore("s_" + k)) for k in self.ENG}
        self.cnt = {k: 0 for k in self.ENG}
        self.seen = {k: {} for k in self.ENG}
        self.dsem = {}
        self.dval = {}
        self.dnext = {}
        for q in ("sp", "act", "pool"):
            for i in range(self.NDS):
                self.dsem[(q, i)] = es.enter_context(nc.semaphore("d_%s%d" % (q, i)))
                self.dval[(q, i)] = 0
            self.dnext[q] = 0
        self.ninst = 0

    def _wait(self, eng, tok):
        if tok is None:
            return
        kind, key, val = tok
        if kind == "c":
            if eng == "pe" and key == "pe":
                return
            sem = self.sem[key]
        else:
            sem = self.dsem[key]
        if self.seen[eng].get((kind, key), 0) >= val:
            return
        self.e[eng].wait_ge(sem, val)
        self.seen[eng][(kind, key)] = val

    def _deps(self, eng, r, w, wp=()):
        for x in r:
            for t in list(x.w.values()):
                self._wait(eng, t)
        for x in w:
            for t in list(x.w.values()):
                self._wait(eng, t)
            for t in x.rs:
                self._wait(eng, t)
        for x in wp:
            for t in x.rs:
                self._wait(eng, t)

    def _commit(self, tok, r, w, wp=()):
        for x in wp:
            x.w[(tok[0], tok[1])] = tok
        for x in r:
            x.rs.append(tok)
            if len(x.rs) > 24:
                best = {}
                for t in x.rs:
                    k = (t[0], t[1])
                    if k not in best or best[k][2] < t[2]:
                        best[k] = t
                x.rs = list(best.values())
        for x in w:
            x.w = {(tok[0], tok[1]): tok}
            x.rs = []

    def op(self, eng, fn, r=(), w=(), inc=True, wp=()):
        self._deps(eng, r, w, wp)
        inst = fn(self.e[eng])
        self.ninst += 1
        if inc:
            self.cnt[eng] += 1
            inst.then_inc(self.sem[eng], 1)
            tok = ("c", eng, self.cnt[eng])
        else:
            tok = ("c", eng, self.cnt[eng] + 1)
        self._commit(tok, r, w, wp)
        return tok

    def dma(self, q, out, in_, r=(), w=(), wp=()):
        self._deps(q, r, w, wp)
        i = self.dnext[q]
        self.dnext[q] = (i + 1) % self.NDS
        key = (q, i)
        if self.dval[key] > 0:
            self._wait(q, ("d", key, self.dval[key]))
        self.dval[key] += 16
        self.e[q].dma_start(out=out, in_=in_).then_inc(self.dsem[key], 16)
        self.ninst += 1
        tok = ("d", key, self.dval[key])
        self._commit(tok, r, w, wp)
        return tok

    def barrier(self):
        toks = [("c", k, self.cnt[k]) for k in self.ENG if self.cnt[k] > 0]
        toks += [("d", k, v) for k, v in self.dval.items() if v > 0]
        for eng in self.ENG:
            for t in toks:
                if t[0] == "c" and t[1] == eng:
                    continue
                if eng == "pe" and t[0] == "c" and t[1] == "pe":
                    continue
                self._wait(eng, t)


def build_program(stage=99):
    nc = bass.Bass("TRN2", target_bir_lowering=False)
    Dr = {}

    def din(name, shape, dt=F32):
        Dr[name] = nc.dram_tensor(name, list(shape), dt, kind="ExternalInput").ap()

    def dscr(name, shape, dt):
        Dr[name] = nc.dram_tensor(name, list(shape), dt, kind="Internal").ap()

    def dout(name, shape, dt=F32):
        Dr[name] = nc.dram_tensor(name, list(shape), dt, kind="ExternalOutput").ap()

    din("xT", [DM, 4 * TOK])
    din("cT", [128, 16])
    din("smask", [128, 4])
    din("w_ada", [DM, 6 * DM])
    din("badaT", [128, 96])
    din("n1w", [128, 16])
    din("n2w", [128, 16])
    din("w_in", [DM, 4096])
    din("a_re", [128, 32])
    din("a_im", [128, 32])
    din("ldt", [128, 32])
    din("b_re", [128, 512])
    din("b_im", [128, 512])
    din("c_reT", [128, 512])
    din("c_imT", [128, 512])
    din("ssm_d", [128, 8])
    din("w_glu", [1024, 1024])
    din("qnw", [128, 1])
    din("knw", [128, 1])
    din("w_out", [DM, DM])
    din("wg_t", [22, 128, 16 * 256])
    din("wu_t", [22, 128, 16 * 256])
    din("wd_t", [16, 128, NHT * 128])
    din("btab", [8, 128, 12 * 512])
    din("ident", [128, 128])
    din("bdmask", [128, 128])
    dout("outT", [DM, TOK])
    dscr("uT_s", [8, 128, 4 * TOK], BF16)
    dscr("qT_s", [8, 128, TOK], BF16)
    dscr("kT_s", [8, 128, 2 * TOK], BF16)
    dscr("vT_s", [8, 128, 2 * TOK], BF16)
    dscr("x1_s", [16, 128, TOK], F32)
    dscr("h2_s", [16, 128, TOK], BF16)
    if stage < 99:
        dout("dbg", [128, 4096])

    es = contextlib.ExitStack()
    with es:
        S = Sched(nc, es)

        def sb(st, name, shape, dt=F32):
            return st.enter_context(nc.sbuf_tensor(name, list(shape), dt))

        PS = [es.enter_context(nc.psum_tensor("ps%d" % i, [128, 512], F32)) for i in range(8)]
        RPS = [Res() for _ in range(8)]

        identf = sb(es, "identf", [128, 128])
        identb = sb(es, "identb", [128, 128], BF16)
        onesb = sb(es, "onesb", [128, 128], BF16)
        bdm = sb(es, "bdm", [128, 128])
        modT = sb(es, "modT", [128, 96])
        vecs = sb(es, "vecs", [128, 16 * 8])
        smk = sb(es, "smk", [128, 4])
        qkw = sb(es, "qkw", [128, 2])
        epsb = sb(es, "epsb", [128, 1])
        RC = Res()
        S.dma("sp", identf[:], Dr["ident"][:, :], wp=[RC])
        S.dma("sp", bdm[:], Dr["bdmask"][:, :], wp=[RC])
        S.dma("sp", vecs[:, 0:16], Dr["n1w"][:, :], wp=[RC])
        S.dma("sp", vecs[:, 16:32], Dr["n2w"][:, :], wp=[RC])
        S.dma("sp", smk[:], Dr["smask"][:, :], wp=[RC])
        S.dma("sp", qkw[:, 0:1], Dr["qnw"][:, :], wp=[RC])
        S.dma("sp", qkw[:, 1:2], Dr["knw"][:, :], wp=[RC])
        S.op("dve", lambda e: e.tensor_copy(out=identb[:], in_=identf[:]), r=[RC], w=[RC])
        S.op("dve", lambda e: e.memset(onesb[:], 1.0), w=[RC])
        S.op("dve", lambda e: e.memset(epsb[:], EPS), w=[RC])
        S.op("dve", lambda e: e.tensor_scalar(out=qkw[:, 0:1], in0=qkw[:, 0:1], scalar1=float(128.0 ** -0.5), scalar2=None, op0=ALU.mult), r=[RC], w=[RC])
        n1w = vecs[:, 0:16]
        n2w = vecs[:, 16:32]
        s1p = vecs[:, 32:48]
        s2p = vecs[:, 48:64]
        shift1 = modT[:, 0:16]
        gate1 = modT[:, 32:48]
        shift2 = modT[:, 48:64]
        gate2 = modT[:, 80:96]
        RMOD = Res()

        st0 = contextlib.ExitStack()
        st0.__enter__()
        cTs = sb(st0, "cTs", [128, 16])
        condb = sb(st0, "condb", [128, 16], BF16)
        badaT = sb(st0, "badaTs", [128, 96])
        wa = [sb(st0, "wa%d" % i, [128, 16, 128], BF16) for i in range(2)]
        RWA = [Res(), Res()]
        S.dma("sp", cTs[:], Dr["cT"][:, :], wp=[RC])
        S.dma("sp", badaT[:], Dr["badaT"][:, :], wp=[RC])
        S.op("act", lambda e: e.activation(out=condb[:], in_=cTs[:], func=AF.Silu), r=[RC], w=[RC])
        wada_v = Dr["w_ada"].rearrange("(k p) c -> p k c", p=128)

        def p0_block(cb):
            buf = cb % 2
            S.dma("pool", wa[buf][:], wada_v[:, :, cb * 128:(cb + 1) * 128], w=[RWA[buf]])
            for ct in range(1):
                col = cb
                for k in range(16):
                    S.op("pe", lambda e, k=k, ct=ct, col=col: e.matmul(PS[7][:, col:col + 1], lhsT=wa[buf][:, k, ct * 128:(ct + 1) * 128], rhs=condb[:, k:k + 1], start=(k == 0), stop=(k == 15)),
                         r=[RWA[buf], RC], w=[RPS[7]], inc=(k == 15))

        for cb in range(32):
            p0_block(cb)
        S.op("dve", lambda e: e.tensor_tensor(out=modT[:, 0:32], in0=PS[7][:, 0:32], in1=badaT[:, 0:32], op=ALU.add), r=[RPS[7], RC], w=[RMOD])
        S.op("dve", lambda e: e.scalar_tensor_tensor(out=s1p, in0=modT[:, 16:32], scalar=1.0, in1=n1w, op0=ALU.add, op1=ALU.mult), r=[RMOD, RC], w=[RMOD])

        st1 = contextlib.ExitStack()
        st1.__enter__()
        win = sb(st1, "win", [128, 16, 4096], BF16)
        RWIN = Res()
        winv = Dr["w_in"].rearrange("(k p) c -> p k c", p=128)
        RWIN2 = Res()
        for k in range(16):
            S.dma("pool", win[:, k, 0:1024], winv[:, k, 0:1024], wp=[RWIN])
        for k in range(16):
            S.dma("pool", win[:, k, 1024:4096], winv[:, k, 1024:4096], wp=[RWIN2])
        xt = [sb(st1, "xt%d" % i, [128, 16, 256]) for i in range(2)]
        sq = [sb(st1, "sq0", [128, 16, 256], BF16)] * 2
        hT = [sb(st1, "hT%d" % i, [128, 16, 256], BF16) for i in range(2)]
        rstd = [sb(st1, "rstd%d" % i, [128, 256]) for i in range(2)]
        stg = [sb(st1, "stg%d" % i, [128, 2, 256], BF16) for i in range(4)]
        sqq = [sb(st1, "sqq%d" % i, [128, 512], BF16) for i in range(2)]
        rq = [sb(st1, "rq%d" % i, [128, 512]) for i in range(2)]
        RXT = [Res(), Res()]
        RSQ = [Res()] * 2
        RHT = [Res(), Res()]
        RRS = [Res(), Res()]
        RSTG = [Res() for _ in range(4)]
        RSQQ = [Res(), Res()]
        RRQ = [Res(), Res()]
        RSCR = Res()
        xTv = Dr["xT"].rearrange("(k p) c -> p k c", p=128)
        nstg = [0]
        nqk = [0]
        nbank = [0]

        def norm_mod(xtile, Rx, sqb, Rsq, rs, Rrs, hout, Rh, ncol, sp_, sh_, psb, Rpsb):
            S.op("act", lambda e: e.activation(out=sqb, in_=xtile, func=AF.Square), r=[Rx], w=[Rsq])
            for k in range(16):
                S.op("pe", lambda e, k=k: e.matmul(psb, lhsT=onesb[:], rhs=sqb[:, k, :], start=(k == 0), stop=(k == 15)), r=[Rsq, RC], w=[Rpsb], inc=(k == 15))
            S.op("act", lambda e: e.activation(out=rs, in_=psb, func=AF.Sqrt, bias=epsb[:, 0:1], scale=1.0 / DM), r=[Rpsb, RC], w=[Rrs])
            S.op("dve", lambda e: e.reciprocal(out=rs, in_=rs), r=[Rrs], w=[Rrs])
            S.op("dve", lambda e: e.tensor_tensor(out=xtile, in0=xtile, in1=rs.unsqueeze(1).broadcast_to([128, 16, ncol]), op=ALU.mult), r=[Rx, Rrs], w=[Rx])
            for k in range(16):
                if k % 2 == 0:
                    S.op("act", lambda e, k=k: e.activation(out=hout[:, k, :], in_=xtile[:, k, :], func=AF.Identity, bias=sh_[:, k:k + 1], scale=sp_[:, k:k + 1]), r=[Rx, RMOD], wp=[Rh])
                else:
                    S.op("dve", lambda e, k=k: e.tensor_scalar(out=hout[:, k, :], in0=xtile[:, k, :], scalar1=sp_[:, k:k + 1], scalar2=sh_[:, k:k + 1], op0=ALU.mult, op1=ALU.add), r=[Rx, RMOD], wp=[Rh])

        def load_x(t_):
            S.dma("sp", xt[t_ % 2][:], xTv[:, :, (t_ // 8) * TOK + (t_ % 8) * 256: (t_ // 8) * TOK + (t_ % 8) * 256 + 256], w=[RXT[t_ % 2]])

        def norm_a(t_):
            bb = t_ % 2
            S.op("act", lambda e: e.activation(out=sq[bb][:], in_=xt[bb][:], func=AF.Square), r=[RXT[bb]], w=[RSQ[bb]])

        def norm_b(t_):
            bb = t_ % 2
            xtile, rs, hout = xt[bb][:], rstd[bb][:], hT[bb]
            psb = PS[6][:, 0:256]
            for k in range(16):
                S.op("pe", lambda e, k=k: e.matmul(psb, lhsT=onesb[:], rhs=sq[bb][:, k, :], start=(k == 0), stop=(k == 15)), r=[RSQ[bb], RC], w=[RPS[6]], inc=(k == 15))
            S.op("act", lambda e: e.activation(out=rs, in_=psb, func=AF.Sqrt, bias=epsb[:, 0:1], scale=1.0 / DM), r=[RPS[6], RC], w=[RRS[bb]])
            S.op("dve", lambda e: e.reciprocal(out=rs, in_=rs), r=[RRS[bb]], w=[RRS[bb]])
            S.op("dve", lambda e: e.tensor_tensor(out=xtile, in0=xtile, in1=rs.unsqueeze(1).broadcast_to([128, 16, 256]), op=ALU.mult), r=[RXT[bb], RRS[bb]], w=[RXT[bb]])
            for k in range(16):
                if k % 2 == 0:
                    S.op("act", lambda e, k=k: e.activation(out=hout[:, k, :], in_=xtile[:, k, :], func=AF.Identity, bias=shift1[:, k:k + 1], scale=s1p[:, k:k + 1]), r=[RXT[bb], RMOD], wp=[RHT[bb]])
                else:
                    S.op("dve", lambda e, k=k: e.tensor_scalar(out=hout[:, k, :], in0=xtile[:, k, :], scalar1=s1p[:, k:k + 1], scalar2=shift1[:, k:k + 1], op0=ALU.mult, op1=ALU.add), r=[RXT[bb], RMOD], wp=[RHT[bb]])

        pending = []

        def flush():
            while pending:
                pending.pop(0)()

        load_x(0)
        norm_a(0)
        norm_b(0)
        load_x(1)
        for tt in range(32):
            slot = tt // 8
            loc = (tt % 8) * 256
            b = tt % 2
            if tt + 1 < 32:
                norm_a(tt + 1)
            if slot < 2:
                ots = list(range(0, 8))
            elif slot == 2:
                ots = list(range(0, 8)) + list(range(16, 32))
            else:
                ots = list(range(0, 32))
            for oi in range(0, len(ots), 2):
                ot = ots[oi]
                bank = nbank[0] % 5
                nbank[0] += 1
                for half in range(2):
                    o = ot + half
                    for k in range(16):
                        S.op("pe", lambda e, k=k, o=o, half=half, bank=bank: e.matmul(PS[bank][:, half * 256:(half + 1) * 256], lhsT=win[:, k, o * 128:(o + 1) * 128], rhs=hT[b][:, k, :], start=(k == 0), stop=(k == 15)),
                             r=[(RWIN if o < 8 else RWIN2), RHT[b]], w=[RPS[bank]], inc=(k == 15))
                flush()
                if oi == 0 and tt + 1 < 32:
                    norm_b(tt + 1)
                    if tt + 2 < 32:
                        load_x(tt + 2)
                si = nstg[0] % 4
                nstg[0] += 1
                sv = stg[si][:].rearrange("p a c -> p (a c)")
                kind = ot // 8
                if kind == 0:
                    S.op("act", lambda e, bank=bank, sv=sv: e.activation(out=sv, in_=PS[bank][:], func=AF.Copy, scale=smk[:, slot:slot + 1]), r=[RPS[bank], RC], w=[RSTG[si]])
                    dst = Dr["uT_s"][ot:ot + 2, :, slot * TOK + loc: slot * TOK + loc + 256]
                    S.dma("sp", dst.rearrange("t p c -> p t c"), stg[si][:], r=[RSTG[si]], wp=[RSCR])
                elif kind == 3:
                    S.op("dve", lambda e, bank=bank, sv=sv: e.tensor_copy(out=sv, in_=PS[bank][:]), r=[RPS[bank]], w=[RSTG[si]])
                    dst = Dr["vT_s"][ot - 24:ot - 22, :, (slot - 2) * TOK + loc:(slot - 2) * TOK + loc + 256]
                    S.dma("sp", dst.rearrange("t p c -> p t c"), stg[si][:], r=[RSTG[si]], wp=[RSCR])
                else:
                    qi = nqk[0] % 2
                    nqk[0] += 1
                    S.op("act", lambda e, bank=bank, qi=qi: e.activation(out=sqq[qi][:], in_=PS[bank][:], func=AF.Square), r=[RPS[bank]], w=[RSQQ[qi]])
                    wcol = qkw[:, 0:1] if kind == 1 else qkw[:, 1:2]
                    if kind == 1:
                        dst = Dr["qT_s"][ot - 8:ot - 6, :, loc:loc + 256]
                    else:
                        dst = Dr["kT_s"][ot - 16:ot - 14, :, (slot - 2) * TOK + loc:(slot - 2) * TOK + loc + 256]

                    def rest(bank=bank, qi=qi, sv=sv, si=si, wcol=wcol, dst=dst):
                        S.op("pe", lambda e: e.matmul(PS[5][:], lhsT=onesb[:], rhs=sqq[qi][:], start=True, stop=True), r=[RSQQ[qi], RC], w=[RPS[5]])
                        S.op("act", lambda e: e.activation(out=rq[qi][:], in_=PS[5][:], func=AF.Sqrt, bias=epsb[:, 0:1], scale=1.0 / 128.0), r=[RPS[5], RC], w=[RRQ[qi]])
                        S.op("dve", lambda e: e.reciprocal(out=rq[qi][:], in_=rq[qi][:]), r=[RRQ[qi]], w=[RRQ[qi]])
                        S.op("dve", lambda e: e.scalar_tensor_tensor(out=sv, in0=PS[bank][:], scalar=wcol, in1=rq[qi][:], op0=ALU.mult, op1=ALU.mult), r=[RPS[bank], RRQ[qi], RC], w=[RSTG[si]])
                        S.dma("sp", dst.rearrange("t p c -> p t c"), stg[si][:], r=[RSTG[si]], wp=[RSCR])
                    pending.append(rest)
            flush()
            p0_block(32 + 2 * tt)
            p0_block(33 + 2 * tt)
            if tt == 31:
                S.op("dve", lambda e: e.tensor_tensor(out=modT[:, 32:96], in0=PS[7][:, 32:96], in1=badaT[:, 32:96], op=ALU.add), r=[RPS[7], RC], w=[RMOD])
                S.op("dve", lambda e: e.scalar_tensor_tensor(out=s2p, in0=modT[:, 64:80], scalar=1.0, in1=n2w, op0=ALU.add, op1=ALU.mult), r=[RMOD, RC], w=[RMOD])
        S.barrier()
        st1.__exit__(None, None, None)
        st0.__exit__(None, None, None)

        if stage == 1:
            with contextlib.ExitStack() as sd:
                db = sb(sd, "db", [128, 4096], BF16)
                df = sb(sd, "df", [128, 4096])
                RD = Res()
                S.dma("sp", db[:, 0:1024], Dr["uT_s"][0, :, 3 * TOK:3 * TOK + 1024], r=[RSCR], w=[RD])
                S.dma("sp", db[:, 1024:2048], Dr["qT_s"][0, :, 0:1024], r=[RSCR], w=[RD])
                S.dma("sp", db[:, 2048:2560], Dr["kT_s"][0, :, 0:512], r=[RSCR], w=[RD])
                S.dma("sp", db[:, 2560:3072], Dr["kT_s"][0, :, TOK:TOK + 512], r=[RSCR], w=[RD])
                S.dma("sp", db[:, 3072:4096], Dr["vT_s"][0, :, TOK:TOK + 1024], r=[RSCR], w=[RD])
                S.op("dve", lambda e: e.tensor_copy(out=df[:], in_=db[:]), r=[RD], w=[RD])
                S.op("dve", lambda e: e.tensor_copy(out=df[:, 0:96], in_=modT[:]), r=[RD, RMOD], w=[RD])
                S.dma("sp", Dr["dbg"][:, :], df[:], r=[RD], w=[RD])
                S.barrier()
            return nc

        class G:
            pass
        G.nc, G.S, G.es, G.sb, G.Dr, G.PS, G.RPS, G.RC, G.RSCR = nc, S, es, sb, Dr, PS, RPS, RC, RSCR
        G.identf, G.identb, G.onesb, G.bdm, G.stage, G.RMOD = identf, identb, onesb, bdm, stage, RMOD
        G.modT, G.vecs, G.epsb = modT, vecs, epsb
        G.h2_s = Dr["h2_s"]
        stY = contextlib.ExitStack()
        stY.__enter__()
        G.YS = sb(stY, "YS", [128, 8, TOK], BF16)
        G.sub = DEBUG.get('sub', 0)
        G.RYS = Res()
        phase_ssm(G)
        if stage == 2:
            with contextlib.ExitStack() as sd:
                df = sb(sd, "df", [128, 4096])
                RD = Res()
                S.op("dve", lambda e: e.tensor_copy(out=df[:, 0:2048], in_=G.YS[:, 0, :]), r=[G.RYS], w=[RD])
                S.op("dve", lambda e: e.tensor_copy(out=df[:, 2048:4096], in_=G.YS[:, 7, :]), r=[G.RYS], w=[RD])
                S.dma("sp", Dr["dbg"][:, :], df[:], r=[RD], w=[RD])
                S.barrier()
            stY.__exit__(None, None, None)
            return nc
        G.YA = sb(stY, "YA", [128, 8, TOK], BF16)
        G.RYA = Res()
        phase_attn(G)
        if stage == 3:
            with contextlib.ExitStack() as sd:
                df = sb(sd, "df", [128, 4096])
                RD = Res()
                S.op("dve", lambda e: e.tensor_copy(out=df[:, 0:2048], in_=G.YA[:, 0, :]), r=[G.RYA], w=[RD])
                S.op("dve", lambda e: e.tensor_copy(out=df[:, 2048:4096], in_=G.YA[:, 7, :]), r=[G.RYA], w=[RD])
                S.dma("sp", Dr["dbg"][:, :], df[:], r=[RD], w=[RD])
                S.barrier()
            stY.__exit__(None, None, None)
            return nc
        phase_out(G, stY)
    return nc


def phase_ssm(G):
    nc, S, sb, Dr, PS, RPS, RC = G.nc, G.S, G.sb, G.Dr, G.PS, G.RPS, G.RC
    identf, identb, bdm = G.identf, G.identb, G.bdm
    stA = contextlib.ExitStack()
    stA.__enter__()
    YG = sb(stA, "YG", [128, 8, TOK], BF16)
    RYG = Res()
    stB = contextlib.ExitStack()
    stB.__enter__()
    NPL = 36
    pl = sb(stB, "ssm_pl", [128, NPL, 32])
    names = "are aim dt z p e1 t t2 qv wi r wr A B C ur ui ar1 ai rho1 rho vr vi den zr zi Dd".split()
    P = {n: pl[:, i, :] for i, n in enumerate(names)}
    Tm = [sb(stB, "ssm_T%d" % i, [128, 1024]) for i in range(4)]
    Bbr = sb(stB, "Bbr", [128, 32, 16])
    Bbi = sb(stB, "Bbi", [128, 32, 16])
    Cre = sb(stB, "CreT", [128, 32, 16])
    Cim = sb(stB, "CimT", [128, 32, 16])
    Pr = sb(stB, "Pr", [128, 17, 32])
    Pi = sb(stB, "Pi", [128, 17, 32])
    P2 = sb(stB, "P2", [128, 16, 2, 128], BF16)
    Q2 = sb(stB, "Q2", [128, 2, 128], BF16)
    BD = sb(stB, "BD", [128, 16, 128], BF16)
    WS = sb(stB, "WSpad", [128, 2, 16, 2, 128], BF16)
    WH = sb(stB, "WHpad", [128, 16, 2, 4, 128], BF16)
    uT = sb(stB, "uTt", [128, 4 * TOK], BF16)
    Er = sb(stB, "Er", [128, 4, 513])
    Ei = sb(stB, "Ei", [128, 4, 513])
    Gre = sb(stB, "Gre", [128, 513])
    Gim = sb(stB, "Gim", [128, 513])
    Hb = sb(stB, "Hb", [128, 4, 2, 128], BF16)
    pmask = sb(stB, "pmask", [128, 4])
    dsm = sb(stB, "dsm", [128, 8])
    RS = Res()
    R_T, R_P2, R_BD, R_WS, R_WH, R_E, R_G = Res(), Res(), Res(), Res(), Res(), Res(), Res()
    ctx = {"r": [RS], "w": [RS]}

    def tt(o, a, b, op):
        S.op("dve", lambda e: e.tensor_tensor(out=o, in0=a, in1=b, op=op), r=ctx["r"], w=ctx["w"])

    def ts(o, a, s1, s2, op0, op1=None):
        if op1 is None:
            S.op("dve", lambda e: e.tensor_scalar(out=o, in0=a, scalar1=s1, scalar2=None, op0=op0), r=ctx["r"], w=ctx["w"])
        else:
            S.op("dve", lambda e: e.tensor_scalar(out=o, in0=a, scalar1=s1, scalar2=s2, op0=op0, op1=op1), r=ctx["r"], w=ctx["w"])

    def stt(o, a, sc, b, op0, op1):
        S.op("dve", lambda e: e.scalar_tensor_tensor(out=o, in0=a, scalar=sc, in1=b, op0=op0, op1=op1), r=ctx["r"], w=ctx["w"])

    def cp(o, a):
        S.op("dve", lambda e: e.tensor_copy(out=o, in_=a), r=ctx["r"], w=ctx["w"])

    M, AD, SUB = ALU.mult, ALU.add, ALU.subtract
    Bre = Tm[2][:, 0:512].rearrange("p (a c) -> p a c", c=16)
    Bim = Tm[3][:, 0:512].rearrange("p (a c) -> p a c", c=16)
    S.dma("sp", P["are"], Dr["a_re"][:, :], wp=[RS])
    S.dma("sp", P["aim"], Dr["a_im"][:, :], wp=[RS])
    S.dma("sp", P["dt"], Dr["ldt"][:, :], wp=[RS])
    S.dma("sp", Tm[2][:, 0:512], Dr["b_re"][:, :], wp=[RS])
    S.dma("sp", Tm[3][:, 0:512], Dr["b_im"][:, :], wp=[RS])
    S.dma("sp", Cre[:].rearrange("p a c -> p (a c)"), Dr["c_reT"][:, :], wp=[RS])
    S.dma("sp", Cim[:].rearrange("p a c -> p (a c)"), Dr["c_imT"][:, :], wp=[RS])
    S.dma("sp", dsm[:], Dr["ssm_d"][:, :], wp=[RS])
    S.op("act", lambda e: e.activation(out=P["dt"], in_=P["dt"], func=AF.Exp), r=[RS], w=[RS])
    for pp in range(4):
        tt(pmask[:, pp:pp + 1], bdm[:, 32 * pp:32 * pp + 1], bdm[:, 32 * pp + 16:32 * pp + 17], AD)
    S.op("pool", lambda e: e.memset(P2[:].rearrange("p a b c -> p (a b c)"), 0.0), wp=[RS, R_P2])
    S.op("pool", lambda e: e.memset(Q2[:].rearrange("p a c -> p (a c)"), 0.0), wp=[RS, R_P2])
    S.op("pool", lambda e: e.memset(WH[:].rearrange("p a b c d -> p (a b c d)"), 0.0), wp=[RS, R_WH])
    S.op("pool", lambda e: e.memset(Gre[:, 0:1], 0.0), wp=[RS, R_G])
    S.op("pool", lambda e: e.memset(Gim[:, 0:1], 0.0), wp=[RS, R_G])
    tt(P["z"], P["are"], P["dt"], M)
    ts(P["p"], P["z"], 0.2, 1.0, M, AD)
    for cc in (0.25, 1.0 / 3.0, 0.5):
        tt(P["p"], P["p"], P["z"], M)
        ts(P["p"], P["p"], cc, 1.0, M, AD)
    tt(P["e1"], P["p"], P["z"], M)
    tt(P["t"], P["aim"], P["dt"], M)
    ts(P["t"], P["t"], 1.0 / 64.0, None, M)
    tt(P["t2"], P["t"], P["t"], M)
    ts(P["qv"], P["t2"], -1.0 / 72.0, 1.0, M, AD)
    for cc in (42.0, 20.0, 6.0):
        tt(P["qv"], P["qv"], P["t2"], M)
        ts(P["qv"], P["qv"], -1.0 / cc, 1.0, M, AD)
    tt(P["wi"], P["qv"], P["t"], M)
    ts(P["r"], P["t2"], -1.0 / 90.0, 1.0, M, AD)
    for cc in (56.0, 30.0, 12.0):
        tt(P["r"], P["r"], P["t2"], M)
        ts(P["r"], P["r"], -1.0 / cc, 1.0, M, AD)
    tt(P["r"], P["r"], P["t2"], M)
    ts(P["wr"], P["r"], -0.5, None, M)

    def dbl(wr, wi):
        tt(P["A"], wr, wr, M)
        tt(P["B"], wi, wi, M)
        tt(P["C"], wr, wi, M)
        stt(wr, wr, 2.0, P["A"], M, AD)
        tt(wr, wr, P["B"], SUB)
        tt(P["C"], P["C"], wi, AD)
        ts(wi, P["C"], 2.0, None, M)

    for _ in range(6):
        dbl(P["wr"], P["wi"])
    cp(P["ur"], P["wr"])
    cp(P["ui"], P["wi"])
    tt(P["A"], P["e1"], P["ur"], M)
    tt(P["ar1"], P["e1"], P["ur"], AD)
    tt(P["ar1"], P["ar1"], P["A"], AD)
    tt(P["A"], P["e1"], P["ui"], M)
    tt(P["ai"], P["ui"], P["A"], AD)
    cp(P["rho1"], P["e1"])
    for _ in range(4):
        tt(P["A"], P["rho1"], P["rho1"], M)
        stt(P["rho1"], P["rho1"], 2.0, P["A"], M, AD)
    ts(P["rho"], P["rho1"], 1.0, None, AD)
    cp(P["vr"], P["ur"])
    cp(P["vi"], P["ui"])
    for _ in range(4):
        dbl(P["vr"], P["vi"])
    tt(P["A"], P["are"], P["are"], M)
    tt(P["B"], P["aim"], P["aim"], M)
    tt(P["den"], P["A"], P["B"], AD)
    S.op("dve", lambda e: e.reciprocal(out=P["den"], in_=P["den"]), r=[RS], w=[RS])
    tt(P["A"], P["ar1"], P["are"], M)
    tt(P["B"], P["ai"], P["aim"], M)
    tt(P["zr"], P["A"], P["B"], AD)
    tt(P["zr"], P["zr"], P["den"], M)
    tt(P["A"], P["ai"], P["are"], M)
    tt(P["B"], P["ar1"], P["aim"], M)
    tt(P["zi"], P["A"], P["B"], SUB)
    tt(P["zi"], P["zi"], P["den"], M)
    zrb = P["zr"].unsqueeze(2).broadcast_to([128, 32, 16])
    zib = P["zi"].unsqueeze(2).broadcast_to([128, 32, 16])
    t0v = Tm[0][:, 0:512].rearrange("p (a c) -> p a c", c=16)
    t1v = Tm[1][:, 0:512].rearrange("p (a c) -> p a c", c=16)
    tt(t0v, Bre, zrb, M)
    tt(t1v, Bim, zib, M)
    tt(Bbr[:], t0v, t1v, SUB)
    tt(t0v, Bim, zrb, M)
    tt(t1v, Bre, zib, M)
    tt(Bbi[:], t0v, t1v, AD)
    S.op("dve", lambda e: e.memset(Pr[:, 0, :], 0.0), r=[RS], w=[RS])
    S.op("dve", lambda e: e.memset(Pi[:, 0, :], 0.0), r=[RS], w=[RS])
    cp(Pr[:, 1, :], P["ar1"])
    cp(Pi[:, 1, :], P["ai"])
    for lev in range(4):
        n = 1 << lev
        cr = Pr[:, n:n + 1, :].broadcast_to([128, n, 32])
        ci = Pi[:, n:n + 1, :].broadcast_to([128, n, 32])
        xr = Pr[:, 1:n + 1, :]
        xi = Pi[:, 1:n + 1, :]
        tv = [t[:, 0:32 * n].rearrange("p (a c) -> p a c", c=32) for t in Tm[0:2]]
        tt(tv[0], xr, cr, M)
        tt(tv[1], xi, ci, M)
        tt(tv[0], tv[0], tv[1], SUB)
        tt(tv[0], tv[0], xr, AD)
        tt(Pr[:, n + 1:2 * n + 1, :], tv[0], cr, AD)
        tt(tv[0], xr, ci, M)
        tt(tv[1], xi, cr, M)
        tt(tv[0], tv[0], tv[1], AD)
        tt(tv[0], tv[0], xi, AD)
        tt(Pi[:, n + 1:2 * n + 1, :], tv[0], ci, AD)

    Elo_r = sb(stB, "Elo_r", [128, 16, 17])
    Elo_i = sb(stB, "Elo_i", [128, 16, 17])
    Ehi_r = sb(stB, "Ehi_r", [128, 16, 33])
    Ehi_i = sb(stB, "Ehi_i", [128, 16, 33])

    def build_tab(Er_, Ei_, br_, bi_, nlev):
        S.op("dve", lambda e: e.memset(Er_[:, :, 0:1], 0.0), r=[RS], w=[RS, R_T, R_E])
        S.op("dve", lambda e: e.memset(Ei_[:, :, 0:1], 0.0), r=[RS], w=[RS, R_T, R_E])
        cp(Er_[:, :, 1:2], br_)
        cp(Ei_[:, :, 1:2], bi_)
        for lev in range(nlev):
            n = 1 << lev
            cr = Er_[:, :, n:n + 1].broadcast_to([128, 16, n])
            ci = Ei_[:, :, n:n + 1].broadcast_to([128, 16, n])
            xr = Er_[:, :, 1:n + 1]
            xi = Ei_[:, :, 1:n + 1]
            tv = [t[:, 0:16 * n].rearrange("p (a c) -> p a c", a=16) for t in Tm]
            tt(tv[0], xr, cr, M)
            tt(tv[1], xi, ci, M)
            tt(tv[2], xr, ci, M)
            tt(tv[3], xi, cr, M)
            tt(tv[0], tv[0], tv[1], SUB)
            tt(tv[2], tv[2], tv[3], AD)
            tt(tv[0], tv[0], xr, AD)
            tt(tv[2], tv[2], xi, AD)
            tt(Er_[:, :, n + 1:2 * n + 1], tv[0], cr, AD)
            tt(Ei_[:, :, n + 1:2 * n + 1], tv[2], ci, AD)

    def build_coarse(half):
        ctx["r"], ctx["w"] = [RS], [RS, R_T, R_E]
        hs = slice(16 * half, 16 * half + 16)
        build_tab(Elo_r, Elo_i, P["vr"][:, hs].unsqueeze(2), P["vi"][:, hs].unsqueeze(2), 4)
        build_tab(Ehi_r, Ehi_i, Elo_r[:, :, 16:17], Elo_i[:, :, 16:17], 5)
        ts(Elo_r[:].rearrange("p a c -> p (a c)"), Elo_r[:].rearrange("p a c -> p (a c)"), 1.0, None, AD)
        ts(Ehi_r[:].rearrange("p a c -> p (a c)"), Ehi_r[:].rearrange("p a c -> p (a c)"), 1.0, None, AD)

    RUT = Res()
    RHB = Res()
    T4 = [t[:].rearrange("p (m a c) -> p m a c", m=16, a=4) for t in Tm]
    evac_n = [0]

    def evac_eng():
        evac_n[0] += 1
        return "act" if evac_n[0] % 2 else "dve"

    for T in range(8):
        ps4 = slice(4 * T, 4 * T + 4)
        S.dma("sp", uT[:], Dr["uT_s"][T, :, :], r=[G.RSCR], w=[RUT])
        ctx["r"], ctx["w"] = [RS], [R_T]
        PrV = Pr[:, 0:16, ps4].unsqueeze(3).broadcast_to([128, 16, 4, 16])
        PiV = Pi[:, 0:16, ps4].unsqueeze(3).broadcast_to([128, 16, 4, 16])
        BrV = Bbr[:, ps4, :].unsqueeze(1).broadcast_to([128, 16, 4, 16])
        BiV = Bbi[:, ps4, :].unsqueeze(1).broadcast_to([128, 16, 4, 16])
        tt(T4[0], PrV, BrV, M)
        tt(T4[1], PiV, BiV, M)
        tt(T4[2], PrV, BiV, M)
        tt(T4[3], PiV, BrV, M)
        tt(T4[0], T4[0], T4[1], SUB)
        tt(T4[2], T4[2], T4[3], AD)
        ctx["r"], ctx["w"] = [RS, R_T], [R_P2]
        for hh in range(2):
            rows = slice(64 * hh, 64 * hh + 64)
            dre = P2[rows, :, 0, :].rearrange("p m (a c) -> p m a c", c=32)[:, :, :, 16 * hh:16 * hh + 16]
            dim_ = P2[rows, :, 1, :].rearrange("p m (a c) -> p m a c", c=32)[:, :, :, 16 * hh:16 * hh + 16]
            tt(dre, T4[0][rows], BrV[rows], AD)
            tt(dim_, T4[2][rows], BiV[rows], AD)
            qre = Q2[rows, 0, :].rearrange("p (a c) -> p a c", c=32)[:, :, 16 * hh:16 * hh + 16]
            qim = Q2[rows, 1, :].rearrange("p (a c) -> p a c", c=32)[:, :, 16 * hh:16 * hh + 16]
            cp(qre, Cre[rows, ps4, :])
            ts(qim, Cim[rows, ps4, :], -1.0, None, M)
        for q in range(4):
            bank = q
            for sl in range(4):
                m = 4 * q + sl
                S.op("pe", lambda e, m=m, sl=sl, bank=bank: e.matmul(PS[bank][:, sl * 128:(sl + 1) * 128], lhsT=P2[:, m, 0, :], rhs=Q2[:, 0, :], start=True, stop=False), r=[R_P2], w=[RPS[bank]], inc=False)
                S.op("pe", lambda e, m=m, sl=sl, bank=bank: e.matmul(PS[bank][:, sl * 128:(sl + 1) * 128], lhsT=P2[:, m, 1, :], rhs=Q2[:, 1, :], start=False, stop=True), r=[R_P2], w=[RPS[bank]], inc=(sl == 3))
            S.op("dve", lambda e, q=q, bank=bank: e.tensor_tensor(out=BD[:, 4 * q:4 * q + 4, :], in0=PS[bank][:].rearrange("p (a c) -> p a c", c=128), in1=bdm[:].unsqueeze(1).broadcast_to([128, 4, 128]), op=M), r=[RPS[bank], RC], w=[R_BD])
        ctx["r"], ctx["w"] = [RS, RC], [R_BD]
        stt(BD[:, 0, :], identf[:], dsm[:, T:T + 1], BD[:, 0, :], M, AD)
        if getattr(G, 'sub', 0) == 2:
            S.barrier(); stB.__exit__(None, None, None); stA.__exit__(None, None, None); return
        ctx["r"], ctx["w"] = [RS], [R_T]
        PrV = Pr[:, 1:17, ps4].unsqueeze(3).broadcast_to([128, 16, 4, 16])
        PiV = Pi[:, 1:17, ps4].unsqueeze(3).broadcast_to([128, 16, 4, 16])
        CrV = Cre[:, ps4, :].unsqueeze(1).broadcast_to([128, 16, 4, 16])
        CiV = Cim[:, ps4, :].unsqueeze(1).broadcast_to([128, 16, 4, 16])
        tt(T4[0], PrV, CrV, M)
        tt(T4[1], PiV, CiV, M)
        tt(T4[2], PrV, CiV, M)
        tt(T4[3], PiV, CrV, M)
        tt(T4[0], T4[0], T4[1], SUB)
        tt(T4[2], T4[2], T4[3], AD)
        tt(T4[2], T4[2], CiV, AD)
        ctx["r"], ctx["w"] = [RS, R_T], [R_WH]
        for hh in range(2):
            rows = slice(64 * hh, 64 * hh + 64)
            dre = WH[rows, :, 0, :, :].rearrange("p j a c -> p j (a c)").rearrange("p j (a b) -> p j a b", b=32)[:, :, 0:16:5, 16 * hh:16 * hh + 16]
            dim_ = WH[rows, :, 1, :, :].rearrange("p j a c -> p j (a c)").rearrange("p j (a b) -> p j a b", b=32)[:, :, 0:16:5, 16 * hh:16 * hh + 16]
            tt(dre, T4[0][rows], CrV[rows], AD)
            ts(dim_, T4[2][rows], -1.0, None, M)
        if T % 4 == 0:
            build_coarse(T // 4)
        ctx["r"], ctx["w"] = [RS], [R_T, R_E]
        pc0 = 4 * (T % 4)
        for a_ in range(0, 4, 2):
            pa = slice(pc0 + a_, pc0 + a_ + 2)
            hra = Ehi_r[:, pa, 0:32].unsqueeze(3).broadcast_to([128, 2, 32, 16])
            hia = Ehi_i[:, pa, 0:32].unsqueeze(3).broadcast_to([128, 2, 32, 16])
            lra = Elo_r[:, pa, 0:16].unsqueeze(2).broadcast_to([128, 2, 32, 16])
            lia = Elo_i[:, pa, 0:16].unsqueeze(2).broadcast_to([128, 2, 32, 16])
            t0 = Tm[0][:, 0:1024].rearrange("p (a j l) -> p a j l", a=2, j=32)
            t1 = Tm[1][:, 0:1024].rearrange("p (a j l) -> p a j l", a=2, j=32)
            era = Er[:, a_:a_ + 2, 0:512].rearrange("p a (j l) -> p a j l", j=32)
            eia = Ei[:, a_:a_ + 2, 0:512].rearrange("p a (j l) -> p a j l", j=32)
            tt(t0, hra, lra, M)
            tt(t1, hia, lia, M)
            tt(era, t0, t1, SUB)
            tt(t0, hra, lia, M)
            tt(t1, hia, lra, M)
            tt(eia, t0, t1, AD)
        cp(Er[:, :, 512:513], Ehi_r[:, pc0:pc0 + 4, 32:33])
        cp(Ei[:, :, 512:513], Ehi_i[:, pc0:pc0 + 4, 32:33])
        if getattr(G, 'sub', 0) == 3:
            S.barrier(); stB.__exit__(None, None, None); stA.__exit__(None, None, None); return
        uTv = uT[:].rearrange("p (sl r i) -> p sl r i", sl=4, r=16)
        for ph in range(2):
            for q in range(8):
                bank = q % 4
                for sl in range(4):
                    idx = 4 * q + sl
                    m, ri = idx // 2, idx % 2
                    S.op("pe", lambda e, m=m, ri=ri, sl=sl, bank=bank: e.matmul(PS[bank][:, sl * 128:(sl + 1) * 128], lhsT=P2[:, m, ri, :], rhs=identb[:], start=True, stop=True), r=[R_P2, RC], w=[RPS[bank]], inc=(sl == 3))
                for pl_ in range(2):
                    pp = 2 * ph + pl_
                    dst = WS[:, pl_, 2 * q:2 * q + 2, :, :].rearrange("p m r c -> p (m r c)")
                    eng = evac_eng()
                    if eng == "act":
                        S.op("act", lambda e, dst=dst, bank=bank, pp=pp: e.activation(out=dst, in_=PS[bank][:], func=AF.Copy, scale=pmask[:, pp:pp + 1]), r=[RPS[bank], RS], wp=[R_WS])
                    else:
                        S.op("dve", lambda e, dst=dst, bank=bank, pp=pp: e.tensor_scalar(out=dst, in0=PS[bank][:], scalar1=pmask[:, pp:pp + 1], scalar2=None, op0=M), r=[RPS[bank], RS], wp=[R_WS])
            for pl_ in range(2):
                pp = 2 * ph + pl_
                for ri in range(2):
                    bank = 2 * pl_ + ri
                    for m in range(16):
                        S.op("pe", lambda e, pl_=pl_, m=m, ri=ri, bank=bank: e.matmul(PS[bank][:].rearrange("p (a c) -> p a c", a=4), lhsT=WS[:, pl_, m, ri, :], rhs=uTv[:, :, 15 - m, :], start=(m == 0), stop=(m == 15)),
                             r=[R_WS, RUT], w=[RPS[bank]], inc=(m == 15))
            for pl_ in range(2):
                pp = 2 * ph + pl_
                b0, b1 = 2 * pl_, 2 * pl_ + 1
                cosv = Er[:, pp, 1:513]
                sinv = Ei[:, pp, 1:513]
                X0, X1, X2, X3 = (t[:, 0:512] for t in Tm)
                ctx["r"], ctx["w"] = [RS, R_E, R_G], [R_T]
                S.op("dve", lambda e, b0=b0: e.tensor_tensor(out=X0, in0=PS[b0][:], in1=cosv, op=M), r=[RS, R_E, RPS[b0]], w=[R_T])
                S.op("dve", lambda e, b1=b1: e.tensor_tensor(out=X1, in0=PS[b1][:], in1=sinv, op=M), r=[RS, R_E, RPS[b1]], w=[R_T])
                tt(X0, X0, X1, AD)
                S.op("dve", lambda e, b1=b1: e.tensor_tensor(out=X2, in0=PS[b1][:], in1=cosv, op=M), r=[RS, R_E, RPS[b1]], w=[R_T])
                S.op("dve", lambda e, b0=b0: e.tensor_tensor(out=X3, in0=PS[b0][:], in1=sinv, op=M), r=[RS, R_E, RPS[b0]], w=[R_T])
                tt(X2, X2, X3, SUB)
                rb = P["rho"][:, 4 * T + pp:4 * T + pp + 1].broadcast_to([128, 512])
                S.op("dve", lambda e, rb=rb: e.tensor_tensor_scan(out=Gre[:, 1:513], data0=rb, data1=X0, initial=0.0, op0=M, op1=AD), r=[RS, R_T], w=[R_G])
                S.op("dve", lambda e, rb=rb: e.tensor_tensor_scan(out=Gim[:, 1:513], data0=rb, data1=X2, initial=0.0, op0=M, op1=AD), r=[RS, R_T], w=[R_G])
                ck = Er[:, pp, 384:512]
                sk = Ei[:, pp, 384:512]
                gr = Gre[:, 384:512]
                gi = Gim[:, 384:512]
                a0, a1 = Tm[0][:, 512:640], Tm[1][:, 512:640]
                tt(a0, gr, ck, M)
                tt(a1, gi, sk, M)
                S.op("dve", lambda e, pp=pp: e.tensor_tensor(out=Hb[:, pp, 0, :], in0=a0, in1=a1, op=SUB), r=[R_T], w=[RHB])
                tt(a0, gr, sk, M)
                tt(a1, gi, ck, M)
                S.op("dve", lambda e, pp=pp: e.tensor_tensor(out=Hb[:, pp, 1, :], in0=a0, in1=a1, op=AD), r=[R_T], w=[RHB])
        if getattr(G, 'sub', 0) == 4:
            S.barrier(); stB.__exit__(None, None, None); stA.__exit__(None, None, None); return
        own0 = 3 * TOK
        for j in range(16):
            bank = 4 + j // 4
            reg = PS[bank][:, (j % 4) * 128:(j % 4 + 1) * 128]
            first = True
            for s in range(j + 1):
                S.op("pe", lambda e, j=j, s=s, reg=reg, first=first: e.matmul(reg, lhsT=BD[:, j - s, :], rhs=uT[:, own0 + s * 128: own0 + (s + 1) * 128], start=first, stop=False), r=[R_BD, RUT], w=[RPS[bank]], inc=False)
                first = False
            for pp in range(4):
                for ri in range(2):
                    last = (pp == 3 and ri == 1)
                    S.op("pe", lambda e, j=j, pp=pp, ri=ri, reg=reg, last=last: e.matmul(reg, lhsT=WH[:, j, ri, pp, :], rhs=Hb[:, pp, ri, :], start=False, stop=last), r=[R_WH, RHB], w=[RPS[bank]], inc=(last and j % 4 == 3))
            if j % 4 == 3:
                S.op("act", lambda e, bank=bank, T=T, j=j: e.activation(out=YG[:, T, (j // 4) * 512:(j // 4 + 1) * 512], in_=PS[bank][:], func=AF.Gelu_apprx_tanh), r=[RPS[bank]], w=[RYG])
        if getattr(G, 'sub', 0) == 5:
            break
    S.barrier()
    stB.__exit__(None, None, None)
    if getattr(G, 'sub', 0) == 6:
        stA.__exit__(None, None, None); return
    with contextlib.ExitStack() as stC:
        sg = [sb(stC, "sg%d" % i, [128, 512]) for i in range(2)]
        wglu = sb(stC, "wglu", [128, 8, 1024], BF16)
        RWG = Res()
        wgv = Dr["w_glu"].rearrange("(k p) c -> p k c", p=128)
        for k in range(8):
            S.dma("pool", wglu[:, k, :], wgv[:, k, :], wp=[RWG])
        RSG = [Res(), Res()]
        n = 0
        for ot in range(8):
            for tb in range(4):
                bank = n % 4
                si = n % 2
                n += 1
                for k in range(8):
                    S.op("pe", lambda e, k=k, ot=ot, tb=tb, bank=bank: e.matmul(PS[bank][:], lhsT=wglu[:, k, ot * 128:(ot + 1) * 128], rhs=YG[:, k, tb * 512:(tb + 1) * 512], start=(k == 0), stop=(k == 7)), r=[RWG, RYG], w=[RPS[bank]], inc=(k == 7))
                S.op("act", lambda e, bank=bank, si=si: e.activation(out=sg[si][:], in_=PS[bank][:], func=AF.Sigmoid), r=[RPS[bank]], w=[RSG[si]])
                S.op("dve", lambda e, ot=ot, tb=tb, si=si: e.tensor_tensor(out=G.YS[:, ot, tb * 512:(tb + 1) * 512], in0=YG[:, ot, tb * 512:(tb + 1) * 512], in1=sg[si][:], op=M), r=[RSG[si], RYG], w=[G.RYS])
        S.barrier()
    stA.__exit__(None, None, None)


def phase_attn(G):
    nc, S, sb, Dr, PS, RPS, RC = G.nc, G.S, G.sb, G.Dr, G.PS, G.RPS, G.RC
    identb, onesb = G.identb, G.onesb
    M = ALU.mult
    with contextlib.ExitStack() as stT:
        btbs = [sb(stT, "btb%d" % i, [128, 12, 512], BF16) for i in range(2)]
        RBTS = [Res(), Res()]
        RBT = Res()
        qTB = [sb(stT, "qTh%d" % i, [128, TOK], BF16) for i in range(2)]
        k3B = [sb(stT, "k3%d" % i, [128, 2 * TOK], BF16) for i in range(2)]
        v3B = [sb(stT, "v3%d" % i, [128, 2 * TOK], BF16) for i in range(2)]
        RQB, RKB, RVB3 = [Res(), Res()], [Res(), Res()], [Res(), Res()]
        k2 = sb(stT, "k2", [128, 20 * 128], BF16)
        v2 = sb(stT, "v2", [128, 20 * 128], BF16)
        k1 = sb(stT, "k1", [128, 17 * 128], BF16)
        v1 = sb(stT, "v1", [128, 17 * 128], BF16)
        Vb = sb(stT, "Vb", [128, 72, 128], BF16)
        PT = [sb(stT, "PT%d" % i, [128, 512], BF16) for i in range(4)]
        rden = sb(stT, "rden", [128, 512])
        numS = sb(stT, "numS", [128, 512])
        denS = sb(stT, "denS", [128, 512])
        RNUM = Res()
        cnt = [0]
        RK2, RV2, RVB, RDEN = Res(), Res(), Res(), Res()

        def load_head(h_):
            bb = h_ % 2
            S.dma("sp", qTB[bb][:], Dr["qT_s"][h_, :, :], r=[G.RSCR], w=[RQB[bb]])
            S.dma("sp", k3B[bb][:], Dr["kT_s"][h_, :, :], r=[G.RSCR], w=[RKB[bb]])
            S.dma("sp", v3B[bb][:], Dr["vT_s"][h_, :, :], r=[G.RSCR], w=[RVB3[bb]])

        load_head(0)
        RPT = [Res() for _ in range(6)]
        nev = [0]
        for h in range(8):
            btb = btbs[h % 2]
            RBTh = RBTS[h % 2]
            S.dma("pool", btb[:].rearrange("p a c -> p (a c)"), Dr["btab"][h, :, :], w=[RBTh])
            qT, k3, v3 = qTB[h % 2], k3B[h % 2], v3B[h % 2]
            RQ, RK, RV = RQB[h % 2], RKB[h % 2], RVB3[h % 2]
            if h + 1 < 8:
                load_head(h + 1)
            for (src, dst2, dst1, Rs, Rd, eng) in ((k3, k2, k1, RK, RK2, "dve"), (v3, v2, v1, RV, RV2, "act")):
                own = src[:, TOK:2 * TOK]
                halo = src[:, 0:TOK]
                ops = []
                for r4 in range(4):
                    o = dst2[:, (4 + 4 * r4) * 128:(8 + 4 * r4) * 128].rearrange("p (s4 mm ii) -> p s4 mm ii", s4=4, mm=4)
                    i_ = own.rearrange("p (mm r4 s4 ii) -> p mm r4 s4 ii", mm=4, r4=4, s4=4)[:, :, r4, :, :].rearrange("p mm s4 ii -> p s4 mm ii")
                    ops.append((o, i_))
                    o = dst2[:, r4 * 128:(r4 + 1) * 128].rearrange("p (mm ii) -> p mm ii", mm=4)
                    i_ = halo.rearrange("p (mm r4 s4 ii) -> p mm r4 s4 ii", mm=4, r4=4, s4=4)[:, :, r4, 3, :]
                    ops.append((o, i_))
                o = dst1[:, 128:17 * 128].rearrange("p (n r c) -> p n r c", n=16, r=16)
                i_ = own.rearrange("p (r n c) -> p n r c", r=16, n=16)
                ops.append((o, i_))
                o = dst1[:, 0:128].rearrange("p (r c) -> p r c", r=16)
                i_ = halo.rearrange("p (r n c) -> p r n c", r=16, n=16)[:, :, 15, :]
                ops.append((o, i_))
                for (o, i_) in ops:
                    if eng == "dve":
                        S.op("dve", lambda e, o=o, i_=i_: e.tensor_copy(out=o, in_=i_), r=[Rs], wp=[Rd])
                    else:
                        S.op("act", lambda e, o=o, i_=i_: e.activation(out=o, in_=i_, func=AF.Copy), r=[Rs], wp=[Rd])
            vsrc = [(v3, b) for b in range(32)] + [(v2, b) for b in range(20)] + [(v1, b) for b in range(17)]
            for q in range(18):
                bank = q % 8
                blks = vsrc[4 * q:4 * q + 4]
                for sl, (vt, b) in enumerate(blks):
                    S.op("pe", lambda e, vt=vt, b=b, sl=sl, bank=bank: e.matmul(PS[bank][:, sl * 128:(sl + 1) * 128], lhsT=vt[:, b * 128:(b + 1) * 128], rhs=identb[:], start=True, stop=True),
                         r=[RV, RV2, RC], w=[RPS[bank]], inc=(sl == len(blks) - 1))
                nb_ = len(blks)
                nev[0] += 1
                dst = Vb[:, 4 * q:4 * q + nb_, :].rearrange("p a c -> p (a c)")
                if nev[0] % 2:
                    S.op("act", lambda e, dst=dst, bank=bank, nb_=nb_: e.activation(out=dst, in_=PS[bank][:, 0:nb_ * 128], func=AF.Copy), r=[RPS[bank]], wp=[RVB])
                else:
                    S.op("dve", lambda e, dst=dst, bank=bank, nb_=nb_: e.tensor_copy(out=dst, in_=PS[bank][:, 0:nb_ * 128]), r=[RPS[bank]], wp=[RVB])
            q2v = qT[:].rearrange("p (mm r4 s4 ii) -> p mm r4 s4 ii", mm=4, r4=4, s4=4)
            q1v = qT[:].rearrange("p (mm r4 n c) -> p mm r4 n c", mm=4, r4=4, n=16)
            for g in range(4):
                brs = []
                us = []
                for mm in range(4):
                    r16 = 4 * mm + g
                    us.append((k3, qT[:, r16 * 128:(r16 + 1) * 128], mm * 128, 128, r16, 16 + r16, 0))
                brs.append((0, 1, us, None, None))
                us = []
                for s4 in range(4):
                    kown = 4 + 4 * g + s4
                    kprev = g if s4 == 0 else kown - 1
                    us.append((k2, q2v[:, :, g, s4, :], s4 * 128, 128, kprev, kown, 32))
                brs.append((2, 3, us, "p (mm s4 ii) -> p mm s4 ii", "p (s4 mm ii) -> p mm s4 ii"))
                us = []
                for n in range(16):
                    us.append((k1, q1v[:, :, g, n, :], n * 32, 32, n, 1 + n, 52))
                brs.append((4 + 2 * g, 5 + 2 * g, us, "p (mm n c) -> p mm n c", "p (n mm c) -> p mm n c"))
                for bi, (tprev, town, us, vS, vP) in enumerate(brs):
                    par = cnt[0] % 2
                    cnt[0] += 1
                    sbk = (2 * par, 2 * par + 1)
                    acc, den = 4 + 2 * par, 5 + 2 * par
                    for which in range(2):
                        bank = sbk[which]
                        tab = tprev if which == 0 else town
                        S.op("pe", lambda e, bank=bank, tab=tab: e.matmul(PS[bank][:], lhsT=identb[:], rhs=btb[:, tab, :], start=True, stop=False), r=[RBTh, RC], w=[RPS[bank]], inc=False)
                        for ui, (ksrc, qap, c0, N, kprev, kown, voff) in enumerate(us):
                            lastu = (ui == len(us) - 1)
                            kb = kprev if which == 0 else kown
                            rk = RK if ksrc is k3 else RK2
                            S.op("pe", lambda e, bank=bank, ksrc=ksrc, kb=kb, qap=qap, c0=c0, N=N, lastu=lastu: e.matmul(PS[bank][:, c0:c0 + N], lhsT=ksrc[:, kb * 128:(kb + 1) * 128], rhs=qap, start=False, stop=lastu), r=[rk, RQ], w=[RPS[bank]], inc=lastu)
                        S.op("act", lambda e, bank=bank: e.activation(out=PT[bank][:], in_=PS[bank][:], func=AF.Exp), r=[RPS[bank]], w=[RPT[bank]])
                    for ui, (ksrc, qap, c0, N, kprev, kown, voff) in enumerate(us):
                        lastu = (ui == len(us) - 1)
                        for which in range(2):
                            bank = sbk[which]
                            vidx = voff + (kprev if which == 0 else kown)
                            S.op("pe", lambda e, acc=acc, bank=bank, vidx=vidx, c0=c0, N=N, which=which: e.matmul(PS[acc][:, c0:c0 + N], lhsT=Vb[:, vidx, :], rhs=PT[bank][:, c0:c0 + N], start=(which == 0), stop=(which == 1)), r=[RVB, RPT[bank]], w=[RPS[acc]], inc=False)
                    for which in range(2):
                        bank = sbk[which]
                        S.op("pe", lambda e, den=den, bank=bank, which=which: e.matmul(PS[den][:], lhsT=onesb[:], rhs=PT[bank][:], start=(which == 0), stop=(which == 1)), r=[RC, RPT[bank]], w=[RPS[den]], inc=(which == 1))
                    if bi == 0:
                        S.op("dve", lambda e, acc=acc: e.tensor_copy(out=numS[:], in_=PS[acc][:]), r=[RPS[acc]], w=[RNUM])
                        S.op("dve", lambda e, den=den: e.tensor_copy(out=denS[:], in_=PS[den][:]), r=[RPS[den]], w=[RNUM])
                    else:
                        S.op("dve", lambda e, acc=acc, vS=vS, vP=vP: e.tensor_tensor(out=numS[:].rearrange(vS, mm=4, **({"s4": 4} if "s4" in vS else {"n": 16})), in0=numS[:].rearrange(vS, mm=4, **({"s4": 4} if "s4" in vS else {"n": 16})), in1=PS[acc][:].rearrange(vP, mm=4, **({"s4": 4} if "s4" in vP else {"n": 16})), op=ALU.add), r=[RPS[acc]], w=[RNUM])
                        S.op("dve", lambda e, den=den, vS=vS, vP=vP: e.tensor_tensor(out=denS[:].rearrange(vS, mm=4, **({"s4": 4} if "s4" in vS else {"n": 16})), in0=denS[:].rearrange(vS, mm=4, **({"s4": 4} if "s4" in vS else {"n": 16})), in1=PS[den][:].rearrange(vP, mm=4, **({"s4": 4} if "s4" in vP else {"n": 16})), op=ALU.add), r=[RPS[den]], w=[RNUM])
                S.op("dve", lambda e: e.reciprocal(out=denS[:], in_=denS[:]), r=[RNUM], w=[RNUM])
                outv = G.YA[:, h, :].rearrange("p (mm r4 i) -> p mm r4 i", mm=4, r4=4)[:, :, g, :]
                S.op("dve", lambda e, outv=outv: e.tensor_tensor(out=outv, in0=numS[:].rearrange("p (mm i) -> p mm i", mm=4), in1=denS[:].rearrange("p (mm i) -> p mm i", mm=4), op=M), r=[RNUM], w=[G.RYA])
        S.barrier()


def phase_out(G, stY):
    nc, S, sb, Dr, PS, RPS, RC = G.nc, G.S, G.sb, G.Dr, G.PS, G.RPS, G.RC
    onesb, epsb, modT, vecs = G.onesb, G.epsb, G.modT, G.vecs
    M, AD = ALU.mult, ALU.add
    gate1 = modT[:, 32:48]
    shift2 = modT[:, 48:64]
    gate2 = modT[:, 80:96]
    s2p = vecs[:, 48:64]
    Dr_h2 = G.h2_s
    RX1S, RH2S = Res(), Res()
    with contextlib.ExitStack() as st:
        wout = sb(st, "wout", [128, 16, DM], BF16)
        RWOB = [Res() for _ in range(8)]
        wov = Dr["w_out"].rearrange("(k p) c -> p k c", p=128)
        for cb in range(8):
            S.dma("pool", wout[:, :, cb * 256:(cb + 1) * 256], wov[:, :, cb * 256:(cb + 1) * 256], w=[RWOB[cb]])
        xt = [sb(st, "xo%d" % i, [128, 16, 256]) for i in range(2)]
        sq = sb(st, "sqo", [128, 16, 256], BF16)
        h2 = [sb(st, "h2o%d" % i, [128, 16, 256], BF16) for i in range(2)]
        rs = [sb(st, "rso%d" % i, [128, 256]) for i in range(2)]
        RXT, RH2, RRS, RSQ = [Res(), Res()], [Res(), Res()], [Res(), Res()], Res()
        xTv = Dr["xT"].rearrange("(k p) c -> p k c", p=128)
        nb = 0
        def load_xo(t_):
            S.dma("sp", xt[t_ % 2][:], xTv[:, :, 3 * TOK + t_ * 256: 3 * TOK + t_ * 256 + 256], w=[RXT[t_ % 2]])

        load_xo(0)
        for tb in range(8):
            b = tb % 2
            c0 = tb * 256
            if tb + 1 < 8:
                load_xo(tb + 1)
            for dp in range(8):
                bank = nb % 6
                nb += 1
                for half in range(2):
                    dt_ = 2 * dp + half
                    for k in range(16):
                        src = G.YS if k < 8 else G.YA
                        rr = G.RYS if k < 8 else G.RYA
                        S.op("pe", lambda e, k=k, dt_=dt_, half=half, bank=bank, src=src: e.matmul(PS[bank][:, half * 256:(half + 1) * 256], lhsT=wout[:, k, dt_ * 128:(dt_ + 1) * 128], rhs=src[:, k % 8, c0:c0 + 256], start=(k == 0), stop=(k == 15)),
                             r=[RWOB[dp], rr], w=[RPS[bank]], inc=(k == 15))
                for half in range(2):
                    dt_ = 2 * dp + half
                    S.op("dve", lambda e, dt_=dt_, half=half, bank=bank: e.scalar_tensor_tensor(out=xt[b][:, dt_, :], in0=PS[bank][:, half * 256:(half + 1) * 256], scalar=gate1[:, dt_:dt_ + 1], in1=xt[b][:, dt_, :], op0=M, op1=AD),
                         r=[RPS[bank], RXT[b], G.RMOD], w=[RXT[b]])
            S.dma("sp", Dr["x1_s"][:, :, c0:c0 + 256].rearrange("k p c -> p k c"), xt[b][:], r=[RXT[b]], wp=[RX1S])
            S.op("act", lambda e: e.activation(out=sq[:], in_=xt[b][:], func=AF.Square), r=[RXT[b]], w=[RSQ])
            for k in range(16):
                S.op("pe", lambda e, k=k: e.matmul(PS[6][:, 0:256], lhsT=onesb[:], rhs=sq[:, k, :], start=(k == 0), stop=(k == 15)), r=[RSQ, RC], w=[RPS[6]], inc=(k == 15))
            S.op("act", lambda e: e.activation(out=rs[b][:], in_=PS[6][:, 0:256], func=AF.Sqrt, bias=epsb[:, 0:1], scale=1.0 / DM), r=[RPS[6], RC], w=[RRS[b]])
            S.op("dve", lambda e: e.reciprocal(out=rs[b][:], in_=rs[b][:]), r=[RRS[b]], w=[RRS[b]])
            S.op("dve", lambda e: e.tensor_tensor(out=xt[b][:], in0=xt[b][:], in1=rs[b][:].unsqueeze(1).broadcast_to([128, 16, 256]), op=M), r=[RXT[b], RRS[b]], w=[RXT[b]])
            for k in range(16):
                if k % 2 == 0:
                    S.op("act", lambda e, k=k: e.activation(out=h2[b][:, k, :], in_=xt[b][:, k, :], func=AF.Identity, bias=shift2[:, k:k + 1], scale=s2p[:, k:k + 1]), r=[RXT[b], G.RMOD], wp=[RH2[b]])
                else:
                    S.op("dve", lambda e, k=k: e.tensor_scalar(out=h2[b][:, k, :], in0=xt[b][:, k, :], scalar1=s2p[:, k:k + 1], scalar2=shift2[:, k:k + 1], op0=M, op1=AD), r=[RXT[b], G.RMOD], wp=[RH2[b]])
            S.dma("sp", Dr_h2[:, :, c0:c0 + 256].rearrange("k p c -> p k c"), h2[b][:], r=[RH2[b]], wp=[RH2S])
        S.barrier()
    stY.__exit__(None, None, None)
    with contextlib.ExitStack() as st:
        h2T = sb(st, "h2T", [128, 16, 1024], BF16)
        act = sb(st, "ffact", [128, NHT, 1024], BF16)
        wgb = [sb(st, "wgb%d" % i, [128, 16, 256], BF16) for i in range(2)]
        wub = [sb(st, "wub%d" % i, [128, 16, 256], BF16) for i in range(2)]
        wdb = [sb(st, "wdb%d" % i, [128, NHT, 128], BF16) for i in range(2)]
        sgt = [sb(st, "sgt%d" % i, [128, 512]) for i in range(2)]
        x1t = [sb(st, "x1t%d" % i, [128, 512]) for i in range(2)]
        ost = [sb(st, "ost%d" % i, [128, 512]) for i in range(2)]
        RH, RACT = Res(), Res()
        RWG, RWU, RWD = [Res(), Res()], [Res(), Res()], [Res(), Res()]
        RSG, RX1, ROS = [Res(), Res()], [Res(), Res()], [Res(), Res()]
        ROUT = Res()
        nbk = 0
        nsg = 0
        nwd = 0
        nwg = 0
        for tile in range(2):
            t0 = tile * 1024
            S.dma("sp", h2T[:], Dr_h2[:, :, t0:t0 + 1024].rearrange("k p c -> p k c"), r=[RH2S], w=[RH])
            for hb in range(22):
                wb_ = nwg % 2
                nwg += 1
                S.dma("pool", wgb[wb_][:].rearrange("p k c -> p (k c)"), Dr["wg_t"][hb, :, :], w=[RWG[wb_]])
                S.dma("pool", wub[wb_][:].rearrange("p k c -> p (k c)"), Dr["wu_t"][hb, :, :], w=[RWU[wb_]])
                for ht2 in range(2):
                    ht = 2 * hb + ht2
                    for half in range(2):
                        bg = (nbk % 4) * 2
                        bu = bg + 1
                        nbk += 1
                        for k in range(16):
                            S.op("pe", lambda e, k=k, ht2=ht2, half=half, bg=bg: e.matmul(PS[bg][:], lhsT=wgb[wb_][:, k, ht2 * 128:(ht2 + 1) * 128], rhs=h2T[:, k, half * 512:(half + 1) * 512], start=(k == 0), stop=(k == 15)), r=[RWG[wb_], RH], w=[RPS[bg]], inc=(k == 15))
                        for k in range(16):
                            S.op("pe", lambda e, k=k, ht2=ht2, half=half, bu=bu: e.matmul(PS[bu][:], lhsT=wub[wb_][:, k, ht2 * 128:(ht2 + 1) * 128], rhs=h2T[:, k, half * 512:(half + 1) * 512], start=(k == 0), stop=(k == 15)), r=[RWU[wb_], RH], w=[RPS[bu]], inc=(k == 15))
                        si = nsg % 2
                        nsg += 1
                        S.op("act", lambda e, bg=bg, si=si: e.activation(out=sgt[si][:], in_=PS[bg][:], func=AF.Silu), r=[RPS[bg]], w=[RSG[si]])
                        S.op("dve", lambda e, bu=bu, si=si, ht=ht, half=half: e.tensor_tensor(out=act[:, ht, half * 512:(half + 1) * 512], in0=sgt[si][:], in1=PS[bu][:], op=M), r=[RSG[si], RPS[bu]], w=[RACT])
            for dt_ in range(16):
                wd_ = nwd % 2
                nwd += 1
                S.dma("pool", wdb[wd_][:].rearrange("p k c -> p (k c)"), Dr["wd_t"][dt_, :, :], w=[RWD[wd_]])
                for half in range(2):
                    bank = (nbk % 4) * 2
                    nbk += 1
                    si = nsg % 2
                    nsg += 1
                    c0 = t0 + half * 512
                    S.dma("sp", x1t[si][:], Dr["x1_s"][dt_, :, c0:c0 + 512], r=[RX1S], w=[RX1[si]])
                    for k in range(NHT):
                        S.op("pe", lambda e, k=k, half=half, bank=bank: e.matmul(PS[bank][:], lhsT=wdb[wd_][:, k, :], rhs=act[:, k, half * 512:(half + 1) * 512], start=(k == 0), stop=(k == NHT - 1)), r=[RWD[wd_], RACT], w=[RPS[bank]], inc=(k == NHT - 1))
                    S.op("dve", lambda e, bank=bank, si=si, dt_=dt_: e.scalar_tensor_tensor(out=ost[si][:], in0=PS[bank][:], scalar=gate2[:, dt_:dt_ + 1], in1=x1t[si][:], op0=M, op1=AD), r=[RPS[bank], RX1[si], G.RMOD], w=[ROS[si]])
                    S.dma("sp", Dr["outT"][dt_ * 128:(dt_ + 1) * 128, c0:c0 + 512], ost[si][:], r=[ROS[si]], wp=[ROUT])
        S.barrier()


def _perm_slot(xs):
    d = xs.shape[1]
    return xs.reshape(128, 16, d).transpose(2, 1, 0).reshape(d, 2048)


def _unperm_slot(yT):
    d = yT.shape[0]
    return yT.reshape(d, 16, 128).transpose(2, 1, 0).reshape(2048, d)


def _bias_tables(halo_valid):
    slopes = np.exp2(-8.0 * np.arange(1, 9, dtype=np.float32) / 8).astype(np.float32)
    kk = np.arange(128)[:, None]
    cc = np.arange(512)[None, :]
    out = np.zeros((8, 128, 12, 512), np.float32)

    def fill(t, mk, mq, d, is_prev, halo_cols):
        if is_prev:
            dist = mq + 128 - mk
            valid = dist <= 128
        else:
            dist = mq - mk
            valid = dist >= 0
        if is_prev and not halo_valid:
            valid = valid & (~halo_cols)
        for h in range(8):
            out[h, :, t, :] = np.where(valid, -slopes[h] * d * dist, NEG)

    allc = np.ones((1, 512), bool)
    fill(0, kk, cc % 128, 16, True, allc)
    fill(1, kk, cc % 128, 16, False, allc)
    mk2 = 4 * (kk % 32) + kk // 32
    mq2 = 4 * (cc % 32) + (cc % 128) // 32
    fill(2, mk2, mq2, 4, True, (cc // 128) == 0)
    fill(3, mk2, mq2, 4, False, allc)
    tk1 = 16 * (kk % 8) + kk // 8
    for r4 in range(4):
        tq1 = 16 * (cc % 8) + 4 * ((cc % 32) // 8) + r4
        fill(4 + 2 * r4, tk1, tq1, 1, True, (cc // 32) == 0)
        fill(5 + 2 * r4, tk1, tq1, 1, False, allc)
    return out.reshape(8, 128, 12 * 512)


def _prep_shared(inp):
    f = np.float32
    sh = {}
    sh["w_ada"] = np.ascontiguousarray(inp["w_ada"][0], f)
    sh["badaT"] = np.ascontiguousarray(inp["b_ada"][0].reshape(96, 128).T, f)
    sh["n1w"] = np.ascontiguousarray(inp["norm1_w"][0].reshape(16, 128).T, f)
    sh["n2w"] = np.ascontiguousarray(inp["norm2_w"][0].reshape(16, 128).T, f)
    sh["w_in"] = np.ascontiguousarray(inp["w_in"][0], f)

    def gn(a):
        return np.ascontiguousarray(a.reshape(32, 128).T, f)
    sh["a_re"] = gn(inp["ssm_a_re"][0])
    sh["a_im"] = gn(inp["ssm_a_im"][0])
    sh["ldt"] = gn(np.repeat(inp["ssm_log_dt"][0][:, None], 64, axis=1))

    def gnc(a):
        return np.ascontiguousarray(a.reshape(32, 2, 64, 16).transpose(1, 2, 0, 3).reshape(128, 512), f)
    sh["b_re"] = gnc(inp["ssm_b_re"][0])
    sh["b_im"] = gnc(inp["ssm_b_im"][0])
    sh["c_reT"] = gnc(inp["ssm_c_re"][0].transpose(0, 2, 1))
    sh["c_imT"] = gnc(inp["ssm_c_im"][0].transpose(0, 2, 1))
    sh["ssm_d"] = np.ascontiguousarray(inp["ssm_d"][0].reshape(8, 128).T, f)
    sh["w_glu"] = np.ascontiguousarray(inp["ssm_w_glu"][0], f)
    sh["qnw"] = np.ascontiguousarray(inp["q_norm_w"][0].reshape(128, 1), f)
    sh["knw"] = np.ascontiguousarray(inp["k_norm_w"][0].reshape(128, 1), f)
    sh["w_out"] = np.ascontiguousarray(inp["w_out"][0], f)
    wg = inp["w_ffn_gate"][0].reshape(16, 128, 22, 256).transpose(2, 1, 0, 3)
    wu = inp["w_ffn_up"][0].reshape(16, 128, 22, 256).transpose(2, 1, 0, 3)
    sh["wg_t"] = np.ascontiguousarray(wg, f).reshape(22, 128, 16 * 256)
    sh["wu_t"] = np.ascontiguousarray(wu, f).reshape(22, 128, 16 * 256)
    wd = inp["w_ffn_down"][0].reshape(NHT, 128, 16, 128).transpose(2, 1, 0, 3)
    sh["wd_t"] = np.ascontiguousarray(wd, f).reshape(16, 128, NHT * 128)
    sh["ident"] = np.eye(128, dtype=f)
    g = np.arange(128) // 16
    sh["bdmask"] = (g[:, None] == g[None, :]).astype(f)
    return sh


def _prep_core(inp, core):
    b, j = core // 4, core % 4
    x = np.asarray(inp["x"], np.float32)
    m = {}
    xT = np.zeros((DM, 4 * TOK), np.float32)
    sm = np.zeros((128, 4), np.float32)
    for s in range(4):
        jj = j - 3 + s
        if jj >= 0:
            xT[:, s * TOK:(s + 1) * TOK] = _perm_slot(x[b, jj * TOK:(jj + 1) * TOK])
            sm[:, s] = 1.0
    m["xT"] = xT
    m["smask"] = sm
    m["cT"] = np.ascontiguousarray(np.asarray(inp["c"], np.float32)[b].reshape(16, 128).T)
    m["btab"] = _bias_tables(j > 0)
    return m


def kernel(**inputs):
    inp = {k: np.asarray(v) for k, v in inputs.items()}
    nc = build_program(STAGE)
    sh = _prep_shared(inp)
    in_maps = []
    ncores = DEBUG.get("ncores", NCORES)
    for core in DEBUG.get("corelist", range(ncores)):
        m = dict(sh)
        m.update(_prep_core(inp, core))
        in_maps.append(m)
    res = run_bass_kernel_spmd(nc, in_maps, core_ids=list(range(ncores)))
    if STAGE < 99:
        DEBUG["res"] = res.results
        return None
    out = np.zeros((2, SEQ, DM), np.float32)
    for core in range(NCORES):
        b, j = core // 4, core % 4
        out[b, j * TOK:(j + 1) * TOK] = _unperm_slot(res.results[core]["outT"])
    return out
```

```python
import contextlib
import numpy as np
import ml_dtypes
import concourse.bass as bass
import concourse.mybir as mybir
from concourse.bass_utils import run_bass_kernel_spmd

F32 = mybir.dt.float32
BF16 = mybir.dt.bfloat16
ALU = mybir.AluOpType
AF = mybir.ActivationFunctionType
AX = mybir.AxisListType

NCORES = 8
DM = 2048
SEQ = 8192
TOK = 2048
FFH = 5632
NHT = FFH // 128
EPS = 1e-6
NEG = -1.0e30
STAGE = 99
DEBUG = {}


class Res:
    __slots__ = ("w", "rs")

    def __init__(self):
        self.w = {}
        self.rs = []


class Sched:
    ENG = ("pe", "act", "dve", "pool", "sp")
    NDS = 12

    def __init__(self, nc, es):
        self.nc = nc
        self.e = dict(pe=nc.tensor, act=nc.scalar, dve=nc.vector, pool=nc.gpsimd, sp=nc.sync)
        self.sem = {k: es.enter_context(nc.semaphore("s_" + k)) for k in self.ENG}
        self.cnt = {k: 0 for k in self.ENG}
        self.seen = {k: {} for k in self.ENG}
        self.dsem = {}
        self.dval = {}
        self.dnext = {}
        for q in ("sp", "act", "pool"):
            for i in range(self.NDS):
                self.dsem[(q, i)] = es.enter_context(nc.semaphore("d_%s%d" % (q, i)))
                self.dval[(q, i)] = 0
            self.dnext[q] = 0
        self.ninst = 0

    def _wait(self, eng, tok):
        if tok is None:
            return
        kind, key, val = tok
        if kind == "c":
            if eng == "pe" and key == "pe":
                return
            sem = self.sem[key]
        else:
            sem = self.dsem[key]
        if self.seen[eng].get((kind, key), 0) >= val:
            return
        self.e[eng].wait_ge(sem, val)
        self.seen[eng][(kind, key)] = val

    def _deps(self, eng, r, w, wp=()):
        for x in r:
            for t in list(x.w.values()):
                self._wait(eng, t)
        for x in w:
            for t in list(x.w.values()):
                self._wait(eng, t)
            for t in x.rs:
                self._wait(eng, t)
        for x in wp:
            for t in x.rs:
                self._wait(eng, t)

    def _commit(self, tok, r, w, wp=()):
        for x in wp:
            x.w[(tok[0], tok[1])] = tok
        for x in r:
            x.rs.append(tok)
            if len(x.rs) > 24:
                best = {}
                for t in x.rs:
                    k = (t[0], t[1])
                    if k not in best or best[k][2] < t[2]:
                        best[k] = t
                x.rs = list(best.values())
        for x in w:
            x.w = {(tok[0], tok[1]): tok}
            x.rs = []

    def op(self, eng, fn, r=(), w=(), inc=True, wp=()):
        self._deps(eng, r, w, wp)
        inst = fn(self.e[eng])
        self.ninst += 1
        if inc:
            self.cnt[eng] += 1
            inst.then_inc(self.sem[eng], 1)
            tok = ("c", eng, self.cnt[eng])
        else:
            tok = ("c", eng, self.cnt[eng] + 1)
        self._commit(tok, r, w, wp)
        return tok

    def dma(self, q, out, in_, r=(), w=(), wp=()):
        self._deps(q, r, w, wp)
        i = self.dnext[q]
        self.dnext[q] = (i + 1) % self.NDS
        key = (q, i)
        if self.dval[key] > 0:
            self._wait(q, ("d", key, self.dval[key]))
        self.dval[key] += 16
        self.e[q].dma_start(out=out, in_=in_).then_inc(self.dsem[key], 16)
        self.ninst += 1
        tok = ("d", key, self.dval[key])
        self._commit(tok, r, w, wp)
        return tok

    def barrier(self):
        toks = [("c", k, self.cnt[k]) for k in self.ENG if self.cnt[k] > 0]
        toks += [("d", k, v) for k, v in self.dval.items() if v > 0]
        for eng in self.ENG:
            for t in toks:
                if t[0] == "c" and t[1] == eng:
                    continue
                if eng == "pe" and t[0] == "c" and t[1] == "pe":
                    continue
                self._wait(eng, t)


def build_program(stage=99):
    nc = bass.Bass("TRN2", target_bir_lowering=False)
    Dr = {}

    def din(name, shape, dt=F32):
        Dr[name] = nc.dram_tensor(name, list(shape), dt, kind="ExternalInput").ap()

    def dscr(name, shape, dt):
        Dr[name] = nc.dram_tensor(name, list(shape), dt, kind="Internal").ap()

    def dout(name, shape, dt=F32):
        Dr[name] = nc.dram_tensor(name, list(shape), dt, kind="ExternalOutput").ap()

    din("xT", [DM, 4 * TOK])
    din("cT", [128, 16])
    din("smask", [128, 4])
    din("w_ada", [DM, 6 * DM])
    din("badaT", [128, 96])
    din("n1w", [128, 16])
    din("n2w", [128, 16])
    din("w_in", [DM, 4096])
    din("a_re", [128, 32])
    din("a_im", [128, 32])
    din("ldt", [128, 32])
    din("b_re", [128, 512])
    din("b_im", [128, 512])
    din("c_reT", [128, 512])
    din("c_imT", [128, 512])
    din("ssm_d", [128, 8])
    din("w_glu", [1024, 1024])
    din("qnw", [128, 1])
    din("knw", [128, 1])
    din("w_out", [DM, DM])
    din("wg_t", [22, 128, 16 * 256])
    din("wu_t", [22, 128, 16 * 256])
    din("wd_t", [16, 128, NHT * 128])
    din("btab", [8, 128, 12 * 512])
    din("ident", [128, 128])
    din("bdmask", [128, 128])
    dout("outT", [DM, TOK])
    dscr("uT_s", [8, 128, 4 * TOK], BF16)
    dscr("qT_s", [8, 128, TOK], BF16)
    dscr("kT_s", [8, 128, 2 * TOK], BF16)
    dscr("vT_s", [8, 128, 2 * TOK], BF16)
    dscr("x1_s", [16, 128, TOK], F32)
    dscr("h2_s", [16, 128, TOK], BF16)
    if stage < 99:
        dout("dbg", [128, 4096])

    es = contextlib.ExitStack()
    with es:
        S = Sched(nc, es)

        def sb(st, name, shape, dt=F32):
            return st.enter_context(nc.sbuf_tensor(name, list(shape), dt))

        PS = [es.enter_context(nc.psum_tensor("ps%d" % i, [128, 512], F32)) for i in range(8)]
        RPS = [Res() for _ in range(8)]

        identf = sb(es, "identf", [128, 128])
        identb = sb(es, "identb", [128, 128], BF16)
        onesb = sb(es, "onesb", [128, 128], BF16)
        bdm = sb(es, "bdm", [128, 128])
        modT = sb(es, "modT", [128, 96])
        vecs = sb(es, "vecs", [128, 16 * 8])
        smk = sb(es, "smk", [128, 4])
        qkw = sb(es, "qkw", [128, 2])
        epsb = sb(es, "epsb", [128, 1])
        RC = Res()
        S.dma("sp", identf[:], Dr["ident"][:, :], wp=[RC])
        S.dma("sp", bdm[:], Dr["bdmask"][:, :], wp=[RC])
        S.dma("sp", vecs[:, 0:16], Dr["n1w"][:, :], wp=[RC])
        S.dma("sp", vecs[:, 16:32], Dr["n2w"][:, :], wp=[RC])
        S.dma("sp", smk[:], Dr["smask"][:, :], wp=[RC])
        S.dma("sp", qkw[:, 0:1], Dr["qnw"][:, :], wp=[RC])
        S.dma("sp", qkw[:, 1:2], Dr["knw"][:, :], wp=[RC])
        S.op("dve", lambda e: e.tensor_copy(out=identb[:], in_=identf[:]), r=[RC], w=[RC])
        S.op("dve", lambda e: e.memset(onesb[:], 1.0), w=[RC])
        S.op("dve", lambda e: e.memset(epsb[:], EPS), w=[RC])
        S.op("dve", lambda e: e.tensor_scalar(out=qkw[:, 0:1], in0=qkw[:, 0:1], scalar1=float(128.0 ** -0.5), scalar2=None, op0=ALU.mult), r=[RC], w=[RC])
        n1w = vecs[:, 0:16]
        n2w = vecs[:, 16:32]
        s1p = vecs[:, 32:48]
        s2p = vecs[:, 48:64]
        shift1 = modT[:, 0:16]
        gate1 = modT[:, 32:48]
        shift2 = modT[:, 48:64]
        gate2 = modT[:, 80:96]
        RMOD = Res()

        st0 = contextlib.ExitStack()
        st0.__enter__()
        cTs = sb(st0, "cTs", [128, 16])
        condb = sb(st0, "condb", [128, 16], BF16)
        badaT = sb(st0, "badaTs", [128, 96])
        wa = [sb(st0, "wa%d" % i, [128, 16, 128], BF16) for i in range(2)]
        RWA = [Res(), Res()]
        S.dma("sp", cTs[:], Dr["cT"][:, :], wp=[RC])
        S.dma("sp", badaT[:], Dr["badaT"][:, :], wp=[RC])
        S.op("act", lambda e: e.activation(out=condb[:], in_=cTs[:], func=AF.Silu), r=[RC], w=[RC])
        wada_v = Dr["w_ada"].rearrange("(k p) c -> p k c", p=128)

        def p0_block(cb):
            buf = cb % 2
            S.dma("pool", wa[buf][:], wada_v[:, :, cb * 128:(cb + 1) * 128], w=[RWA[buf]])
            for ct in range(1):
                col = cb
                for k in range(16):
                    S.op("pe", lambda e, k=k, ct=ct, col=col: e.matmul(PS[7][:, col:col + 1], lhsT=wa[buf][:, k, ct * 128:(ct + 1) * 128], rhs=condb[:, k:k + 1], start=(k == 0), stop=(k == 15)),
                         r=[RWA[buf], RC], w=[RPS[7]], inc=(k == 15))

        for cb in range(32):
            p0_block(cb)
        S.op("dve", lambda e: e.tensor_tensor(out=modT[:, 0:32], in0=PS[7][:, 0:32], in1=badaT[:, 0:32], op=ALU.add), r=[RPS[7], RC], w=[RMOD])
        S.op("dve", lambda e: e.scalar_tensor_tensor(out=s1p, in0=modT[:, 16:32], scalar=1.0, in1=n1w, op0=ALU.add, op1=ALU.mult), r=[RMOD, RC], w=[RMOD])

        st1 = contextlib.ExitStack()
        st1.__enter__()
        win = sb(st1, "win", [128, 16, 4096], BF16)
        RWIN = Res()
        winv = Dr["w_in"].rearrange("(k p) c -> p k c", p=128)
        RWIN2 = Res()
        for k in range(16):
            S.dma("pool", win[:, k, 0:1024], winv[:, k, 0:1024], wp=[RWIN])
        for k in range(16):
            S.dma("pool", win[:, k, 1024:4096], winv[:, k, 1024:4096], wp=[RWIN2])
        xt = [sb(st1, "xt%d" % i, [128, 16, 256]) for i in range(2)]
        sq = [sb(st1, "sq0", [128, 16, 256], BF16)] * 2
        hT = [sb(st1, "hT%d" % i, [128, 16, 256], BF16) for i in range(2)]
        rstd = [sb(st1, "rstd%d" % i, [128, 256]) for i in range(2)]
        stg = [sb(st1, "stg%d" % i, [128, 2, 256], BF16) for i in range(4)]
        sqq = [sb(st1, "sqq%d" % i, [128, 512], BF16) for i in range(2)]
        rq = [sb(st1, "rq%d" % i, [128, 512]) for i in range(2)]
        RXT = [Res(), Res()]
        RSQ = [Res()] * 2
        RHT = [Res(), Res()]
        RRS = [Res(), Res()]
        RSTG = [Res() for _ in range(4)]
        RSQQ = [Res(), Res()]
        RRQ = [Res(), Res()]
        RSCR = Res()
        xTv = Dr["xT"].rearrange("(k p) c -> p k c", p=128)
        nstg = [0]
        nqk = [0]
        nbank = [0]

        def norm_mod(xtile, Rx, sqb, Rsq, rs, Rrs, hout, Rh, ncol, sp_, sh_, psb, Rpsb):
            S.op("act", lambda e: e.activation(out=sqb, in_=xtile, func=AF.Square), r=[Rx], w=[Rsq])
            for k in range(16):
                S.op("pe", lambda e, k=k: e.matmul(psb, lhsT=onesb[:], rhs=sqb[:, k, :], start=(k == 0), stop=(k == 15)), r=[Rsq, RC], w=[Rpsb], inc=(k == 15))
            S.op("act", lambda e: e.activation(out=rs, in_=psb, func=AF.Sqrt, bias=epsb[:, 0:1], scale=1.0 / DM), r=[Rpsb, RC], w=[Rrs])
            S.op("dve", lambda e: e.reciprocal(out=rs, in_=rs), r=[Rrs], w=[Rrs])
            S.op("dve", lambda e: e.tensor_tensor(out=xtile, in0=xtile, in1=rs.unsqueeze(1).broadcast_to([128, 16, ncol]), op=ALU.mult), r=[Rx, Rrs], w=[Rx])
            for k in range(16):
                if k % 2 == 0:
                    S.op("act", lambda e, k=k: e.activation(out=hout[:, k, :], in_=xtile[:, k, :], func=AF.Identity, bias=sh_[:, k:k + 1], scale=sp_[:, k:k + 1]), r=[Rx, RMOD], wp=[Rh])
                else:
                    S.op("dve", lambda e, k=k: e.tensor_scalar(out=hout[:, k, :], in0=xtile[:, k, :], scalar1=sp_[:, k:k + 1], scalar2=sh_[:, k:k + 1], op0=ALU.mult, op1=ALU.add), r=[Rx, RMOD], wp=[Rh])

        def load_x(t_):
            S.dma("sp", xt[t_ % 2][:], xTv[:, :, (t_ // 8) * TOK + (t_ % 8) * 256: (t_ // 8) * TOK + (t_ % 8) * 256 + 256], w=[RXT[t_ % 2]])

        def norm_a(t_):
            bb = t_ % 2
            S.op("act", lambda e: e.activation(out=sq[bb][:], in_=xt[bb][:], func=AF.Square), r=[RXT[bb]], w=[RSQ[bb]])

        def norm_b(t_):
            bb = t_ % 2
            xtile, rs, hout = xt[bb][:], rstd[bb][:], hT[bb]
            psb = PS[6][:, 0:256]
            for k in range(16):
                S.op("pe", lambda e, k=k: e.matmul(psb, lhsT=onesb[:], rhs=sq[bb][:, k, :], start=(k == 0), stop=(k == 15)), r=[RSQ[bb], RC], w=[RPS[6]], inc=(k == 15))
            S.op("act", lambda e: e.activation(out=rs, in_=psb, func=AF.Sqrt, bias=epsb[:, 0:1], scale=1.0 / DM), r=[RPS[6], RC], w=[RRS[bb]])
            S.op("dve", lambda e: e.reciprocal(out=rs, in_=rs), r=[RRS[bb]], w=[RRS[bb]])
            S.op("dve", lambda e: e.tensor_tensor(out=xtile, in0=xtile, in1=rs.unsqueeze(1).broadcast_to([128, 16, 256]), op=ALU.mult), r=[RXT[bb], RRS[bb]], w=[RXT[bb]])
            for k in range(16):
                if k % 2 == 0:
                    S.op("act", lambda e, k=k: e.activation(out=hout[:, k, :], in_=xtile[:, k, :], func=AF.Identity, bias=shift1[:, k:k + 1], scale=s1p[:, k:k + 1]), r=[RXT[bb], RMOD], wp=[RHT[bb]])
                else:
                    S.op("dve", lambda e, k=k: e.tensor_scalar(out=hout[:, k, :], in0=xtile[:, k, :], scalar1=s1p[:, k:k + 1], scalar2=shift1[:, k:k + 1], op0=ALU.mult, op1=ALU.add), r=[RXT[bb], RMOD], wp=[RHT[bb]])

        pending = []

        def flush():
            while pending:
                pending.pop(0)()

        load_x(0)
        norm_a(0)
        norm_b(0)
        load_x(1)
        for tt in range(32):
            slot = tt // 8
            loc = (tt % 8) * 256
            b = tt % 2
            if tt + 1 < 32:
                norm_a(tt + 1)
            if slot < 2:
                ots = list(range(0, 8))
            elif slot == 2:
                ots = list(range(0, 8)) + list(range(16, 32))
            else:
                ots = list(range(0, 32))
            for oi in range(0, len(ots), 2):
                ot = ots[oi]
                bank = nbank[0] % 5
                nbank[0] += 1
                for half in range(2):
                    o = ot + half
                    for k in range(16):
                        S.op("pe", lambda e, k=k, o=o, half=half, bank=bank: e.matmul(PS[bank][:, half * 256:(half + 1) * 256], lhsT=win[:, k, o * 128:(o + 1) * 128], rhs=hT[b][:, k, :], start=(k == 0), stop=(k == 15)),
                             r=[(RWIN if o < 8 else RWIN2), RHT[b]], w=[RPS[bank]], inc=(k == 15))
                flush()
                if oi == 0 and tt + 1 < 32:
                    norm_b(tt + 1)
                    if tt + 2 < 32:
                        load_x(tt + 2)
                si = nstg[0] % 4
                nstg[0] += 1
                sv = stg[si][:].rearrange("p a c -> p (a c)")
                kind = ot // 8
                if kind == 0:
                    S.op("act", lambda e, bank=bank, sv=sv: e.activation(out=sv, in_=PS[bank][:], func=AF.Copy, scale=smk[:, slot:slot + 1]), r=[RPS[bank], RC], w=[RSTG[si]])
                    dst = Dr["uT_s"][ot:ot + 2, :, slot * TOK + loc: slot * TOK + loc + 256]
                    S.dma("sp", dst.rearrange("t p c -> p t c"), stg[si][:], r=[RSTG[si]], wp=[RSCR])
                elif kind == 3:
                    S.op("dve", lambda e, bank=bank, sv=sv: e.tensor_copy(out=sv, in_=PS[bank][:]), r=[RPS[bank]], w=[RSTG[si]])
                    dst = Dr["vT_s"][ot - 24:ot - 22, :, (slot - 2) * TOK + loc:(slot - 2) * TOK + loc + 256]
                    S.dma("sp", dst.rearrange("t p c -> p t c"), stg[si][:], r=[RSTG[si]], wp=[RSCR])
                else:
                    qi = nqk[0] % 2
                    nqk[0] += 1
                    S.op("act", lambda e, bank=bank, qi=qi: e.activation(out=sqq[qi][:], in_=PS[bank][:], func=AF.Square), r=[RPS[bank]], w=[RSQQ[qi]])
                    wcol = qkw[:, 0:1] if kind == 1 else qkw[:, 1:2]
                    if kind == 1:
                        dst = Dr["qT_s"][ot - 8:ot - 6, :, loc:loc + 256]
                    else:
                        dst = Dr["kT_s"][ot - 16:ot - 14, :, (slot - 2) * TOK + loc:(slot - 2) * TOK + loc + 256]

                    def rest(bank=bank, qi=qi, sv=sv, si=si, wcol=wcol, dst=dst):
                        S.op("pe", lambda e: e.matmul(PS[5][:], lhsT=onesb[:], rhs=sqq[qi][:], start=True, stop=True), r=[RSQQ[qi], RC], w=[RPS[5]])
                        S.op("act", lambda e: e.activation(out=rq[qi][:], in_=PS[5][:], func=AF.Sqrt, bias=epsb[:, 0:1], scale=1.0 / 128.0), r=[RPS[5], RC], w=[RRQ[qi]])
                        S.op("dve", lambda e: e.reciprocal(out=rq[qi][:], in_=rq[qi][:]), r=[RRQ[qi]], w=[RRQ[qi]])
                        S.op("dve", lambda e: e.scalar_tensor_tensor(out=sv, in0=PS[bank][:], scalar=wcol, in1=rq[qi][:], op0=ALU.mult, op1=ALU.mult), r=[RPS[bank], RRQ[qi], RC], w=[RSTG[si]])
                        S.dma("sp", dst.rearrange("t p c -> p t c"), stg[si][:], r=[RSTG[si]], wp=[RSCR])
                    pending.append(rest)
            flush()
            p0_block(32 + 2 * tt)
            p0_block(33 + 2 * tt)
            if tt == 31:
                S.op("dve", lambda e: e.tensor_tensor(out=modT[:, 32:96], in0=PS[7][:, 32:96], in1=badaT[:, 32:96], op=ALU.add), r=[RPS[7], RC], w=[RMOD])
                S.op("dve", lambda e: e.scalar_tensor_tensor(out=s2p, in0=modT[:, 64:80], scalar=1.0, in1=n2w, op0=ALU.add, op1=ALU.mult), r=[RMOD, RC], w=[RMOD])
        S.barrier()
        st1.__exit__(None, None, None)
        st0.__exit__(None, None, None)

        if stage == 1:
            with contextlib.ExitStack() as sd:
                db = sb(sd, "db", [128, 4096], BF16)
                df = sb(sd, "df", [128, 4096])
                RD = Res()
                S.dma("sp", db[:, 0:1024], Dr["uT_s"][0, :, 3 * TOK:3 * TOK + 1024], r=[RSCR], w=[RD])
                S.dma("sp", db[:, 1024:2048], Dr["qT_s"][0, :, 0:1024], r=[RSCR], w=[RD])
                S.dma("sp", db[:, 2048:2560], Dr["kT_s"][0, :, 0:512], r=[RSCR], w=[RD])
                S.dma("sp", db[:, 2560:3072], Dr["kT_s"][0, :, TOK:TOK + 512], r=[RSCR], w=[RD])
                S.dma("sp", db[:, 3072:4096], Dr["vT_s"][0, :, TOK:TOK + 1024], r=[RSCR], w=[RD])
                S.op("dve", lambda e: e.tensor_copy(out=df[:], in_=db[:]), r=[RD], w=[RD])
                S.op("dve", lambda e: e.tensor_copy(out=df[:, 0:96], in_=modT[:]), r=[RD, RMOD], w=[RD])
                S.dma("sp", Dr["dbg"][:, :], df[:], r=[RD], w=[RD])
                S.barrier()
            return nc

        class G:
            pass
        G.nc, G.S, G.es, G.sb, G.Dr, G.PS, G.RPS, G.RC, G.RSCR = nc, S, es, sb, Dr, PS, RPS, RC, RSCR
        G.identf, G.identb, G.onesb, G.bdm, G.stage, G.RMOD = identf, identb, onesb, bdm, stage, RMOD
        G.modT, G.vecs, G.epsb = modT, vecs, epsb
        G.h2_s = Dr["h2_s"]
        stY = contextlib.ExitStack()
        stY.__enter__()
        G.YS = sb(stY, "YS", [128, 8, TOK], BF16)
        G.sub = DEBUG.get('sub', 0)
        G.RYS = Res()
        phase_ssm(G)
        if stage == 2:
            with contextlib.ExitStack() as sd:
                df = sb(sd, "df", [128, 4096])
                RD = Res()
                S.op("dve", lambda e: e.tensor_copy(out=df[:, 0:2048], in_=G.YS[:, 0, :]), r=[G.RYS], w=[RD])
                S.op("dve", lambda e: e.tensor_copy(out=df[:, 2048:4096], in_=G.YS[:, 7, :]), r=[G.RYS], w=[RD])
                S.dma("sp", Dr["dbg"][:, :], df[:], r=[RD], w=[RD])
                S.barrier()
            stY.__exit__(None, None, None)
            return nc
        G.YA = sb(stY, "YA", [128, 8, TOK], BF16)
        G.RYA = Res()
        phase_attn(G)
        if stage == 3:
            with contextlib.ExitStack() as sd:
                df = sb(sd, "df", [128, 4096])
                RD = Res()
                S.op("dve", lambda e: e.tensor_copy(out=df[:, 0:2048], in_=G.YA[:, 0, :]), r=[G.RYA], w=[RD])
                S.op("dve", lambda e: e.tensor_copy(out=df[:, 2048:4096], in_=G.YA[:, 7, :]), r=[G.RYA], w=[RD])
                S.dma("sp", Dr["dbg"][:, :], df[:], r=[RD], w=[RD])
                S.barrier()
            stY.__exit__(None, None, None)
            return nc
        phase_out(G, stY)
    return nc


def phase_ssm(G):
    nc, S, sb, Dr, PS, RPS, RC = G.nc, G.S, G.sb, G.Dr, G.PS, G.RPS, G.RC
    identf, identb, bdm = G.identf, G.identb, G.bdm
    stA = contextlib.ExitStack()
    stA.__enter__()
    YG = sb(stA, "YG", [128, 8, TOK], BF16)
    RYG = Res()
    stB = contextlib.ExitStack()
    stB.__enter__()
    NPL = 36
    pl = sb(stB, "ssm_pl", [128, NPL, 32])
    names = "are aim dt z p e1 t t2 qv wi r wr A B C ur ui ar1 ai rho1 rho vr vi den zr zi Dd".split()
    P = {n: pl[:, i, :] for i, n in enumerate(names)}
    Tm = [sb(stB, "ssm_T%d" % i, [128, 1024]) for i in range(4)]
    Bbr = sb(stB, "Bbr", [128, 32, 16])
    Bbi = sb(stB, "Bbi", [128, 32, 16])
    Cre = sb(stB, "CreT", [128, 32, 16])
    Cim = sb(stB, "CimT", [128, 32, 16])
    Pr = sb(stB, "Pr", [128, 17, 32])
    Pi = sb(stB, "Pi", [128, 17, 32])
    P2 = sb(stB, "P2", [128, 16, 2, 128], BF16)
    Q2 = sb(stB, "Q2", [128, 2, 128], BF16)
    BD = sb(stB, "BD", [128, 16, 128], BF16)
    WS = sb(stB, "WSpad", [128, 2, 16, 2, 128], BF16)
    WH = sb(stB, "WHpad", [128, 16, 2, 4, 128], BF16)
    uT = sb(stB, "uTt", [128, 4 * TOK], BF16)
    Er = sb(stB, "Er", [128, 4, 513])
    Ei = sb(stB, "Ei", [128, 4, 513])
    Gre = sb(stB, "Gre", [128, 513])
    Gim = sb(stB, "Gim", [128, 513])
    Hb = sb(stB, "Hb", [128, 4, 2, 128], BF16)
    pmask = sb(stB, "pmask", [128, 4])
    dsm = sb(stB, "dsm", [128, 8])
    RS = Res()
    R_T, R_P2, R_BD, R_WS, R_WH, R_E, R_G = Res(), Res(), Res(), Res(), Res(), Res(), Res()
    ctx = {"r": [RS], "w": [RS]}

    def tt(o, a, b, op):
        S.op("dve", lambda e: e.tensor_tensor(out=o, in0=a, in1=b, op=op), r=ctx["r"], w=ctx["w"])

    def ts(o, a, s1, s2, op0, op1=None):
        if op1 is None:
            S.op("dve", lambda e: e.tensor_scalar(out=o, in0=a, scalar1=s1, scalar2=None, op0=op0), r=ctx["r"], w=ctx["w"])
        else:
            S.op("dve", lambda e: e.tensor_scalar(out=o, in0=a, scalar1=s1, scalar2=s2, op0=op0, op1=op1), r=ctx["r"], w=ctx["w"])

    def stt(o, a, sc, b, op0, op1):
        S.op("dve", lambda e: e.scalar_tensor_tensor(out=o, in0=a, scalar=sc, in1=b, op0=op0, op1=op1), r=ctx["r"], w=ctx["w"])

    def cp(o, a):
        S.op("dve", lambda e: e.tensor_copy(out=o, in_=a), r=ctx["r"], w=ctx["w"])

    M, AD, SUB = ALU.mult, ALU.add, ALU.subtract
    Bre = Tm[2][:, 0:512].rearrange("p (a c) -> p a c", c=16)
    Bim = Tm[3][:, 0:512].rearrange("p (a c) -> p a c", c=16)
    S.dma("sp", P["are"], Dr["a_re"][:, :], wp=[RS])
    S.dma("sp", P["aim"], Dr["a_im"][:, :], wp=[RS])
    S.dma("sp", P["dt"], Dr["ldt"][:, :], wp=[RS])
    S.dma("sp", Tm[2][:, 0:512], Dr["b_re"][:, :], wp=[RS])
    S.dma("sp", Tm[3][:, 0:512], Dr["b_im"][:, :], wp=[RS])
    S.dma("sp", Cre[:].rearrange("p a c -> p (a c)"), Dr["c_reT"][:, :], wp=[RS])
    S.dma("sp", Cim[:].rearrange("p a c -> p (a c)"), Dr["c_imT"][:, :], wp=[RS])
    S.dma("sp", dsm[:], Dr["ssm_d"][:, :], wp=[RS])
    S.op("act", lambda e: e.activation(out=P["dt"], in_=P["dt"], func=AF.Exp), r=[RS], w=[RS])
    for pp in range(4):
        tt(pmask[:, pp:pp + 1], bdm[:, 32 * pp:32 * pp + 1], bdm[:, 32 * pp + 16:32 * pp + 17], AD)
    S.op("pool", lambda e: e.memset(P2[:].rearrange("p a b c -> p (a b c)"), 0.0), wp=[RS, R_P2])
    S.op("pool", lambda e: e.memset(Q2[:].rearrange("p a c -> p (a c)"), 0.0), wp=[RS, R_P2])
    S.op("pool", lambda e: e.memset(WH[:].rearrange("p a b c d -> p (a b c d)"), 0.0), wp=[RS, R_WH])
    S.op("pool", lambda e: e.memset(Gre[:, 0:1], 0.0), wp=[RS, R_G])
    S.op("pool", lambda e: e.memset(Gim[:, 0:1], 0.0), wp=[RS, R_G])
    tt(P["z"], P["are"], P["dt"], M)
    ts(P["p"], P["z"], 0.2, 1.0, M, AD)
    for cc in (0.25, 1.0 / 3.0, 0.5):
        tt(P["p"], P["p"], P["z"], M)
        ts(P["p"], P["p"], cc, 1.0, M, AD)
    tt(P["e1"], P["p"], P["z"], M)
    tt(P["t"], P["aim"], P["dt"], M)
    ts(P["t"], P["t"], 1.0 / 64.0, None, M)
    tt(P["t2"], P["t"], P["t"], M)
    ts(P["qv"], P["t2"], -1.0 / 72.0, 1.0, M, AD)
    for cc in (42.0, 20.0, 6.0):
        tt(P["qv"], P["qv"], P["t2"], M)
        ts(P["qv"], P["qv"], -1.0 / cc, 1.0, M, AD)
    tt(P["wi"], P["qv"], P["t"], M)
    ts(P["r"], P["t2"], -1.0 / 90.0, 1.0, M, AD)
    for cc in (56.0, 30.0, 12.0):
        tt(P["r"], P["r"], P["t2"], M)
        ts(P["r"], P["r"], -1.0 / cc, 1.0, M, AD)
    tt(P["r"], P["r"], P["t2"], M)
    ts(P["wr"], P["r"], -0.5, None, M)

    def dbl(wr, wi):
        tt(P["A"], wr, wr, M)
        tt(P["B"], wi, wi, M)
        tt(P["C"], wr, wi, M)
        stt(wr, wr, 2.0, P["A"], M, AD)
        tt(wr, wr, P["B"], SUB)
        tt(P["C"], P["C"], wi, AD)
        ts(wi, P["C"], 2.0, None, M)

    for _ in range(6):
        dbl(P["wr"], P["wi"])
    cp(P["ur"], P["wr"])
    cp(P["ui"], P["wi"])
    tt(P["A"], P["e1"], P["ur"], M)
    tt(P["ar1"], P["e1"], P["ur"], AD)
    tt(P["ar1"], P["ar1"], P["A"], AD)
    tt(P["A"], P["e1"], P["ui"], M)
    tt(P["ai"], P["ui"], P["A"], AD)
    cp(P["rho1"], P["e1"])
    for _ in range(4):
        tt(P["A"], P["rho1"], P["rho1"], M)
        stt(P["rho1"], P["rho1"], 2.0, P["A"], M, AD)
    ts(P["rho"], P["rho1"], 1.0, None, AD)
    cp(P["vr"], P["ur"])
    cp(P["vi"], P["ui"])
    for _ in range(4):
        dbl(P["vr"], P["vi"])
    tt(P["A"], P["are"], P["are"], M)
    tt(P["B"], P["aim"], P["aim"], M)
    tt(P["den"], P["A"], P["B"], AD)
    S.op("dve", lambda e: e.reciprocal(out=P["den"], in_=P["den"]), r=[RS], w=[RS])
    tt(P["A"], P["ar1"], P["are"], M)
    tt(P["B"], P["ai"], P["aim"], M)
    tt(P["zr"], P["A"], P["B"], AD)
    tt(P["zr"], P["zr"], P["den"], M)
    tt(P["A"], P["ai"], P["are"], M)
    tt(P["B"], P["ar1"], P["aim"], M)
    tt(P["zi"], P["A"], P["B"], SUB)
    tt(P["zi"], P["zi"], P["den"], M)
    zrb = P["zr"].unsqueeze(2).broadcast_to([128, 32, 16])
    zib = P["zi"].unsqueeze(2).broadcast_to([128, 32, 16])
    t0v = Tm[0][:, 0:512].rearrange("p (a c) -> p a c", c=16)
    t1v = Tm[1][:, 0:512].rearrange("p (a c) -> p a c", c=16)
    tt(t0v, Bre, zrb, M)
    tt(t1v, Bim, zib, M)
    tt(Bbr[:], t0v, t1v, SUB)
    tt(t0v, Bim, zrb, M)
    tt(t1v, Bre, zib, M)
    tt(Bbi[:], t0v, t1v, AD)
    S.op("dve", lambda e: e.memset(Pr[:, 0, :], 0.0), r=[RS], w=[RS])
    S.op("dve", lambda e: e.memset(Pi[:, 0, :], 0.0), r=[RS], w=[RS])
    cp(Pr[:, 1, :], P["ar1"])
    cp(Pi[:, 1, :], P["ai"])
    for lev in range(4):
        n = 1 << lev
        cr = Pr[:, n:n + 1, :].broadcast_to([128, n, 32])
        ci = Pi[:, n:n + 1, :].broadcast_to([128, n, 32])
        xr = Pr[:, 1:n + 1, :]
        xi = Pi[:, 1:n + 1, :]
        tv = [t[:, 0:32 * n].rearrange("p (a c) -> p a c", c=32) for t in Tm[0:2]]
        tt(tv[0], xr, cr, M)
        tt(tv[1], xi, ci, M)
        tt(tv[0], tv[0], tv[1], SUB)
        tt(tv[0], tv[0], xr, AD)
        tt(Pr[:, n + 1:2 * n + 1, :], tv[0], cr, AD)
        tt(tv[0], xr, ci, M)
        tt(tv[1], xi, cr, M)
        tt(tv[0], tv[0], tv[1], AD)
        tt(tv[0], tv[0], xi, AD)
        tt(Pi[:, n + 1:2 * n + 1, :], tv[0], ci, AD)

    Elo_r = sb(stB, "Elo_r", [128, 16, 17])
    Elo_i = sb(stB, "Elo_i", [128, 16, 17])
    Ehi_r = sb(stB, "Ehi_r", [128, 16, 33])
    Ehi_i = sb(stB, "Ehi_i", [128, 16, 33])

    def build_tab(Er_, Ei_, br_, bi_, nlev):
        S.op("dve", lambda e: e.memset(Er_[:, :, 0:1], 0.0), r=[RS], w=[RS, R_T, R_E])
        S.op("dve", lambda e: e.memset(Ei_[:, :, 0:1], 0.0), r=[RS], w=[RS, R_T, R_E])
        cp(Er_[:, :, 1:2], br_)
        cp(Ei_[:, :, 1:2], bi_)
        for lev in range(nlev):
            n = 1 << lev
            cr = Er_[:, :, n:n + 1].broadcast_to([128, 16, n])
            ci = Ei_[:, :, n:n + 1].broadcast_to([128, 16, n])
            xr = Er_[:, :, 1:n + 1]
            xi = Ei_[:, :, 1:n + 1]
            tv = [t[:, 0:16 * n].rearrange("p (a c) -> p a c", a=16) for t in Tm]
            tt(tv[0], xr, cr, M)
            tt(tv[1], xi, ci, M)
            tt(tv[2], xr, ci, M)
            tt(tv[3], xi, cr, M)
            tt(tv[0], tv[0], tv[1], SUB)
            tt(tv[2], tv[2], tv[3], AD)
            tt(tv[0], tv[0], xr, AD)
            tt(tv[2], tv[2], xi, AD)
            tt(Er_[:, :, n + 1:2 * n + 1], tv[0], cr, AD)
            tt(Ei_[:, :, n + 1:2 * n + 1], tv[2], ci, AD)

    def build_coarse(half):
        ctx["r"], ctx["w"] = [RS], [RS, R_T, R_E]
        hs = slice(16 * half, 16 * half + 16)
        build_tab(Elo_r, Elo_i, P["vr"][:, hs].unsqueeze(2), P["vi"][:, hs].unsqueeze(2), 4)
        build_tab(Ehi_r, Ehi_i, Elo_r[:, :, 16:17], Elo_i[:, :, 16:17], 5)
        ts(Elo_r[:].rearrange("p a c -> p (a c)"), Elo_r[:].rearrange("p a c -> p (a c)"), 1.0, None, AD)
        ts(Ehi_r[:].rearrange("p a c -> p (a c)"), Ehi_r[:].rearrange("p a c -> p (a c)"), 1.0, None, AD)

    RUT = Res()
    RHB = Res()
    T4 = [t[:].rearrange("p (m a c) -> p m a c", m=16, a=4) for t in Tm]
    evac_n = [0]

    def evac_eng():
        evac_n[0] += 1
        return "act" if evac_n[0] % 2 else "dve"

    for T in range(8):
        ps4 = slice(4 * T, 4 * T + 4)
        S.dma("sp", uT[:], Dr["uT_s"][T, :, :], r=[G.RSCR], w=[RUT])
        ctx["r"], ctx["w"] = [RS], [R_T]
        PrV = Pr[:, 0:16, ps4].unsqueeze(3).broadcast_to([128, 16, 4, 16])
        PiV = Pi[:, 0:16, ps4].unsqueeze(3).broadcast_to([128, 16, 4, 16])
        BrV = Bbr[:, ps4, :].unsqueeze(1).broadcast_to([128, 16, 4, 16])
        BiV = Bbi[:, ps4, :].unsqueeze(1).broadcast_to([128, 16, 4, 16])
        tt(T4[0], PrV, BrV, M)
        tt(T4[1], PiV, BiV, M)
        tt(T4[2], PrV, BiV, M)
        tt(T4[3], PiV, BrV, M)
        tt(T4[0], T4[0], T4[1], SUB)
        tt(T4[2], T4[2], T4[3], AD)
        ctx["r"], ctx["w"] = [RS, R_T], [R_P2]
        for hh in range(2):
            rows = slice(64 * hh, 64 * hh + 64)
            dre = P2[rows, :, 0, :].rearrange("p m (a c) -> p m a c", c=32)[:, :, :, 16 * hh:16 * hh + 16]
            dim_ = P2[rows, :, 1, :].rearrange("p m (a c) -> p m a c", c=32)[:, :, :, 16 * hh:16 * hh + 16]
            tt(dre, T4[0][rows], BrV[rows], AD)
            tt(dim_, T4[2][rows], BiV[rows], AD)
            qre = Q2[rows, 0, :].rearrange("p (a c) -> p a c", c=32)[:, :, 16 * hh:16 * hh + 16]
            qim = Q2[rows, 1, :].rearrange("p (a c) -> p a c", c=32)[:, :, 16 * hh:16 * hh + 16]
            cp(qre, Cre[rows, ps4, :])
            ts(qim, Cim[rows, ps4, :], -1.0, None, M)
        for q in range(4):
            bank = q
            for sl in range(4):
                m = 4 * q + sl
                S.op("pe", lambda e, m=m, sl=sl, bank=bank: e.matmul(PS[bank][:, sl * 128:(sl + 1) * 128], lhsT=P2[:, m, 0, :], rhs=Q2[:, 0, :], start=True, stop=False), r=[R_P2], w=[RPS[bank]], inc=False)
                S.op("pe", lambda e, m=m, sl=sl, bank=bank: e.matmul(PS[bank][:, sl * 128:(sl + 1) * 128], lhsT=P2[:, m, 1, :], rhs=Q2[:, 1, :], start=False, stop=True), r=[R_P2], w=[RPS[bank]], inc=(sl == 3))
            S.op("dve", lambda e, q=q, bank=bank: e.tensor_tensor(out=BD[:, 4 * q:4 * q + 4, :], in0=PS[bank][:].rearrange("p (a c) -> p a c", c=128), in1=bdm[:].unsqueeze(1).broadcast_to([128, 4, 128]), op=M), r=[RPS[bank], RC], w=[R_BD])
        ctx["r"], ctx["w"] = [RS, RC], [R_BD]
        stt(BD[:, 0, :], identf[:], dsm[:, T:T + 1], BD[:, 0, :], M, AD)
        if getattr(G, 'sub', 0) == 2:
            S.barrier(); stB.__exit__(None, None, None); stA.__exit__(None, None, None); return
        ctx["r"], ctx["w"] = [RS], [R_T]
        PrV = Pr[:, 1:17, ps4].unsqueeze(3).broadcast_to([128, 16, 4, 16])
        PiV = Pi[:, 1:17, ps4].unsqueeze(3).broadcast_to([128, 16, 4, 16])
        CrV = Cre[:, ps4, :].unsqueeze(1).broadcast_to([128, 16, 4, 16])
        CiV = Cim[:, ps4, :].unsqueeze(1).broadcast_to([128, 16, 4, 16])
        tt(T4[0], PrV, CrV, M)
        tt(T4[1], PiV, CiV, M)
        tt(T4[2], PrV, CiV, M)
        tt(T4[3], PiV, CrV, M)
        tt(T4[0], T4[0], T4[1], SUB)
        tt(T4[2], T4[2], T4[3], AD)
        tt(T4[2], T4[2], CiV, AD)
        ctx["r"], ctx["w"] = [RS, R_T], [R_WH]
        for hh in range(2):
            rows = slice(64 * hh, 64 * hh + 64)
            dre = WH[rows, :, 0, :, :].rearrange("p j a c -> p j (a c)").rearrange("p j (a b) -> p j a b", b=32)[:, :, 0:16:5, 16 * hh:16 * hh + 16]
            dim_ = WH[rows, :, 1, :, :].rearrange("p j a c -> p j (a c)").rearrange("p j (a b) -> p j a b", b=32)[:, :, 0:16:5, 16 * hh:16 * hh + 16]
            tt(dre, T4[0][rows], CrV[rows], AD)
            ts(dim_, T4[2][rows], -1.0, None, M)
        if T % 4 == 0:
            build_coarse(T // 4)
        ctx["r"], ctx["w"] = [RS], [R_T, R_E]
        pc0 = 4 * (T % 4)
        for a_ in range(0, 4, 2):
            pa = slice(pc0 + a_, pc0 + a_ + 2)
            hra = Ehi_r[:, pa, 0:32].unsqueeze(3).broadcast_to([128, 2, 32, 16])
            hia = Ehi_i[:, pa, 0:32].unsqueeze(3).broadcast_to([128, 2, 32, 16])
            lra = Elo_r[:, pa, 0:16].unsqueeze(2).broadcast_to([128, 2, 32, 16])
            lia = Elo_i[:, pa, 0:16].unsqueeze(2).broadcast_to([128, 2, 32, 16])
            t0 = Tm[0][:, 0:1024].rearrange("p (a j l) -> p a j l", a=2, j=32)
            t1 = Tm[1][:, 0:1024].rearrange("p (a j l) -> p a j l", a=2, j=32)
            era = Er[:, a_:a_ + 2, 0:512].rearrange("p a (j l) -> p a j l", j=32)
            eia = Ei[:, a_:a_ + 2, 0:512].rearrange("p a (j l) -> p a j l", j=32)
            tt(t0, hra, lra, M)
            tt(t1, hia, lia, M)
            tt(era, t0, t1, SUB)
            tt(t0, hra, lia, M)
            tt(t1, hia, lra, M)
            tt(eia, t0, t1, AD)
        cp(Er[:, :, 512:513], Ehi_r[:, pc0:pc0 + 4, 32:33])
        cp(Ei[:, :, 512:513], Ehi_i[:, pc0:pc0 + 4, 32:33])
        if getattr(G, 'sub', 0) == 3:
            S.barrier(); stB.__exit__(None, None, None); stA.__exit__(None, None, None); return
        uTv = uT[:].rearrange("p (sl r i) -> p sl r i", sl=4, r=16)
        for ph in range(2):
            for q in range(8):
                bank = q % 4
                for sl in range(4):
                    idx = 4 * q + sl
                    m, ri = idx // 2, idx % 2
                    S.op("pe", lambda e, m=m, ri=ri, sl=sl, bank=bank: e.matmul(PS[bank][:, sl * 128:(sl + 1) * 128], lhsT=P2[:, m, ri, :], rhs=identb[:], start=True, stop=True), r=[R_P2, RC], w=[RPS[bank]], inc=(sl == 3))
                for pl_ in range(2):
                    pp = 2 * ph + pl_
                    dst = WS[:, pl_, 2 * q:2 * q + 2, :, :].rearrange("p m r c -> p (m r c)")
                    eng = evac_eng()
                    if eng == "act":
                        S.op("act", lambda e, dst=dst, bank=bank, pp=pp: e.activation(out=dst, in_=PS[bank][:], func=AF.Copy, scale=pmask[:, pp:pp + 1]), r=[RPS[bank], RS], wp=[R_WS])
                    else:
                        S.op("dve", lambda e, dst=dst, bank=bank, pp=pp: e.tensor_scalar(out=dst, in0=PS[bank][:], scalar1=pmask[:, pp:pp + 1], scalar2=None, op0=M), r=[RPS[bank], RS], wp=[R_WS])
            for pl_ in range(2):
                pp = 2 * ph + pl_
                for ri in range(2):
                    bank = 2 * pl_ + ri
                    for m in range(16):
                        S.op("pe", lambda e, pl_=pl_, m=m, ri=ri, bank=bank: e.matmul(PS[bank][:].rearrange("p (a c) -> p a c", a=4), lhsT=WS[:, pl_, m, ri, :], rhs=uTv[:, :, 15 - m, :], start=(m == 0), stop=(m == 15)),
                             r=[R_WS, RUT], w=[RPS[bank]], inc=(m == 15))
            for pl_ in range(2):
                pp = 2 * ph + pl_
                b0, b1 = 2 * pl_, 2 * pl_ + 1
                cosv = Er[:, pp, 1:513]
                sinv = Ei[:, pp, 1:513]
                X0, X1, X2, X3 = (t[:, 0:512] for t in Tm)
                ctx["r"], ctx["w"] = [RS, R_E, R_G], [R_T]
                S.op("dve", lambda e, b0=b0: e.tensor_tensor(out=X0, in0=PS[b0][:], in1=cosv, op=M), r=[RS, R_E, RPS[b0]], w=[R_T])
                S.op("dve", lambda e, b1=b1: e.tensor_tensor(out=X1, in0=PS[b1][:], in1=sinv, op=M), r=[RS, R_E, RPS[b1]], w=[R_T])
                tt(X0, X0, X1, AD)
                S.op("dve", lambda e, b1=b1: e.tensor_tensor(out=X2, in0=PS[b1][:], in1=cosv, op=M), r=[RS, R_E, RPS[b1]], w=[R_T])
                S.op("dve", lambda e, b0=b0: e.tensor_tensor(out=X3, in0=PS[b0][:], in1=sinv, op=M), r=[RS, R_E, RPS[b0]], w=[R_T])
                tt(X2, X2, X3, SUB)
                rb = P["rho"][:, 4 * T + pp:4 * T + pp + 1].broadcast_to([128, 512])
                S.op("dve", lambda e, rb=rb: e.tensor_tensor_scan(out=Gre[:, 1:513], data0=rb, data1=X0, initial=0.0, op0=M, op1=AD), r=[RS, R_T], w=[R_G])
                S.op("dve", lambda e, rb=rb: e.tensor_tensor_scan(out=Gim[:, 1:513], data0=rb, data1=X2, initial=0.0, op0=M, op1=AD), r=[RS, R_T], w=[R_G])
                ck = Er[:, pp, 384:512]
                sk = Ei[:, pp, 384:512]
                gr = Gre[:, 384:512]
                gi = Gim[:, 384:512]
                a0, a1 = Tm[0][:, 512:640], Tm[1][:, 512:640]
                tt(a0, gr, ck, M)
                tt(a1, gi, sk, M)
                S.op("dve", lambda e, pp=pp: e.tensor_tensor(out=Hb[:, pp, 0, :], in0=a0, in1=a1, op=SUB), r=[R_T], w=[RHB])
                tt(a0, gr, sk, M)
                tt(a1, gi, ck, M)
                S.op("dve", lambda e, pp=pp: e.tensor_tensor(out=Hb[:, pp, 1, :], in0=a0, in1=a1, op=AD), r=[R_T], w=[RHB])
        if getattr(G, 'sub', 0) == 4:
            S.barrier(); stB.__exit__(None, None, None); stA.__exit__(None, None, None); return
        own0 = 3 * TOK
        for j in range(16):
            bank = 4 + j // 4
            reg = PS[bank][:, (j % 4) * 128:(j % 4 + 1) * 128]
            first = True
            for s in range(j + 1):
                S.op("pe", lambda e, j=j, s=s, reg=reg, first=first: e.matmul(reg, lhsT=BD[:, j - s, :], rhs=uT[:, own0 + s * 128: own0 + (s + 1) * 128], start=first, stop=False), r=[R_BD, RUT], w=[RPS[bank]], inc=False)
                first = False
            for pp in range(4):
                for ri in range(2):
                    last = (pp == 3 and ri == 1)
                    S.op("pe", lambda e, j=j, pp=pp, ri=ri, reg=reg, last=last: e.matmul(reg, lhsT=WH[:, j, ri, pp, :], rhs=Hb[:, pp, ri, :], start=False, stop=last), r=[R_WH, RHB], w=[RPS[bank]], inc=(last and j % 4 == 3))
            if j % 4 == 3:
                S.op("act", lambda e, bank=bank, T=T, j=j: e.activation(out=YG[:, T, (j // 4) * 512:(j // 4 + 1) * 512], in_=PS[bank][:], func=AF.Gelu_apprx_tanh), r=[RPS[bank]], w=[RYG])
        if getattr(G, 'sub', 0) == 5:
            break
    S.barrier()
    stB.__exit__(None, None, None)
    if getattr(G, 'sub', 0) == 6:
        stA.__exit__(None, None, None); return
    with contextlib.ExitStack() as stC:
        sg = [sb(stC, "sg%d" % i, [128, 512]) for i in range(2)]
        wglu = sb(stC, "wglu", [128, 8, 1024], BF16)
        RWG = Res()
        wgv = Dr["w_glu"].rearrange("(k p) c -> p k c", p=128)
        for k in range(8):
            S.dma("pool", wglu[:, k, :], wgv[:, k, :], wp=[RWG])
        RSG = [Res(), Res()]
        n = 0
        for ot in range(8):
            for tb in range(4):
                bank = n % 4
                si = n % 2
                n += 1
                for k in range(8):
                    S.op("pe", lambda e, k=k, ot=ot, tb=tb, bank=bank: e.matmul(PS[bank][:], lhsT=wglu[:, k, ot * 128:(ot + 1) * 128], rhs=YG[:, k, tb * 512:(tb + 1) * 512], start=(k == 0), stop=(k == 7)), r=[RWG, RYG], w=[RPS[bank]], inc=(k == 7))
                S.op("act", lambda e, bank=bank, si=si: e.activation(out=sg[si][:], in_=PS[bank][:], func=AF.Sigmoid), r=[RPS[bank]], w=[RSG[si]])
                S.op("dve", lambda e, ot=ot, tb=tb, si=si: e.tensor_tensor(out=G.YS[:, ot, tb * 512:(tb + 1) * 512], in0=YG[:, ot, tb * 512:(tb + 1) * 512], in1=sg[si][:], op=M), r=[RSG[si], RYG], w=[G.RYS])
        S.barrier()
    stA.__exit__(None, None, None)


def phase_attn(G):
    nc, S, sb, Dr, PS, RPS, RC = G.nc, G.S, G.sb, G.Dr, G.PS, G.RPS, G.RC
    identb, onesb = G.identb, G.onesb
    M = ALU.mult
    with contextlib.ExitStack() as stT:
        btbs = [sb(stT, "btb%d" % i, [128, 12, 512], BF16) for i in range(2)]
        RBTS = [Res(), Res()]
        RBT = Res()
        qTB = [sb(stT, "qTh%d" % i, [128, TOK], BF16) for i in range(2)]
        k3B = [sb(stT, "k3%d" % i, [128, 2 * TOK], BF16) for i in range(2)]
        v3B = [sb(stT, "v3%d" % i, [128, 2 * TOK], BF16) for i in range(2)]
        RQB, RKB, RVB3 = [Res(), Res()], [Res(), Res()], [Res(), Res()]
        k2 = sb(stT, "k2", [128, 20 * 128], BF16)
        v2 = sb(stT, "v2", [128, 20 * 128], BF16)
        k1 = sb(stT, "k1", [128, 17 * 128], BF16)
        v1 = sb(stT, "v1", [128, 17 * 128], BF16)
        Vb = sb(stT, "Vb", [128, 72, 128], BF16)
        PT = [sb(stT, "PT%d" % i, [128, 512], BF16) for i in range(4)]
        rden = sb(stT, "rden", [128, 512])
        numS = sb(stT, "numS", [128, 512])
        denS = sb(stT, "denS", [128, 512])
        RNUM = Res()
        cnt = [0]
        RK2, RV2, RVB, RDEN = Res(), Res(), Res(), Res()

        def load_head(h_):
            bb = h_ % 2
            S.dma("sp", qTB[bb][:], Dr["qT_s"][h_, :, :], r=[G.RSCR], w=[RQB[bb]])
            S.dma("sp", k3B[bb][:], Dr["kT_s"][h_, :, :], r=[G.RSCR], w=[RKB[bb]])
            S.dma("sp", v3B[bb][:], Dr["vT_s"][h_, :, :], r=[G.RSCR], w=[RVB3[bb]])

        load_head(0)
        RPT = [Res() for _ in range(6)]
        nev = [0]
        for h in range(8):
            btb = btbs[h % 2]
            RBTh = RBTS[h % 2]
            S.dma("pool", btb[:].rearrange("p a c -> p (a c)"), Dr["btab"][h, :, :], w=[RBTh])
            qT, k3, v3 = qTB[h % 2], k3B[h % 2], v3B[h % 2]
            RQ, RK, RV = RQB[h % 2], RKB[h % 2], RVB3[h % 2]
            if h + 1 < 8:
                load_head(h + 1)
            for (src, dst2, dst1, Rs, Rd, eng) in ((k3, k2, k1, RK, RK2, "dve"), (v3, v2, v1, RV, RV2, "act")):
                own = src[:, TOK:2 * TOK]
                halo = src[:, 0:TOK]
                ops = []
                for r4 in range(4):
                    o = dst2[:, (4 + 4 * r4) * 128:(8 + 4 * r4) * 128].rearrange("p (s4 mm ii) -> p s4 mm ii", s4=4, mm=4)
                    i_ = own.rearrange("p (mm r4 s4 ii) -> p mm r4 s4 ii", mm=4, r4=4, s4=4)[:, :, r4, :, :].rearrange("p mm s4 ii -> p s4 mm ii")
                    ops.append((o, i_))
                    o = dst2[:, r4 * 128:(r4 + 1) * 128].rearrange("p (mm ii) -> p mm ii", mm=4)
                    i_ = halo.rearrange("p (mm r4 s4 ii) -> p mm r4 s4 ii", mm=4, r4=4, s4=4)[:, :, r4, 3, :]
                    ops.append((o, i_))
                o = dst1[:, 128:17 * 128].rearrange("p (n r c) -> p n r c", n=16, r=16)
                i_ = own.rearrange("p (r n c) -> p n r c", r=16, n=16)
                ops.append((o, i_))
                o = dst1[:, 0:128].rearrange("p (r c) -> p r c", r=16)
                i_ = halo.rearrange("p (r n c) -> p r n c", r=16, n=16)[:, :, 15, :]
                ops.append((o, i_))
                for (o, i_) in ops:
                    if eng == "dve":
                        S.op("dve", lambda e, o=o, i_=i_: e.tensor_copy(out=o, in_=i_), r=[Rs], wp=[Rd])
                    else:
                        S.op("act", lambda e, o=o, i_=i_: e.activation(out=o, in_=i_, func=AF.Copy), r=[Rs], wp=[Rd])
            vsrc = [(v3, b) for b in range(32)] + [(v2, b) for b in range(20)] + [(v1, b) for b in range(17)]
            for q in range(18):
                bank = q % 8
                blks = vsrc[4 * q:4 * q + 4]
                for sl, (vt, b) in enumerate(blks):
                    S.op("pe", lambda e, vt=vt, b=b, sl=sl, bank=bank: e.matmul(PS[bank][:, sl * 128:(sl + 1) * 128], lhsT=vt[:, b * 128:(b + 1) * 128], rhs=identb[:], start=True, stop=True),
                         r=[RV, RV2, RC], w=[RPS[bank]], inc=(sl == len(blks) - 1))
                nb_ = len(blks)
                nev[0] += 1
                dst = Vb[:, 4 * q:4 * q + nb_, :].rearrange("p a c -> p (a c)")
                if nev[0] % 2:
                    S.op("act", lambda e, dst=dst, bank=bank, nb_=nb_: e.activation(out=dst, in_=PS[bank][:, 0:nb_ * 128], func=AF.Copy), r=[RPS[bank]], wp=[RVB])
                else:
                    S.op("dve", lambda e, dst=dst, bank=bank, nb_=nb_: e.tensor_copy(out=dst, in_=PS[bank][:, 0:nb_ * 128]), r=[RPS[bank]], wp=[RVB])
            q2v = qT[:].rearrange("p (mm r4 s4 ii) -> p mm r4 s4 ii", mm=4, r4=4, s4=4)
            q1v = qT[:].rearrange("p (mm r4 n c) -> p mm r4 n c", mm=4, r4=4, n=16)
            for g in range(4):
                brs = []
                us = []
                for mm in range(4):
                    r16 = 4 * mm + g
                    us.append((k3, qT[:, r16 * 128:(r16 + 1) * 128], mm * 128, 128, r16, 16 + r16, 0))
                brs.append((0, 1, us, None, None))
                us = []
                for s4 in range(4):
                    kown = 4 + 4 * g + s4
                    kprev = g if s4 == 0 else kown - 1
                    us.append((k2, q2v[:, :, g, s4, :], s4 * 128, 128, kprev, kown, 32))
                brs.append((2, 3, us, "p (mm s4 ii) -> p mm s4 ii", "p (s4 mm ii) -> p mm s4 ii"))
                us = []
                for n in range(16):
                    us.append((k1, q1v[:, :, g, n, :], n * 32, 32, n, 1 + n, 52))
                brs.append((4 + 2 * g, 5 + 2 * g, us, "p (mm n c) -> p mm n c", "p (n mm c) -> p mm n c"))
                for bi, (tprev, town, us, vS, vP) in enumerate(brs):
                    par = cnt[0] % 2
                    cnt[0] += 1
                    sbk = (2 * par, 2 * par + 1)
                    acc, den = 4 + 2 * par, 5 + 2 * par
                    for which in range(2):
                        bank = sbk[which]
                        tab = tprev if which == 0 else town
                        S.op("pe", lambda e, bank=bank, tab=tab: e.matmul(PS[bank][:], lhsT=identb[:], rhs=btb[:, tab, :], start=True, stop=False), r=[RBTh, RC], w=[RPS[bank]], inc=False)
                        for ui, (ksrc, qap, c0, N, kprev, kown, voff) in enumerate(us):
                            lastu = (ui == len(us) - 1)
                            kb = kprev if which == 0 else kown
                            rk = RK if ksrc is k3 else RK2
                            S.op("pe", lambda e, bank=bank, ksrc=ksrc, kb=kb, qap=qap, c0=c0, N=N, lastu=lastu: e.matmul(PS[bank][:, c0:c0 + N], lhsT=ksrc[:, kb * 128:(kb + 1) * 128], rhs=qap, start=False, stop=lastu), r=[rk, RQ], w=[RPS[bank]], inc=lastu)
                        S.op("act", lambda e, bank=bank: e.activation(out=PT[bank][:], in_=PS[bank][:], func=AF.Exp), r=[RPS[bank]], w=[RPT[bank]])
                    for ui, (ksrc, qap, c0, N, kprev, kown, voff) in enumerate(us):
                        lastu = (ui == len(us) - 1)
                        for which in range(2):
                            bank = sbk[which]
                            vidx = voff + (kprev if which == 0 else kown)
                            S.op("pe", lambda e, acc=acc, bank=bank, vidx=vidx, c0=c0, N=N, which=which: e.matmul(PS[acc][:, c0:c0 + N], lhsT=Vb[:, vidx, :], rhs=PT[bank][:, c0:c0 + N], start=(which == 0), stop=(which == 1)), r=[RVB, RPT[bank]], w=[RPS[acc]], inc=False)
                    for which in range(2):
                        bank = sbk[which]
                        S.op("pe", lambda e, den=den, bank=bank, which=which: e.matmul(PS[den][:], lhsT=onesb[:], rhs=PT[bank][:], start=(which == 0), stop=(which == 1)), r=[RC, RPT[bank]], w=[RPS[den]], inc=(which == 1))
                    if bi == 0:
                        S.op("dve", lambda e, acc=acc: e.tensor_copy(out=numS[:], in_=PS[acc][:]), r=[RPS[acc]], w=[RNUM])
                        S.op("dve", lambda e, den=den: e.tensor_copy(out=denS[:], in_=PS[den][:]), r=[RPS[den]], w=[RNUM])
                    else:
                        S.op("dve", lambda e, acc=acc, vS=vS, vP=vP: e.tensor_tensor(out=numS[:].rearrange(vS, mm=4, **({"s4": 4} if "s4" in vS else {"n": 16})), in0=numS[:].rearrange(vS, mm=4, **({"s4": 4} if "s4" in vS else {"n": 16})), in1=PS[acc][:].rearrange(vP, mm=4, **({"s4": 4} if "s4" in vP else {"n": 16})), op=ALU.add), r=[RPS[acc]], w=[RNUM])
                        S.op("dve", lambda e, den=den, vS=vS, vP=vP: e.tensor_tensor(out=denS[:].rearrange(vS, mm=4, **({"s4": 4} if "s4" in vS else {"n": 16})), in0=denS[:].rearrange(vS, mm=4, **({"s4": 4} if "s4" in vS else {"n": 16})), in1=PS[den][:].rearrange(vP, mm=4, **({"s4": 4} if "s4" in vP else {"n": 16})), op=ALU.add), r=[RPS[den]], w=[RNUM])
                S.op("dve", lambda e: e.reciprocal(out=denS[:], in_=denS[:]), r=[RNUM], w=[RNUM])
                outv = G.YA[:, h, :].rearrange("p (mm r4 i) -> p mm r4 i", mm=4, r4=4)[:, :, g, :]
                S.op("dve", lambda e, outv=outv: e.tensor_tensor(out=outv, in0=numS[:].rearrange("p (mm i) -> p mm i", mm=4), in1=denS[:].rearrange("p (mm i) -> p mm i", mm=4), op=M), r=[RNUM], w=[G.RYA])
        S.barrier()


def phase_out(G, stY):
    nc, S, sb, Dr, PS, RPS, RC = G.nc, G.S, G.sb, G.Dr, G.PS, G.RPS, G.RC
    onesb, epsb, modT, vecs = G.onesb, G.epsb, G.modT, G.vecs
    M, AD = ALU.mult, ALU.add
    gate1 = modT[:, 32:48]
    shift2 = modT[:, 48:64]
    gate2 = modT[:, 80:96]
    s2p = vecs[:, 48:64]
    Dr_h2 = G.h2_s
    RX1S, RH2S = Res(), Res()
    with contextlib.ExitStack() as st:
        wout = sb(st, "wout", [128, 16, DM], BF16)
        RWOB = [Res() for _ in range(8)]
        wov = Dr["w_out"].rearrange("(k p) c -> p k c", p=128)
        for cb in range(8):
            S.dma("pool", wout[:, :, cb * 256:(cb + 1) * 256], wov[:, :, cb * 256:(cb + 1) * 256], w=[RWOB[cb]])
        xt = [sb(st, "xo%d" % i, [128, 16, 256]) for i in range(2)]
        sq = sb(st, "sqo", [128, 16, 256], BF16)
        h2 = [sb(st, "h2o%d" % i, [128, 16, 256], BF16) for i in range(2)]
        rs = [sb(st, "rso%d" % i, [128, 256]) for i in range(2)]
        RXT, RH2, RRS, RSQ = [Res(), Res()], [Res(), Res()], [Res(), Res()], Res()
        xTv = Dr["xT"].rearrange("(k p) c -> p k c", p=128)
        nb = 0
        def load_xo(t_):
            S.dma("sp", xt[t_ % 2][:], xTv[:, :, 3 * TOK + t_ * 256: 3 * TOK + t_ * 256 + 256], w=[RXT[t_ % 2]])

        load_xo(0)
        for tb in range(8):
            b = tb % 2
            c0 = tb * 256
            if tb + 1 < 8:
                load_xo(tb + 1)
            for dp in range(8):
                bank = nb % 6
                nb += 1
                for half in range(2):
                    dt_ = 2 * dp + half
                    for k in range(16):
                        src = G.YS if k < 8 else G.YA
                        rr = G.RYS if k < 8 else G.RYA
                        S.op("pe", lambda e, k=k, dt_=dt_, half=half, bank=bank, src=src: e.matmul(PS[bank][:, half * 256:(half + 1) * 256], lhsT=wout[:, k, dt_ * 128:(dt_ + 1) * 128], rhs=src[:, k % 8, c0:c0 + 256], start=(k == 0), stop=(k == 15)),
                             r=[RWOB[dp], rr], w=[RPS[bank]], inc=(k == 15))
                for half in range(2):
                    dt_ = 2 * dp + half
                    S.op("dve", lambda e, dt_=dt_, half=half, bank=bank: e.scalar_tensor_tensor(out=xt[b][:, dt_, :], in0=PS[bank][:, half * 256:(half + 1) * 256], scalar=gate1[:, dt_:dt_ + 1], in1=xt[b][:, dt_, :], op0=M, op1=AD),
                         r=[RPS[bank], RXT[b], G.RMOD], w=[RXT[b]])
            S.dma("sp", Dr["x1_s"][:, :, c0:c0 + 256].rearrange("k p c -> p k c"), xt[b][:], r=[RXT[b]], wp=[RX1S])
            S.op("act", lambda e: e.activation(out=sq[:], in_=xt[b][:], func=AF.Square), r=[RXT[b]], w=[RSQ])
            for k in range(16):
                S.op("pe", lambda e, k=k: e.matmul(PS[6][:, 0:256], lhsT=onesb[:], rhs=sq[:, k, :], start=(k == 0), stop=(k == 15)), r=[RSQ, RC], w=[RPS[6]], inc=(k == 15))
            S.op("act", lambda e: e.activation(out=rs[b][:], in_=PS[6][:, 0:256], func=AF.Sqrt, bias=epsb[:, 0:1], scale=1.0 / DM), r=[RPS[6], RC], w=[RRS[b]])
            S.op("dve", lambda e: e.reciprocal(out=rs[b][:], in_=rs[b][:]), r=[RRS[b]], w=[RRS[b]])
            S.op("dve", lambda e: e.tensor_tensor(out=xt[b][:], in0=xt[b][:], in1=rs[b][:].unsqueeze(1).broadcast_to([128, 16, 256]), op=M), r=[RXT[b], RRS[b]], w=[RXT[b]])
            for k in range(16):
                if k % 2 == 0:
                    S.op("act", lambda e, k=k: e.activation(out=h2[b][:, k, :], in_=xt[b][:, k, :], func=AF.Identity, bias=shift2[:, k:k + 1], scale=s2p[:, k:k + 1]), r=[RXT[b], G.RMOD], wp=[RH2[b]])
                else:
                    S.op("dve", lambda e, k=k: e.tensor_scalar(out=h2[b][:, k, :], in0=xt[b][:, k, :], scalar1=s2p[:, k:k + 1], scalar2=shift2[:, k:k + 1], op0=M, op1=AD), r=[RXT[b], G.RMOD], wp=[RH2[b]])
            S.dma("sp", Dr_h2[:, :, c0:c0 + 256].rearrange("k p c -> p k c"), h2[b][:], r=[RH2[b]], wp=[RH2S])
        S.barrier()
    stY.__exit__(None, None, None)
    with contextlib.ExitStack() as st:
        h2T = sb(st, "h2T", [128, 16, 1024], BF16)
        act = sb(st, "ffact", [128, NHT, 1024], BF16)
        wgb = [sb(st, "wgb%d" % i, [128, 16, 256], BF16) for i in range(2)]
        wub = [sb(st, "wub%d" % i, [128, 16, 256], BF16) for i in range(2)]
        wdb = [sb(st, "wdb%d" % i, [128, NHT, 128], BF16) for i in range(2)]
        sgt = [sb(st, "sgt%d" % i, [128, 512]) for i in range(2)]
        x1t = [sb(st, "x1t%d" % i, [128, 512]) for i in range(2)]
        ost = [sb(st, "ost%d" % i, [128, 512]) for i in range(2)]
        RH, RACT = Res(), Res()
        RWG, RWU, RWD = [Res(), Res()], [Res(), Res()], [Res(), Res()]
        RSG, RX1, ROS = [Res(), Res()], [Res(), Res()], [Res(), Res()]
        ROUT = Res()
        nbk = 0
        nsg = 0
        nwd = 0
        nwg = 0
        for tile in range(2):
            t0 = tile * 1024
            S.dma("sp", h2T[:], Dr_h2[:, :, t0:t0 + 1024].rearrange("k p c -> p k c"), r=[RH2S], w=[RH])
            for hb in range(22):
                wb_ = nwg % 2
                nwg += 1
                S.dma("pool", wgb[wb_][:].rearrange("p k c -> p (k c)"), Dr["wg_t"][hb, :, :], w=[RWG[wb_]])
                S.dma("pool", wub[wb_][:].rearrange("p k c -> p (k c)"), Dr["wu_t"][hb, :, :], w=[RWU[wb_]])
                for ht2 in range(2):
                    ht = 2 * hb + ht2
                    for half in range(2):
                        bg = (nbk % 4) * 2
                        bu = bg + 1
                        nbk += 1
                        for k in range(16):
                            S.op("pe", lambda e, k=k, ht2=ht2, half=half, bg=bg: e.matmul(PS[bg][:], lhsT=wgb[wb_][:, k, ht2 * 128:(ht2 + 1) * 128], rhs=h2T[:, k, half * 512:(half + 1) * 512], start=(k == 0), stop=(k == 15)), r=[RWG[wb_], RH], w=[RPS[bg]], inc=(k == 15))
                        for k in range(16):
                            S.op("pe", lambda e, k=k, ht2=ht2, half=half, bu=bu: e.matmul(PS[bu][:], lhsT=wub[wb_][:, k, ht2 * 128:(ht2 + 1) * 128], rhs=h2T[:, k, half * 512:(half + 1) * 512], start=(k == 0), stop=(k == 15)), r=[RWU[wb_], RH], w=[RPS[bu]], inc=(k == 15))
                        si = nsg % 2
                        nsg += 1
                        S.op("act", lambda e, bg=bg, si=si: e.activation(out=sgt[si][:], in_=PS[bg][:], func=AF.Silu), r=[RPS[bg]], w=[RSG[si]])
                        S.op("dve", lambda e, bu=bu, si=si, ht=ht, half=half: e.tensor_tensor(out=act[:, ht, half * 512:(half + 1) * 512], in0=sgt[si][:], in1=PS[bu][:], op=M), r=[RSG[si], RPS[bu]], w=[RACT])
            for dt_ in range(16):
                wd_ = nwd % 2
                nwd += 1
                S.dma("pool", wdb[wd_][:].rearrange("p k c -> p (k c)"), Dr["wd_t"][dt_, :, :], w=[RWD[wd_]])
                for half in range(2):
                    bank = (nbk % 4) * 2
                    nbk += 1
                    si = nsg % 2
                    nsg += 1
                    c0 = t0 + half * 512
                    S.dma("sp", x1t[si][:], Dr["x1_s"][dt_, :, c0:c0 + 512], r=[RX1S], w=[RX1[si]])
                    for k in range(NHT):
                        S.op("pe", lambda e, k=k, half=half, bank=bank: e.matmul(PS[bank][:], lhsT=wdb[wd_][:, k, :], rhs=act[:, k, half * 512:(half + 1) * 512], start=(k == 0), stop=(k == NHT - 1)), r=[RWD[wd_], RACT], w=[RPS[bank]], inc=(k == NHT - 1))
                    S.op("dve", lambda e, bank=bank, si=si, dt_=dt_: e.scalar_tensor_tensor(out=ost[si][:], in0=PS[bank][:], scalar=gate2[:, dt_:dt_ + 1], in1=x1t[si][:], op0=M, op1=AD), r=[RPS[bank], RX1[si], G.RMOD], w=[ROS[si]])
                    S.dma("sp", Dr["outT"][dt_ * 128:(dt_ + 1) * 128, c0:c0 + 512], ost[si][:], r=[ROS[si]], wp=[ROUT])
        S.barrier()


def _perm_slot(xs):
    d = xs.shape[1]
    return xs.reshape(128, 16, d).transpose(2, 1, 0).reshape(d, 2048)


def _unperm_slot(yT):
    d = yT.shape[0]
    return yT.reshape(d, 16, 128).transpose(2, 1, 0).reshape(2048, d)


def _bias_tables(halo_valid):
    slopes = np.exp2(-8.0 * np.arange(1, 9, dtype=np.float32) / 8).astype(np.float32)
    kk = np.arange(128)[:, None]
    cc = np.arange(512)[None, :]
    out = np.zeros((8, 128, 12, 512), np.float32)

    def fill(t, mk, mq, d, is_prev, halo_cols):
        if is_prev:
            dist = mq + 128 - mk
            valid = dist <= 128
        else:
            dist = mq - mk
            valid = dist >= 0
        if is_prev and not halo_valid:
            valid = valid & (~halo_cols)
        for h in range(8):
            out[h, :, t, :] = np.where(valid, -slopes[h] * d * dist, NEG)

    allc = np.ones((1, 512), bool)
    fill(0, kk, cc % 128, 16, True, allc)
    fill(1, kk, cc % 128, 16, False, allc)
    mk2 = 4 * (kk % 32) + kk // 32
    mq2 = 4 * (cc % 32) + (cc % 128) // 32
    fill(2, mk2, mq2, 4, True, (cc // 128) == 0)
    fill(3, mk2, mq2, 4, False, allc)
    tk1 = 16 * (kk % 8) + kk // 8
    for r4 in range(4):
        tq1 = 16 * (cc % 8) + 4 * ((cc % 32) // 8) + r4
        fill(4 + 2 * r4, tk1, tq1, 1, True, (cc // 32) == 0)
        fill(5 + 2 * r4, tk1, tq1, 1, False, allc)
    return out.reshape(8, 128, 12 * 512)


def _prep_shared(inp):
    f = np.float32
    sh = {}
    sh["w_ada"] = np.ascontiguousarray(inp["w_ada"][0], f)
    sh["badaT"] = np.ascontiguousarray(inp["b_ada"][0].reshape(96, 128).T, f)
    sh["n1w"] = np.ascontiguousarray(inp["norm1_w"][0].reshape(16, 128).T, f)
    sh["n2w"] = np.ascontiguousarray(inp["norm2_w"][0].reshape(16, 128).T, f)
    sh["w_in"] = np.ascontiguousarray(inp["w_in"][0], f)

    def gn(a):
        return np.ascontiguousarray(a.reshape(32, 128).T, f)
    sh["a_re"] = gn(inp["ssm_a_re"][0])
    sh["a_im"] = gn(inp["ssm_a_im"][0])
    sh["ldt"] = gn(np.repeat(inp["ssm_log_dt"][0][:, None], 64, axis=1))

    def gnc(a):
        return np.ascontiguousarray(a.reshape(32, 2, 64, 16).transpose(1, 2, 0, 3).reshape(128, 512), f)
    sh["b_re"] = gnc(inp["ssm_b_re"][0])
    sh["b_im"] = gnc(inp["ssm_b_im"][0])
    sh["c_reT"] = gnc(inp["ssm_c_re"][0].transpose(0, 2, 1))
    sh["c_imT"] = gnc(inp["ssm_c_im"][0].transpose(0, 2, 1))
    sh["ssm_d"] = np.ascontiguousarray(inp["ssm_d"][0].reshape(8, 128).T, f)
    sh["w_glu"] = np.ascontiguousarray(inp["ssm_w_glu"][0], f)
    sh["qnw"] = np.ascontiguousarray(inp["q_norm_w"][0].reshape(128, 1), f)
    sh["knw"] = np.ascontiguousarray(inp["k_norm_w"][0].reshape(128, 1), f)
    sh["w_out"] = np.ascontiguousarray(inp["w_out"][0], f)
    wg = inp["w_ffn_gate"][0].reshape(16, 128, 22, 256).transpose(2, 1, 0, 3)
    wu = inp["w_ffn_up"][0].reshape(16, 128, 22, 256).transpose(2, 1, 0, 3)
    sh["wg_t"] = np.ascontiguousarray(wg, f).reshape(22, 128, 16 * 256)
    sh["wu_t"] = np.ascontiguousarray(wu, f).reshape(22, 128, 16 * 256)
    wd = inp["w_ffn_down"][0].reshape(NHT, 128, 16, 128).transpose(2, 1, 0, 3)
    sh["wd_t"] = np.ascontiguousarray(wd, f).reshape(16, 128, NHT * 128)
    sh["ident"] = np.eye(128, dtype=f)
    g = np.arange(128) // 16
    sh["bdmask"] = (g[:, None] == g[None, :]).astype(f)
    return sh


def _prep_core(inp, core):
    b, j = core // 4, core % 4
    x = np.asarray(inp["x"], np.float32)
    m = {}
    xT = np.zeros((DM, 4 * TOK), np.float32)
    sm = np.zeros((128, 4), np.float32)
    for s in range(4):
        jj = j - 3 + s
        if jj >= 0:
            xT[:, s * TOK:(s + 1) * TOK] = _perm_slot(x[b, jj * TOK:(jj + 1) * TOK])
            sm[:, s] = 1.0
    m["xT"] = xT
    m["smask"] = sm
    m["cT"] = np.ascontiguousarray(np.asarray(inp["c"], np.float32)[b].reshape(16, 128).T)
    m["btab"] = _bias_tables(j > 0)
    return m


def kernel(**inputs):
    inp = {k: np.asarray(v) for k, v in inputs.items()}
    nc = build_program(STAGE)
    sh = _prep_shared(inp)
    in_maps = []
    ncores = DEBUG.get("ncores", NCORES)
    for core in DEBUG.get("corelist", range(ncores)):
        m = dict(sh)
        m.update(_prep_core(inp, core))
        in_maps.append(m)
    res = run_bass_kernel_spmd(nc, in_maps, core_ids=list(range(ncores)))
    if STAGE < 99:
        DEBUG["res"] = res.results
        return None
    out = np.zeros((2, SEQ, DM), np.float32)
    for core in range(NCORES):
        b, j = core // 4, core % 4
        out[b, j * TOK:(j + 1) * TOK] = _unperm_slot(res.results[core]["outT"])
    return out
```
